# Optimizing a Trainium2 kernel written in Bass

```python
import math
import jax, jax.numpy as jnp
from jax import lax
import numpy as np

D_MODEL = 1024
BATCH = 4
SEQ = 4096
DEPTH = 4
DEC_BATCH = 128
DEC_SEQ = 1
PAST_LEN = 8192
PAGE_SIZE = 128

N_META = 16
EPS = 1e-6
D_FF = 2816
N_HEADS = 8
N_KV = 2
HEAD_DIM = 64
Q_PER_KV = N_HEADS // N_KV
WINDOW = 128
BLOCK = 128
ROPE_DIM = HEAD_DIM // 4
ROPE_THETA = 500000.0
SCALE = HEAD_DIM ** -0.5
C_CONV = 512
CONV_W = 31
D_SSM = D_MODEL
SSM_GROUP = 16
N_GROUPS = D_SSM // SSM_GROUP
SSM_STATE = 64
N_EVEN = (DEPTH + 1) // 2
N_ODD = DEPTH // 2
ATT_W = N_HEADS * HEAD_DIM
KV_W = N_KV * HEAD_DIM
IN_AB = ATT_W + 2 * KV_W + 2 * C_CONV
OUT_AB = ATT_W + C_CONV

kernel_name = 'hybrid_conformer_swa_s5_macaron_step'


def rmsnorm(x, g):
    xf = x.astype(jnp.float32)
    y = xf * lax.rsqrt(jnp.mean(xf * xf, axis=-1, keepdims=True) + EPS)
    return (y * g.astype(jnp.float32)).astype(x.dtype)


def layernorm(x, g, b):
    xf = x.astype(jnp.float32)
    mu = jnp.mean(xf, axis=-1, keepdims=True)
    xc = xf - mu
    y = xc * lax.rsqrt(jnp.mean(xc * xc, axis=-1, keepdims=True) + EPS)
    return (y * g.astype(jnp.float32) + b.astype(jnp.float32)).astype(x.dtype)


def swiglu(h, w_gu, w_down):
    gate, up = jnp.split(h @ w_gu, 2, axis=-1)
    return (jax.nn.silu(gate) * up) @ w_down


def rope_partial(x, pos):
    half = ROPE_DIM // 2
    inv = ROPE_THETA ** (-jnp.arange(half, dtype=jnp.float32) * 2.0 / ROPE_DIM)
    ang = pos.astype(jnp.float32)[:, None] * inv[None, :]
    cos = jnp.cos(ang)[:, None, :]
    sin = jnp.sin(ang)[:, None, :]
    xr = x[..., :ROPE_DIM].astype(jnp.float32)
    x1, x2 = xr[..., :half], xr[..., half:]
    rot = jnp.concatenate([x1 * cos - x2 * sin, x2 * cos + x1 * sin], axis=-1).astype(x.dtype)
    return jnp.concatenate([rot, x[..., ROPE_DIM:]], axis=-1)


def sink_probs(scores, mask, sink):
    s = jnp.where(mask, scores, -jnp.inf)
    sk = jnp.broadcast_to(sink.astype(jnp.float32)[:, :, None, None], s.shape[:-1] + (1,))
    p = jax.nn.softmax(jnp.concatenate([s, sk], axis=-1), axis=-1)
    return p[..., :-1]


def ab_project(h, pos, w_in):
    bsz, t, _ = h.shape
    proj = h @ w_in
    o1 = ATT_W
    o2 = o1 + KV_W
    o3 = o2 + KV_W
    o4 = o3 + C_CONV
    q = rope_partial(proj[..., :o1].reshape(bsz, t, N_HEADS, HEAD_DIM), pos)
    k = rope_partial(proj[..., o1:o2].reshape(bsz, t, N_KV, HEAD_DIM), pos)
    v = proj[..., o2:o3].reshape(bsz, t, N_KV, HEAD_DIM)
    z = proj[..., o3:o4] * jax.nn.sigmoid(proj[..., o4:])
    return q, k, v, z


def conv_branch(z, ctx, conv_w, conv_b, ln_g, ln_b):
    zf = jnp.concatenate([ctx.astype(z.dtype), z], axis=1)
    y = lax.conv_general_dilated(zf, conv_w.astype(z.dtype)[:, None, :], (1,), 'VALID',
                                 dimension_numbers=('NWC', 'WIO', 'NWC'),
                                 feature_group_count=C_CONV) + conv_b
    y = jax.nn.silu(layernorm(y, ln_g, ln_b))
    return y, zf[:, -(CONV_W - 1):]


def swa_prompt(q, k, v, sink):
    bsz, L = q.shape[0], q.shape[1]
    pad = (-L) % BLOCK
    lp = L + pad
    nb = lp // BLOCK
    padt = lambda t: jnp.pad(t, ((0, 0), (pad, 0), (0, 0), (0, 0)))
    qb = padt(q).reshape(bsz, nb, BLOCK, N_KV, Q_PER_KV, HEAD_DIM)
    kb = padt(k).reshape(bsz, nb, BLOCK, N_KV, HEAD_DIM)
    vb = padt(v).reshape(bsz, nb, BLOCK, N_KV, HEAD_DIM)
    prev = lambda t: jnp.pad(t[:, :-1], ((0, 0), (1, 0), (0, 0), (0, 0), (0, 0)))
    kk = jnp.concatenate([prev(kb), kb], axis=2)
    vv = jnp.concatenate([prev(vb), vb], axis=2)
    posb = (jnp.arange(lp, dtype=jnp.int32) - pad).reshape(nb, BLOCK)
    kpos = jnp.concatenate([posb - BLOCK, posb], axis=-1)
    diff = posb[:, :, None] - kpos[:, None, :]
    mask = (diff >= 0) & (diff <= WINDOW) & (kpos[:, None, :] >= 0)
    scores = jnp.einsum('bnqkgd,bnskd->bnkgqs', qb.astype(jnp.float32) * SCALE, kk.astype(jnp.float32))
    p = sink_probs(scores, mask[None, :, None, None], sink.reshape(N_KV, Q_PER_KV))
    o = jnp.einsum('bnkgqs,bnskd->bnqkgd', p.astype(vv.dtype), vv)
    o = o.reshape(bsz, lp, ATT_W)[:, pad:]
    keep = min(WINDOW, PAST_LEN)
    return o, k[:, -keep:], v[:, -keep:]


def swa_sample(q, k, v, buf_k, buf_v, sink):
    bsz, t = q.shape[0], q.shape[1]
    keep = buf_k.shape[1]
    kk = jnp.concatenate([buf_k.astype(k.dtype), k], axis=1)
    vv = jnp.concatenate([buf_v.astype(v.dtype), v], axis=1)
    kpos = PAST_LEN - keep + jnp.arange(keep + t, dtype=jnp.int32)
    qpos = PAST_LEN + jnp.arange(t, dtype=jnp.int32)
    diff = qpos[:, None] - kpos[None, :]
    mask = (diff >= 0) & (diff <= WINDOW)
    qg = q.reshape(bsz, t, N_KV, Q_PER_KV, HEAD_DIM)
    scores = jnp.einsum('btkgd,bskd->bkgts', qg.astype(jnp.float32) * SCALE, kk.astype(jnp.float32))
    p = sink_probs(scores, mask[None, None, None], sink.reshape(N_KV, Q_PER_KV))
    o = jnp.einsum('bkgts,bskd->btkgd', p.astype(vv.dtype), vv).reshape(bsz, t, ATT_W)
    return o, kk[:, -keep:], vv[:, -keep:]


def ab_mixer_prompt(h, pos, w_in, w_out, sink, conv_w, conv_b, ln_g, ln_b):
    q, k, v, z = ab_project(h, pos, w_in)
    ctx = jnp.zeros((h.shape[0], CONV_W - 1, C_CONV), z.dtype)
    c_out, c_state = conv_branch(z, ctx, conv_w, conv_b, ln_g, ln_b)
    a_out, k_state, v_state = swa_prompt(q, k, v, sink)
    out = jnp.concatenate([a_out, c_out], axis=-1) @ w_out
    return out, c_state, k_state, v_state


def ab_mixer_sample(h, pos, conv_ctx, buf_k, buf_v, w_in, w_out, sink, conv_w, conv_b, ln_g, ln_b):
    q, k, v, z = ab_project(h, pos, w_in)
    c_out, c_state = conv_branch(z, conv_ctx, conv_w, conv_b, ln_g, ln_b)
    a_out, k_state, v_state = swa_sample(q, k, v, buf_k, buf_v, sink)
    out = jnp.concatenate([a_out, c_out], axis=-1) @ w_out
    return out, c_state, k_state, v_state


def s5_scan(u, s_re, s_im, a_re, a_im, log_dt, b_re, b_im, c_re, c_im, d_skip):
    bsz, t, _ = u.shape
    f32 = jnp.float32
    a_re = a_re.astype(f32)
    a_im = a_im.astype(f32)
    dt = jnp.exp(log_dt.astype(f32))[:, None]
    mag = jnp.exp(a_re * dt)
    ang = a_im * dt
    lam_re = mag * jnp.cos(ang)
    lam_im = mag * jnp.sin(ang)
    den = a_re * a_re + a_im * a_im
    nr = lam_re - 1.0
    f_re = (nr * a_re + lam_im * a_im) / den
    f_im = (lam_im * a_re - nr * a_im) / den
    b_re = b_re.astype(f32)
    b_im = b_im.astype(f32)
    bb_re = f_re[..., None] * b_re - f_im[..., None] * b_im
    bb_im = f_re[..., None] * b_im + f_im[..., None] * b_re
    ug = u.astype(f32).reshape(bsz, t, N_GROUPS, SSM_GROUP)
    x_re = jnp.einsum('btgc,gpc->btgp', ug, bb_re)
    x_im = jnp.einsum('btgc,gpc->btgp', ug, bb_im)
    s_re = s_re.astype(f32)
    s_im = s_im.astype(f32)
    x_re = x_re.at[:, 0].add(lam_re * s_re - lam_im * s_im)
    x_im = x_im.at[:, 0].add(lam_re * s_im + lam_im * s_re)
    la_re = jnp.broadcast_to(lam_re, (1, t) + lam_re.shape)
    la_im = jnp.broadcast_to(lam_im, (1, t) + lam_im.shape)

    def combine(e, l):
        ear, eai, ebr, ebi = e
        lar, lai, lbr, lbi = l
        return (lar * ear - lai * eai, lar * eai + lai * ear,
                lar * ebr - lai * ebi + lbr, lar * ebi + lai * ebr + lbi)

    _, _, h_re, h_im = lax.associative_scan(combine, (la_re, la_im, x_re, x_im), axis=1)
    y = (jnp.einsum('btgp,gcp->btgc', h_re, c_re.astype(f32))
         - jnp.einsum('btgp,gcp->btgc', h_im, c_im.astype(f32)))
    y = y.reshape(bsz, t, D_SSM) + d_skip.astype(f32) * u.astype(f32)
    return y.astype(u.dtype), h_re[:, -1], h_im[:, -1]


def ssm_mixer(h, s_re, s_im, w_in, a_re, a_im, log_dt, b_re, b_im, c_re, c_im, d_skip, w_glu):
    u = h @ w_in
    y, n_re, n_im = s5_scan(u, s_re, s_im, a_re, a_im, log_dt, b_re, b_im, c_re, c_im, d_skip)
    val, gate = jnp.split(jax.nn.gelu(y) @ w_glu, 2, axis=-1)
    return val * jax.nn.sigmoid(gate), n_re, n_im


def setup_inputs(seed: int = 0) -> dict:
    key = jax.random.key(seed)
    ks = iter(jax.random.split(key, 40))
    nrm = lambda shape, s: jax.random.normal(next(ks), shape, jnp.float32) * s
    keep = min(WINDOW, PAST_LEN)
    a_im0 = jnp.broadcast_to(math.pi * jnp.arange(SSM_STATE, dtype=jnp.float32), (N_ODD, N_GROUPS, SSM_STATE))
    return {
        'x_prompt': nrm((BATCH, SEQ, D_MODEL), 1.0),
        'x_sample': nrm((DEC_BATCH, DEC_SEQ, D_MODEL), 1.0),
        'state_conv': nrm((N_EVEN, DEC_BATCH, CONV_W - 1, C_CONV), 0.5),
        'cache_win_k': nrm((N_EVEN, DEC_BATCH, keep, N_KV, HEAD_DIM), 1.0),
        'cache_win_v': nrm((N_EVEN, DEC_BATCH, keep, N_KV, HEAD_DIM), 1.0),
        'state_ssm_re': nrm((N_ODD, DEC_BATCH, N_GROUPS, SSM_STATE), 0.1),
        'state_ssm_im': nrm((N_ODD, DEC_BATCH, N_GROUPS, SSM_STATE), 0.1),
        'meta_tokens': nrm((N_META, D_MODEL), 1.0),
        'norm_g': 1.0 + nrm((DEPTH, 3, D_MODEL), 0.02),
        'final_norm_g': 1.0 + nrm((D_MODEL,), 0.02),
        'ffn1_w_gu': nrm((DEPTH, D_MODEL, 2 * D_FF), D_MODEL ** -0.5),
        'ffn1_w_down': nrm((DEPTH, D_FF, D_MODEL), D_FF ** -0.5),
        'ffn2_w_gu': nrm((DEPTH, D_MODEL, 2 * D_FF), D_MODEL ** -0.5),
        'ffn2_w_down': nrm((DEPTH, D_FF, D_MODEL), D_FF ** -0.5),
        'ab_w_in': nrm((N_EVEN, D_MODEL, IN_AB), D_MODEL ** -0.5),
        'ab_w_out': nrm((N_EVEN, OUT_AB, D_MODEL), OUT_AB ** -0.5),
        'attn_sink': nrm((N_EVEN, N_HEADS), 0.5),
        'conv_w': nrm((N_EVEN, CONV_W, C_CONV), CONV_W ** -0.5),
        'conv_b': nrm((N_EVEN, C_CONV), 0.02),
        'conv_ln_g': 1.0 + nrm((N_EVEN, C_CONV), 0.02),
        'conv_ln_b': nrm((N_EVEN, C_CONV), 0.02),
        'ssm_w_in': nrm((N_ODD, D_MODEL, D_SSM), D_MODEL ** -0.5),
        'ssm_a_re': -0.5 * jnp.exp(nrm((N_ODD, N_GROUPS, SSM_STATE), 0.02)),
        'ssm_a_im': a_im0 + nrm((N_ODD, N_GROUPS, SSM_STATE), 0.01),
        'ssm_log_dt': jax.random.uniform(next(ks), (N_ODD, N_GROUPS), jnp.float32, math.log(1e-3), math.log(1e-1)),
        'ssm_b_re': nrm((N_ODD, N_GROUPS, SSM_STATE, SSM_GROUP), (2 * SSM_GROUP) ** -0.5),
        'ssm_b_im': nrm((N_ODD, N_GROUPS, SSM_STATE, SSM_GROUP), (2 * SSM_GROUP) ** -0.5),
        'ssm_c_re': nrm((N_ODD, N_GROUPS, SSM_GROUP, SSM_STATE), SSM_STATE ** -0.5),
        'ssm_c_im': nrm((N_ODD, N_GROUPS, SSM_GROUP, SSM_STATE), SSM_STATE ** -0.5),
        'ssm_d': nrm((N_ODD, D_SSM), 1.0),
        'ssm_w_glu': nrm((N_ODD, D_SSM, 2 * D_MODEL), D_SSM ** -0.5),
    }


def reference(x_prompt, x_sample, state_conv, cache_win_k, cache_win_v, state_ssm_re, state_ssm_im,
              meta_tokens, norm_g, final_norm_g, ffn1_w_gu, ffn1_w_down, ffn2_w_gu, ffn2_w_down,
              ab_w_in, ab_w_out, attn_sink, conv_w, conv_b, conv_ln_g, conv_ln_b,
              ssm_w_in, ssm_a_re, ssm_a_im, ssm_log_dt, ssm_b_re, ssm_b_im, ssm_c_re, ssm_c_im,
              ssm_d, ssm_w_glu):
    bp = x_prompt.shape[0]
    meta = jnp.broadcast_to(meta_tokens.astype(x_prompt.dtype)[None], (bp, N_META, D_MODEL))
    xp = jnp.concatenate([meta, x_prompt], axis=1)
    xs = x_sample
    pos_p = jnp.arange(xp.shape[1], dtype=jnp.int32)
    pos_s = PAST_LEN + jnp.arange(xs.shape[1], dtype=jnp.int32)
    p_conv, p_k, p_v, p_re, p_im = [], [], [], [], []
    s_conv, s_k, s_v, s_re, s_im = [], [], [], [], []
    for l in range(DEPTH):
        xp = xp + 0.5 * swiglu(rmsnorm(xp, norm_g[l, 0]), ffn1_w_gu[l], ffn1_w_down[l])
        xs = xs + 0.5 * swiglu(rmsnorm(xs, norm_g[l, 0]), ffn1_w_gu[l], ffn1_w_down[l])
        hp = rmsnorm(xp, norm_g[l, 1])
        hs = rmsnorm(xs, norm_g[l, 1])
        i = l // 2
        if l % 2 == 0:
            w = (ab_w_in[i], ab_w_out[i], attn_sink[i], conv_w[i], conv_b[i], conv_ln_g[i], conv_ln_b[i])
            mp, cp, kp, vp = ab_mixer_prompt(hp, pos_p, *w)
            ms, cs, kS, vS = ab_mixer_sample(hs, pos_s, state_conv[i], cache_win_k[i], cache_win_v[i], *w)
            p_conv.append(cp)
            p_k.append(kp)
            p_v.append(vp)
            s_conv.append(cs)
            s_k.append(kS)
            s_v.append(vS)
        else:
            w = (ssm_w_in[i], ssm_a_re[i], ssm_a_im[i], ssm_log_dt[i], ssm_b_re[i], ssm_b_im[i],
                 ssm_c_re[i], ssm_c_im[i], ssm_d[i], ssm_w_glu[i])
            zeros = jnp.zeros((bp, N_GROUPS, SSM_STATE), jnp.float32)
            mp, rp, ip = ssm_mixer(hp, zeros, zeros, *w)
            ms, rS, iS = ssm_mixer(hs, state_ssm_re[i], state_ssm_im[i], *w)
            p_re.append(rp)
            p_im.append(ip)
            s_re.append(rS)
            s_im.append(iS)
        xp = xp + mp
        xs = xs + ms
        xp = xp + 0.5 * swiglu(rmsnorm(xp, norm_g[l, 2]), ffn2_w_gu[l], ffn2_w_down[l])
        xs = xs + 0.5 * swiglu(rmsnorm(xs, norm_g[l, 2]), ffn2_w_gu[l], ffn2_w_down[l])
    y_prompt = rmsnorm(xp, final_norm_g)[:, N_META:]
    y_sample = rmsnorm(xs, final_norm_g)
    return (y_prompt, y_sample,
            jnp.stack(p_conv), jnp.stack(p_k), jnp.stack(p_v), jnp.stack(p_re), jnp.stack(p_im),
            jnp.stack(s_conv), jnp.stack(s_k), jnp.stack(s_v), jnp.stack(s_re), jnp.stack(s_im))
```

```python
import numpy as np
from contextlib import ExitStack
import concourse.bass as bass
import concourse.mybir as mybir
from concourse.bass_utils import run_bass_kernel_spmd

F32 = mybir.dt.float32
BF16 = mybir.dt.bfloat16
ALU = mybir.AluOpType
AF = mybir.ActivationFunctionType

D = 1024
KC = 8
DFF = 2816
NJ = 22
DEPTH = 4
NCOL = 2080
STW = 1040
NMAIN = 1024
TILES = [(0, 352), (352, 704), (704, 1040)]
EPS = 1e-6
LCH = 16
NBK = 2
NCH = 129
NCL = (65, 64)
CG0 = (0, 65)
STILES = [(0, 6), (6, 11), (11, 16)]

DEBUG = {"stop_after": None}


class DSem:
    def __init__(self, name, step=16):
        self.name = name
        self.total = 0
        self.step = step


class Prog:
    ENG = ("pe", "act", "dve", "pool", "sp")

    def __init__(self, dry=False):
        self.dry = dry
        self.ops = {e: [] for e in self.ENG}
        self.count = {e: 0 for e in self.ENG}
        self.waited = {e: {} for e in self.ENG}
        self.res = {}
        self.dsems = {}
        self.dirty_dsems = set()

    def dsem(self, name, step=16):
        if name not in self.dsems:
            self.dsems[name] = DSem("d_" + name, step)
        return self.dsems[name]

    def _deps(self, eng, reads, writes):
        deps = {}

        def add(tok):
            if tok is None:
                return
            k, v = tok
            if deps.get(k, 0) < v:
                deps[k] = v

        for k in reads:
            st = self.res.get(k)
            if st:
                add(st[0])
        for k in writes:
            st = self.res.get(k)
            if st:
                add(st[0])
                for kk, vv in st[1].items():
                    add((kk, vv))
        waits = []
        for k, v in deps.items():
            if k == eng and eng == "pe":
                continue
            if self.waited[eng].get(k, 0) >= v:
                continue
            self.waited[eng][k] = v
            waits.append((k, v))
        return waits

    def _commit(self, tok, reads, writes):
        for k in reads:
            st = self.res.setdefault(k, [None, {}])
            if st[1].get(tok[0], 0) < tok[1]:
                st[1][tok[0]] = tok[1]
        for k in writes:
            self.res[k] = [tok, {}]

    def op(self, eng, fn, reads=(), writes=(), inc=True):
        if self.dry:
            return
        waits = self._deps(eng, reads, writes)
        if inc:
            self.count[eng] += 1
            tok = (eng, self.count[eng])
        else:
            tok = (eng, self.count[eng] + 1)
        self.ops[eng].append((waits, fn, "inc" if inc else None, None))
        self._commit(tok, reads, writes)

    def dma(self, eng, fn, reads, writes, dsem):
        if self.dry:
            return
        waits = self._deps(eng, reads, writes)
        dsem.total += dsem.step
        tok = (dsem.name, dsem.total)
        self.ops[eng].append((waits, fn, "dma", dsem))
        self.dirty_dsems.add(dsem.name)
        self._commit(tok, reads, writes)

    def barrier(self, engines=("pe", "act", "dve", "sp")):
        if self.dry:
            return
        toks = [(e, self.count[e]) for e in engines if self.count[e] > 0]
        for n in sorted(self.dirty_dsems):
            ds = [d for d in self.dsems.values() if d.name == n][0]
            toks.append((ds.name, ds.total))
        self.dirty_dsems = set()
        for e in engines:
            waits = []
            for k, v in toks:
                if k == e and e == "pe":
                    continue
                if self.waited[e].get(k, 0) >= v:
                    continue
                self.waited[e][k] = v
                waits.append((k, v))
            if waits:
                self.ops[e].append((waits, None, None, None))

    def final_wait(self, eng="sp"):
        if self.dry:
            return
        waits = []
        for d in self.dsems.values():
            if d.total > 0 and self.waited[eng].get(d.name, 0) < d.total:
                waits.append((d.name, d.total))
        for e in self.ENG:
            if e != eng and self.count[e] > 0:
                waits.append((e, self.count[e]))
        self.ops[eng].append((waits, None, None, None))

    def replay(self, nc, es):
        sems = {}
        for e in self.ENG:
            sems[e] = es.enter_context(nc.semaphore("s_" + e))
        for d in self.dsems.values():
            sems[d.name] = es.enter_context(nc.semaphore(d.name))
        block = es.enter_context(nc.Block())
        reg = {"pe": block.tensor, "act": block.scalar, "dve": block.vector, "pool": block.gpsimd, "sp": block.sync}
        for e in self.ENG:
            ops = self.ops[e]

            def body(engine, ops=ops, e=e):
                for waits, fn, kind, dsem in ops:
                    for k, v in waits:
                        engine.wait_ge(sems[k], v)
                    if fn is None:
                        continue
                    inst = fn(engine)
                    if kind == "inc":
                        inst.then_inc(sems[e], 1)
                    elif kind == "dma":
                        inst.then_inc(sems[dsem.name], dsem.step)

            reg[e](body)


class WeightStream:
    def __init__(self, prog, name, slots, lookahead):
        self.p = prog
        self.name = name
        self.slots = slots
        self.n = len(slots)
        self.look = lookahead
        self.plan = []
        self.issued = 0
        self.cur = 0
        self.fence = None
        self.limit = None

    def reset_for_real(self):
        self.issued = 0
        self.cur = 0

    def _issue(self, k):
        slot = self.slots[k % self.n]
        key = (self.name, k % self.n)
        ds = self.p.dsem(f"{self.name}{k % self.n}")
        for out_ap, in_ap in self.plan[k](slot):
            self.p.dma("pool", (lambda e, o=out_ap, i=in_ap: e.dma_start(out=o, in_=i)),
                       reads=((self.fence,) if self.fence else ()), writes=(key,), dsem=ds)

    def next(self, loader):
        k = self.cur
        self.cur += 1
        if self.p.dry:
            self.plan.append(loader)
            return self.slots[k % self.n], (self.name, k % self.n)
        lim = self.limit if self.limit is not None else len(self.plan)
        while self.issued < min(len(self.plan), k + self.look + 1, lim):
            self._issue(self.issued)
            self.issued += 1
        return self.slots[k % self.n], (self.name, k % self.n)


def bcast(ap, shape):
    return ap.broadcast_to(list(shape))


def build_program():
    nc = bass.Bass("TRN2", target_bir_lowering=False)
    dt = {}

    def din(name, shape):
        dt[name] = nc.dram_tensor(name, list(shape), F32, kind="ExternalInput").ap()
        return dt[name]

    def dout(name, shape):
        dt[name] = nc.dram_tensor(name, list(shape), F32, kind="ExternalOutput").ap()
        return dt[name]

    def dscr(name, shape, dtype):
        return nc.dram_tensor(name, list(shape), dtype).ap()

    xT_in = din("xT", (D, NCOL))
    gains_in = din("gains", (128, 13 * KC))
    w_gu = [din("ffn1_w_gu", (DEPTH, D, 2 * DFF)), din("ffn2_w_gu", (DEPTH, D, 2 * DFF))]
    w_dn = [din("ffn1_w_down", (DEPTH, DFF, D)), din("ffn2_w_down", (DEPTH, DFF, D))]
    ident_in = din("ident", (128, 128))
    mask3_in = din("mask3", (128, 384))
    flag_in = din("flag", (128, 2))
    ssm_small_in = din("ssm_small", (2, 128, 96))
    ssm_b_in = din("ssm_b", (2, 128, 1024))
    ssm_c_in = din("ssm_c", (2, 128, 1024))
    ssm_dsh_in = din("ssm_dsh", (2, 128, 64))
    ssm_win = din("ssm_w_in", (2, D, D))
    ssm_wglu = din("ssm_w_glu", (2, D, 2 * D))
    sst_in = din("sst_in", (2, 128, 1024))
    ab_w_in = din("ab_w_in", (2, D, 1792))
    ab_w_out = din("ab_w_out", (2, D, D))
    abc_in = din("abc", (2, 128, 140))
    rmat_in = din("rmat", (128, 128))
    b2_in = din("b2", (128, 128))
    amask_in = din("amask", (128, 512))
    rope_in = din("rope", (2, 2, 128, STW))
    ckt_in = din("ckt", (2, 128, 2, 16, 128))
    cv_in = din("cv", (2, 128, 16, 128))
    cknat_in = din("cknat", (2, 16, 128, 128))
    cvnat_in = din("cvnat", (2, 16, 128, 128))
    sconv_in = din("sconv_in", (2, 128, 4, 16, 30))
    yT_out = dout("yT", (D, NCOL))
    pwk_out = dout("pwk", (2, 64, 2, 128))
    pwv_out = dout("pwv", (2, 128, 128))
    pconv_out = dout("pconv", (2, 128, 4, 30))
    swk_out = dout("swk", (2, 16, 128, 128))
    swv_out = dout("swv", (2, 16, 128, 128))
    sconv_out = dout("sconv", (2, 128, 4, 16, 30))
    pssm_out = dout("pssm", (2, 128, 64))
    sssm_out = dout("sssm", (2, 128, 1024))

    T_scr = dscr("T_scr", (2, KC, 128, 8 * 3 * 128), BF16)
    W_scr = dscr("W_scr", (2, KC, 128, 8 * 256), BF16)
    Y_scr = dscr("Y_scr", (2, 2, 128, 32 * 256), BF16)
    scr_u = dscr("scr_u", (KC, 128, 16 * 65), BF16)
    scr_y = dscr("scr_y", (KC, 128, 16 * 65), BF16)
    scr_su = dscr("scr_su", (128, KC * 16), BF16)
    scr_sy = dscr("scr_sy", (16, 64 * 16), BF16)
    ccab_in = nc.dram_tensor("ccab_in", [128, 256], F32)
    ccab_out = nc.dram_tensor("ccab_out", [256, 256], F32)
    cc_in = nc.dram_tensor("cc_in", [128, 64], F32)
    cc_out = nc.dram_tensor("cc_out", [256, 64], F32)
    RG = [[0, 1], [2, 3], [4, 5], [6, 7]]

    es = ExitStack()
    with es:
        def sbp(stack, name, shape, dtype):
            return stack.enter_context(nc.sbuf_tensor(name, list(shape), dtype))

        xT = sbp(es, "xT_sb", (128, KC, NCOL), F32)
        gains = sbp(es, "gains_sb", (128, 13 * KC), F32)
        ones_bf = sbp(es, "ones_bf", (128, 128), BF16)
        ident_f = sbp(es, "ident_f", (128, 128), F32)
        ident_bf = sbp(es, "ident_bf", (128, 128), BF16)
        flag = sbp(es, "flag_sb", (128, 2), F32)
        lamt = sbp(es, "lamt", (128, 2, 6, 32), F32)
        dsh = sbp(es, "dsh", (128, 2, 64), F32)
        wA = [sbp(es, f"wA{i}", (128, 2048), BF16) for i in range(4)]
        fence_t = sbp(es, "fence_t", (128, 8), F32)
        pall = es.enter_context(nc.psum_tensor("pall", [128, 4096], F32))
        ps = [pall[:, b * 512:(b + 1) * 512] for b in range(8)]

        uid = [0]

        def emit(p, wsA, wsB):
            def nm(s):
                uid[0] += 1
                return f"{s}_{uid[0]}"

            p.dma("sp", (lambda e: e.dma_start(out=xT[:, :, :], in_=xT_in.rearrange("(kc p) n -> p kc n", p=128))),
                  reads=(), writes=[("xT", kc, st) for kc in range(KC) for st in (0, 1)], dsem=p.dsem("xload"))
            p.dma("sp", (lambda e: e.dma_start(out=gains[:, :], in_=gains_in[:, :])), reads=(), writes=["gains"], dsem=p.dsem("gload"))
            p.dma("sp", (lambda e: e.dma_start(out=ident_f[:, :], in_=ident_in[:, :])), reads=(), writes=["ident_f"], dsem=p.dsem("iload"))
            p.dma("sp", (lambda e: e.dma_start(out=flag[:, :], in_=flag_in[:, :])), reads=(), writes=["flag"], dsem=p.dsem("fload"))
            p.dma("sp", (lambda e: e.dma_start(out=dsh[:, :, :], in_=ssm_dsh_in.rearrange("l p g -> p l g"))), reads=(), writes=["dsh"], dsem=p.dsem("dload"))
            p.op("dve", lambda e: e.memset(ones_bf[:, :], 1.0), writes=["ones"])
            p.op("dve", lambda e: e.tensor_copy(out=ident_bf[:, :], in_=ident_f[:, :]), reads=["ident_f"], writes=["ident"])

            def rmsnorm(st, nidx, B, out_h=True):
                c0 = st * STW
                for kc in range(KC):
                    b = kc % 2
                    p.op("act", (lambda e, kc=kc, b=b: e.activation(out=B["sq"][b][:, :], in_=xT[:, kc, c0:c0 + STW], func=AF.Square)),
                         reads=[("xT", kc, st)], writes=[(B["id"], "sq", b)])
                    for ti, (t0, t1) in enumerate(TILES):
                        p.op("pe", (lambda e, kc=kc, b=b, ti=ti, t0=t0, t1=t1: e.matmul(
                            ps[5 + ti][:, 0:t1 - t0], ones_bf[:, :], B["sq"][b][:, t0:t1], start=(kc == 0), stop=(kc == KC - 1))),
                            reads=[(B["id"], "sq", b), "ones"], writes=[("ps", 5 + ti)], inc=True)
                for ti, (t0, t1) in enumerate(TILES):
                    p.op("act", (lambda e, ti=ti, t0=t0, t1=t1: e.activation(
                        out=B["rtmp"][:, t0:t1], in_=ps[5 + ti][:, 0:t1 - t0], func=AF.Sqrt, scale=1.0 / D, bias=EPS)),
                        reads=[("ps", 5 + ti)], writes=[(B["id"], "rtmp", ti)])
                    p.op("dve", (lambda e, t0=t0, t1=t1: e.reciprocal(out=B["rstd"][:, t0:t1], in_=B["rtmp"][:, t0:t1])),
                         reads=[(B["id"], "rtmp", ti)], writes=[(B["id"], "rstd", ti)])
                if out_h:
                    for kc in range(KC):
                        p.op("dve", (lambda e, kc=kc: e.scalar_tensor_tensor(
                            out=B["hT"][:, kc, :], in0=xT[:, kc, c0:c0 + STW], scalar=gains[:, nidx * KC + kc:nidx * KC + kc + 1],
                            in1=B["rstd"][:, :], op0=ALU.mult, op1=ALU.mult)),
                            reads=[("xT", kc, st), "gains"] + [(B["id"], "rstd", ti) for ti in range(3)], writes=[(B["id"], "hT", kc)])

            def norm_bufs(stack, tag):
                B = {"id": tag}
                B["hT"] = sbp(stack, nm("hT"), (128, KC, STW), BF16)
                B["sq"] = [sbp(stack, nm("sq"), (128, STW), BF16) for _ in range(2)]
                B["rstd"] = sbp(stack, nm("rstd"), (128, STW), F32)
                B["rtmp"] = sbp(stack, nm("rtmp"), (128, STW), F32)
                return B

            def ffn_phase(l, which, nidx):
                p.barrier()
                with ExitStack() as ph:
                    B = norm_bufs(ph, nm("ffn"))
                    wsB.slots = [sbp(ph, nm("wB"), (128, NJ * 128), BF16) for _ in range(2)]
                    p.op("dve", lambda e: e.memset(fence_t[:, 0:4], 0.0), writes=["fenceB"])
                    wsB.fence = "fenceB"
                    wsB.limit = wsB.cur + 2 * KC
                    act = sbp(ph, nm("act"), (128, NJ, STW), BF16)
                    sg = [sbp(ph, nm("sg"), (128, 352), BF16) for _ in range(2)]
                    hT = B["hT"]
                    Wgu = w_gu[which]
                    Wdn = w_dn[which]
                    for st in (0, 1):
                        c0 = st * STW
                        rmsnorm(st, nidx, B)
                        step = 0
                        for j in range(NJ):
                            def loader(slot, j=j):
                                v = slot[:, :].rearrange("p (kc two f) -> p kc two f", kc=KC, two=2)
                                return [
                                    (v[:, :, 0, :], Wgu[l, :, j * 128:(j + 1) * 128].rearrange("(kc p) f -> p kc f", p=128)),
                                    (v[:, :, 1, :], Wgu[l, :, DFF + j * 128:DFF + (j + 1) * 128].rearrange("(kc p) f -> p kc f", p=128)),
                                ]
                            slot, wkey = wsA.next(loader)
                            wv = slot[:, :].rearrange("p (kc two f) -> p kc two f", kc=KC, two=2)
                            for ti, (t0, t1) in enumerate(TILES):
                                n = t1 - t0
                                pb = step % 2
                                step += 1
                                gps, ups = ps[pb], ps[2 + pb]
                                for kc in range(KC):
                                    p.op("pe", (lambda e, kc=kc, gps=gps, t0=t0, t1=t1, n=n, wv=wv: e.matmul(
                                        gps[:, 0:n], wv[:, kc, 0, :], hT[:, kc, t0:t1], start=(kc == 0), stop=(kc == KC - 1))),
                                        reads=[wkey, (B["id"], "hT", kc)], writes=[("ps", pb)], inc=(kc == KC - 1))
                                for kc in range(KC):
                                    p.op("pe", (lambda e, kc=kc, ups=ups, t0=t0, t1=t1, n=n, wv=wv: e.matmul(
                                        ups[:, 0:n], wv[:, kc, 1, :], hT[:, kc, t0:t1], start=(kc == 0), stop=(kc == KC - 1))),
                                        reads=[wkey, (B["id"], "hT", kc)], writes=[("ps", 2 + pb)], inc=(kc == KC - 1))
                                p.op("act", (lambda e, gps=gps, pb=pb, n=n: e.activation(out=sg[pb][:, 0:n], in_=gps[:, 0:n], func=AF.Silu)),
                                     reads=[("ps", pb)], writes=[(B["id"], "sg", pb)])
                                p.op("dve", (lambda e, ups=ups, pb=pb, n=n, j=j, t0=t0, t1=t1: e.tensor_tensor(
                                    out=act[:, j, t0:t1], in0=sg[pb][:, 0:n], in1=ups[:, 0:n], op=ALU.mult)),
                                    reads=[(B["id"], "sg", pb), ("ps", 2 + pb)], writes=[(B["id"], "act", j, ti)])
                        step = 0
                        for dc in range(KC):
                            def loader2(slot, dc=dc):
                                v = slot[:, :].rearrange("p (j d) -> p j d", j=NJ)
                                return [(v, Wdn[l, :, dc * 128:(dc + 1) * 128].rearrange("(j p) d -> p j d", p=128))]
                            slot, wkey = wsB.next(loader2)
                            wv = slot[:, :].rearrange("p (j d) -> p j d", j=NJ)
                            for ti, (t0, t1) in enumerate(TILES):
                                n = t1 - t0
                                pb = 4 + (step % 2)
                                step += 1
                                ops_ = ps[pb]
                                for j in range(NJ):
                                    p.op("pe", (lambda e, j=j, ops_=ops_, n=n, t0=t0, t1=t1, wv=wv: e.matmul(
                                        ops_[:, 0:n], wv[:, j, :], act[:, j, t0:t1], start=(j == 0), stop=(j == NJ - 1))),
                                        reads=[wkey, (B["id"], "act", j, ti)], writes=[("ps", pb)], inc=(j == NJ - 1))
                                p.op("dve", (lambda e, ops_=ops_, n=n, dc=dc, t0=t0, t1=t1, c0=c0: e.scalar_tensor_tensor(
                                    out=xT[:, dc, c0 + t0:c0 + t1], in0=ops_[:, 0:n], scalar=0.5, in1=xT[:, dc, c0 + t0:c0 + t1],
                                    op0=ALU.mult, op1=ALU.add)),
                                    reads=[("ps", pb), ("xT", dc, st)], writes=[("xT", dc, st)])

            def final_phase():
                p.barrier()
                with ExitStack() as ph:
                    B = norm_bufs(ph, nm("fin"))
                    yo = [sbp(ph, nm("yo"), (128, STW), F32) for _ in range(2)]
                    for st in (0, 1):
                        c0 = st * STW
                        rmsnorm(st, 12, B, out_h=False)
                        for kc in range(KC):
                            b = kc % 2
                            p.op("dve", (lambda e, kc=kc, b=b, c0=c0: e.scalar_tensor_tensor(
                                out=yo[b][:, :], in0=xT[:, kc, c0:c0 + STW], scalar=gains[:, 12 * KC + kc:12 * KC + kc + 1],
                                in1=B["rstd"][:, :], op0=ALU.mult, op1=ALU.mult)),
                                reads=[("xT", kc, st), "gains"] + [(B["id"], "rstd", ti) for ti in range(3)], writes=[(B["id"], "yo", b)])
                            p.dma("sp", (lambda e, kc=kc, b=b, c0=c0: e.dma_start(out=yT_out[kc * 128:(kc + 1) * 128, c0:c0 + STW], in_=yo[b][:, :])),
                                  reads=[(B["id"], "yo", b)], writes=[("yout", kc, st)], dsem=p.dsem(f"ystore{b}"))

            def ssm_precompute(i):
                p.barrier()
                with ExitStack() as ph:
                    pid = nm("pc")
                    sc = sbp(ph, nm("sc"), (128, 40, 32), F32)
                    small = sbp(ph, nm("small"), (128, 3, 32), F32)
                    Bin = sbp(ph, nm("Bin"), (128, 2, 32, 16), F32)
                    Cin = sbp(ph, nm("Cin"), (128, 2, 32, 16), F32)
                    LPr = sbp(ph, nm("LPr"), (128, LCH + 1, 32), F32)
                    LPi = sbp(ph, nm("LPi"), (128, LCH + 1, 32), F32)
                    ILr = sbp(ph, nm("ILr"), (128, LCH + 1, 32), F32)
                    ILi = sbp(ph, nm("ILi"), (128, LCH + 1, 32), F32)
                    BBr = sbp(ph, nm("BBr"), (128, 32, 16), F32)
                    BBi = sbp(ph, nm("BBi"), (128, 32, 16), F32)
                    t1 = sbp(ph, nm("t1"), (128, 8, 256), F32)
                    t2 = sbp(ph, nm("t2"), (128, 8, 256), F32)
                    Xr = sbp(ph, nm("Xr"), (128, 8, 256), BF16)
                    Xi = sbp(ph, nm("Xi"), (128, 8, 256), BF16)
                    Yr = sbp(ph, nm("Yr"), (128, 8, 256), BF16)
                    Yn = sbp(ph, nm("Yn"), (128, 8, 256), BF16)
                    tsb = [sbp(ph, nm("tsb"), (128, 8, 384), BF16) for _ in range(2)]
                    wsb = [sbp(ph, nm("wsb"), (128, 8, 256), BF16) for _ in range(2)]
                    mask3 = sbp(ph, nm("mask3"), (128, 384), F32)
                    K_ = pid

                    def S(k):
                        return sc[:, k, :]

                    seq = {"n": 0}

                    def v(fn, reads, writes, eng="dve"):
                        p.op(eng, fn, reads=[(K_, r) for r in reads], writes=[(K_, w) for w in writes])

                    def tt(o, a, b, op, reads, writes, eng="dve"):
                        v((lambda e: e.tensor_tensor(out=o, in0=a, in1=b, op=op)), reads, writes, eng)

                    def ts(o, a, s1, s2, op0, op1, reads, writes):
                        v((lambda e: e.tensor_scalar(out=o, in0=a, scalar1=s1, scalar2=s2, op0=op0, op1=op1)), reads, writes)

                    def actf(o, a, func, scale, reads, writes):
                        v((lambda e: e.activation(out=o, in_=a, func=func, scale=scale)), reads, writes, "act")

                    p.dma("sp", (lambda e: e.dma_start(out=small[:, :, :], in_=ssm_small_in[i].rearrange("p (a g) -> p a g", a=3))),
                          reads=(), writes=[(K_, "small")], dsem=p.dsem("pc_small"))
                    p.dma("sp", (lambda e: e.dma_start(out=Bin[:, :, :, :], in_=ssm_b_in[i].rearrange("p (a g c) -> p a g c", a=2, g=32))),
                          reads=(), writes=[(K_, "Bin")], dsem=p.dsem("pc_b"))
                    p.dma("sp", (lambda e: e.dma_start(out=Cin[:, :, :, :], in_=ssm_c_in[i].rearrange("p (a g c) -> p a g c", a=2, g=32))),
                          reads=(), writes=[(K_, "Cin")], dsem=p.dsem("pc_c"))
                    p.dma("sp", (lambda e: e.dma_start(out=mask3[:, :], in_=mask3_in[:, :])),
                          reads=(), writes=[(K_, "mask3")], dsem=p.dsem("pc_m"))
                    a_re, a_im, ldt = small[:, 0, :], small[:, 1, :], small[:, 2, :]
                    DT, ARE, ANG, MAG, SN, SH_, CS, TA, TB, RINV, DEN, RDEN, NR, FRE, FIM, TC = range(16)
                    import math as _m
                    YS, UP, X2, TH = 16, 17, 18, 19

                    def stt(o, a, sc_, b, op0, op1, reads, writes):
                        v((lambda e: e.scalar_tensor_tensor(out=o, in0=a, scalar=sc_, in1=b, op0=op0, op1=op1)), reads, writes)

                    def horner(out_slot, x_slot, coefs, xkey, okey):
                        n_ = len(coefs) - 1
                        ts(S(UP), S(x_slot), float(coefs[n_]), None, ALU.mult, ALU.bypass, [xkey], ["up"])
                        for k_ in range(n_ - 1, 0, -1):
                            stt(S(UP), S(UP), float(coefs[k_]), S(x_slot), ALU.add, ALU.mult, ["up", xkey], ["up"])
                        ts(S(out_slot), S(UP), float(coefs[0]), None, ALU.add, ALU.bypass, ["up"], [okey])

                    ts(S(YS), ldt, 0.125, None, ALU.mult, ALU.bypass, ["small"], ["ys"])
                    horner(DT, YS, [1.0 / _m.factorial(k_) for k_ in range(13)], "ys", "dt")
                    for _ in range(3):
                        tt(S(DT), S(DT), S(DT), ALU.mult, ["dt"], ["dt"])
                    tt(S(ARE), a_re, S(DT), ALU.mult, ["small", "dt"], ["are"])
                    tt(S(ANG), a_im, S(DT), ALU.mult, ["small", "dt"], ["ang"])
                    horner(MAG, ARE, [1.0 / _m.factorial(k_) for k_ in range(9)], "are", "mag")
                    ts(S(TH), S(ANG), 1.0 / 16, None, ALU.mult, ALU.bypass, ["ang"], ["th"])
                    tt(S(X2), S(TH), S(TH), ALU.mult, ["th"], ["x2"])
                    horner(SH_, X2, [(-1.0) ** k_ / _m.factorial(2 * k_ + 1) for k_ in range(8)], "x2", "sh")
                    tt(S(SN), S(SH_), S(TH), ALU.mult, ["sh", "th"], ["sn"])
                    horner(CS, X2, [(-1.0) ** k_ / _m.factorial(2 * k_) for k_ in range(9)], "x2", "cs")
                    for _ in range(4):
                        tt(S(TA), S(CS), S(CS), ALU.mult, ["cs"], ["ta"])
                        tt(S(TB), S(SN), S(SN), ALU.mult, ["sn"], ["tb"])
                        v((lambda e: e.scalar_tensor_tensor(out=S(TC), in0=S(SN), scalar=2.0, in1=S(CS), op0=ALU.mult, op1=ALU.mult)),
                          ["sn", "cs"], ["tc"])
                        tt(S(CS), S(TA), S(TB), ALU.subtract, ["ta", "tb"], ["cs"])
                        v((lambda e: e.tensor_copy(out=S(SN), in_=S(TC))), ["tc"], ["sn"])
                    lam_re, lam_im = lamt[:, i, 0, :], lamt[:, i, 1, :]
                    tt(lam_re, S(MAG), S(CS), ALU.mult, ["mag", "cs"], ["lam"])
                    tt(lam_im, S(MAG), S(SN), ALU.mult, ["mag", "sn", "lam"], ["lam"])
                    tt(S(TA), S(MAG), S(MAG), ALU.mult, ["mag"], ["ta"])
                    v((lambda e: e.reciprocal(out=S(RINV), in_=S(TA))), ["ta"], ["rinv"])
                    v((lambda e: e.memset(LPr[:, 0, :], 1.0)), [], ["LP0"])
                    v((lambda e: e.memset(LPi[:, 0, :], 0.0)), ["LP0"], ["LP0"])
                    v((lambda e: e.tensor_copy(out=LPr[:, 1, :], in_=lam_re)), ["lam"], ["LP"])
                    v((lambda e: e.tensor_copy(out=LPi[:, 1, :], in_=lam_im)), ["lam", "LP"], ["LP"])
                    tt(ILr[:, 1, :], lam_re, S(RINV), ALU.mult, ["lam", "rinv"], ["IL"])
                    v((lambda e: e.scalar_tensor_tensor(out=ILi[:, 1, :], in0=lam_im, scalar=-1.0, in1=S(RINV), op0=ALU.mult, op1=ALU.mult)),
                      ["lam", "rinv", "IL"], ["IL"])
                    PT1 = sbp(ph, nm("pt1"), (128, 8, 32), F32)
                    PT2 = sbp(ph, nm("pt2"), (128, 8, 32), F32)
                    for (Pr, Pi, key) in ((LPr, LPi, "LP"), (ILr, ILi, "IL")):
                        n = 1
                        while n < LCH:
                            br = bcast(Pr[:, n:n + 1, :], (128, n, 32))
                            bi = bcast(Pi[:, n:n + 1, :], (128, n, 32))
                            tt(PT1[:, 0:n, :], Pr[:, 1:n + 1, :], br, ALU.mult, [key], ["pt1"])
                            tt(PT2[:, 0:n, :], Pi[:, 1:n + 1, :], bi, ALU.mult, [key], ["pt2"])
                            tt(Pr[:, n + 1:2 * n + 1, :], PT1[:, 0:n, :], PT2[:, 0:n, :], ALU.subtract, ["pt1", "pt2", key], [key + "w"])
                            tt(PT1[:, 0:n, :], Pr[:, 1:n + 1, :], bi, ALU.mult, [key, key + "w"], ["pt1"])
                            tt(PT2[:, 0:n, :], Pi[:, 1:n + 1, :], br, ALU.mult, [key, key + "w"], ["pt2"])
                            tt(Pi[:, n + 1:2 * n + 1, :], PT1[:, 0:n, :], PT2[:, 0:n, :], ALU.add, ["pt1", "pt2", key, key + "w"], [key])
                            n *= 2
                    v((lambda e: e.tensor_copy(out=lamt[:, i, 2, :], in_=LPr[:, LCH, :])), ["LP"], ["lamL"])
                    v((lambda e: e.tensor_copy(out=lamt[:, i, 3, :], in_=LPi[:, LCH, :])), ["LP", "lamL"], ["lamL"])
                    ts(lamt[:, i, 4, :], LPi[:, LCH, :], -1.0, None, ALU.mult, ALU.bypass, ["LP", "lamL"], ["lamL"])
                    tt(S(TA), a_re, a_re, ALU.mult, ["small"], ["ta"])
                    tt(S(TB), a_im, a_im, ALU.mult, ["small"], ["tb"])
                    tt(S(DEN), S(TA), S(TB), ALU.add, ["ta", "tb"], ["den"])
                    v((lambda e: e.reciprocal(out=S(RDEN), in_=S(DEN))), ["den"], ["rden"])
                    ts(S(NR), lam_re, -1.0, None, ALU.add, ALU.bypass, ["lam"], ["nr"])
                    tt(S(TA), S(NR), a_re, ALU.mult, ["nr", "small"], ["ta"])
                    tt(S(TB), lam_im, a_im, ALU.mult, ["lam", "small"], ["tb"])
                    tt(S(TC), S(TA), S(TB), ALU.add, ["ta", "tb"], ["tc"])
                    tt(S(FRE), S(TC), S(RDEN), ALU.mult, ["tc", "rden"], ["fre"])
                    tt(S(TA), lam_im, a_re, ALU.mult, ["lam", "small"], ["ta"])
                    tt(S(TB), S(NR), a_im, ALU.mult, ["nr", "small"], ["tb"])
                    tt(S(TC), S(TA), S(TB), ALU.subtract, ["ta", "tb"], ["tc"])
                    tt(S(FIM), S(TC), S(RDEN), ALU.mult, ["tc", "rden"], ["fim"])
                    fre_b = bcast(S(FRE).unsqueeze(2), (128, 32, 16))
                    fim_b = bcast(S(FIM).unsqueeze(2), (128, 32, 16))
                    tt(t1[:, 0:2, :].rearrange("p a (g c) -> p (a g) c", c=16), fre_b, Bin[:, 0, :, :], ALU.mult, ["fre", "Bin"], ["t1"])
                    tt(t2[:, 0:2, :].rearrange("p a (g c) -> p (a g) c", c=16), fim_b, Bin[:, 1, :, :], ALU.mult, ["fim", "Bin"], ["t2"])
                    tt(BBr[:, :, :], t1[:, 0:2, :].rearrange("p a (g c) -> p (a g) c", c=16), t2[:, 0:2, :].rearrange("p a (g c) -> p (a g) c", c=16),
                       ALU.subtract, ["t1", "t2"], ["BB"])
                    tt(t1[:, 0:2, :].rearrange("p a (g c) -> p (a g) c", c=16), fre_b, Bin[:, 1, :, :], ALU.mult, ["fre", "Bin", "BB"], ["t1"])
                    tt(t2[:, 0:2, :].rearrange("p a (g c) -> p (a g) c", c=16), fim_b, Bin[:, 0, :, :], ALU.mult, ["fim", "Bin", "BB"], ["t2"])
                    tt(BBi[:, :, :], t1[:, 0:2, :].rearrange("p a (g c) -> p (a g) c", c=16), t2[:, 0:2, :].rearrange("p a (g c) -> p (a g) c", c=16),
                       ALU.add, ["t1", "t2", "BB"], ["BB"])
                    for gb in range(4):
                        g0 = gb * 8

                        def pw(P):
                            return bcast(P[:, 1:LCH + 1, g0:g0 + 8].rearrange("p j g -> p g j").unsqueeze(3), (128, 8, LCH, 16))

                        def gc(X):
                            return bcast(X.unsqueeze(2), (128, 8, LCH, 16))

                        t1v = t1[:, :, :].rearrange("p g (s c) -> p g s c", c=16)
                        t2v = t2[:, :, :].rearrange("p g (s c) -> p g s c", c=16)

                        def cmul(outr, outi_neg, Pr, Pi, Ar, Ai, key, negate_im):
                            tt(t1v, pw(Pr), gc(Ar), ALU.mult, ["LP", "IL", "BB", "Cin", key], ["t1"])
                            tt(t2v, pw(Pi), gc(Ai), ALU.mult, ["LP", "IL", "BB", "Cin", key], ["t2"])
                            for sb_ in range(NBK):
                                ov = outr[:, :, sb_ * 128:(sb_ + 1) * 128].rearrange("p g (c s) -> p g s c", s=8)
                                tt(ov, t1v[:, :, sb_ * 8:(sb_ + 1) * 8, :], t2v[:, :, sb_ * 8:(sb_ + 1) * 8, :], ALU.subtract, ["t1", "t2"], [key + "r"])
                            tt(t1v, pw(Pr), gc(Ai), ALU.mult, ["LP", "IL", "BB", "Cin", key + "r"], ["t1"])
                            tt(t2v, pw(Pi), gc(Ar), ALU.mult, ["LP", "IL", "BB", "Cin", key + "r"], ["t2"])
                            for sb_ in range(NBK):
                                ov = outi_neg[:, :, sb_ * 128:(sb_ + 1) * 128].rearrange("p g (c s) -> p g s c", s=8)
                                a_, b_ = t1v[:, :, sb_ * 8:(sb_ + 1) * 8, :], t2v[:, :, sb_ * 8:(sb_ + 1) * 8, :]
                                tt(ov, a_, b_, ALU.add, ["t1", "t2"], [key + "i"])
                            if negate_im:
                                flat = outi_neg[:, :, :].rearrange("p g f -> p (g f)")
                                ts(flat, flat, -1.0, None, ALU.mult, ALU.bypass, [key + "i"], [key + "i"])

                        cmul(Xr, Xi, ILr, ILi, BBr[:, g0:g0 + 8, :], BBi[:, g0:g0 + 8, :], f"X", False)
                        cmul(Yr, Yn, LPr, LPi, Cin[:, 0, g0:g0 + 8, :], Cin[:, 1, g0:g0 + 8, :], f"Y", True)
                        p.dma("sp", (lambda e, g0=g0: e.dma_start(out=Y_scr[i, 0, :, g0 * 256:(g0 + 8) * 256], in_=Yr[:, :, :].rearrange("p g f -> p (g f)"))),
                              reads=[(K_, "Yr")], writes=[("Y_scr", i, 0, gb)], dsem=p.dsem("pc_y0"))
                        p.dma("sp", (lambda e, g0=g0: e.dma_start(out=Y_scr[i, 1, :, g0 * 256:(g0 + 8) * 256], in_=Yn[:, :, :].rearrange("p g f -> p (g f)"))),
                              reads=[(K_, "Yi")], writes=[("Y_scr", i, 1, gb)], dsem=p.dsem("pc_y1"))
                        for gh in range(2):
                            kc = gh * 4 + gb
                            hb = gh * 64
                            tb_ = tsb[gh]
                            wb_ = wsb[gh]
                            for g8 in range(8):
                                pb = g8 % 2
                                tps = ps[pb][:, 0:384].rearrange("p (b f) -> p b f", b=3)
                                wps = ps[2 + pb][:, 0:256].rearrange("p (b f) -> p b f", b=2)
                                for bi_, (sb_, ib_) in enumerate(((0, 0), (0, 1), (1, 1))):
                                    p.op("pe", (lambda e, tps=tps, bi_=bi_, sb_=sb_, ib_=ib_, g8=g8, hb=hb: e.matmul(
                                        tps[:, bi_, :], Xr[hb:hb + 64, g8, sb_ * 128:(sb_ + 1) * 128], Yr[hb:hb + 64, g8, ib_ * 128:(ib_ + 1) * 128],
                                        start=True, stop=False)), reads=[(K_, "Xr"), (K_, "Yr")], writes=[("ps", pb)], inc=False)
                                    p.op("pe", (lambda e, tps=tps, bi_=bi_, sb_=sb_, ib_=ib_, g8=g8, hb=hb: e.matmul(
                                        tps[:, bi_, :], Xi[hb:hb + 64, g8, sb_ * 128:(sb_ + 1) * 128], Yn[hb:hb + 64, g8, ib_ * 128:(ib_ + 1) * 128],
                                        start=False, stop=True)), reads=[(K_, "Xi"), (K_, "Yi")], writes=[("ps", pb)], inc=(bi_ == 2))
                                for sb_ in range(2):
                                    for ri, X in enumerate((Xr, Xi)):
                                        p.op("pe", (lambda e, wps=wps, sb_=sb_, ri=ri, X=X, g8=g8, hb=hb: e.matmul(
                                            wps[:, sb_, ri * 64:(ri + 1) * 64], X[hb:hb + 64, g8, sb_ * 128:(sb_ + 1) * 128], ident_bf[hb:hb + 64, hb:hb + 64],
                                            start=True, stop=True)), reads=[(K_, "Xr"), (K_, "Xi"), "ident"], writes=[("ps", 2 + pb)],
                                            inc=(sb_ == 1 and ri == 1))
                                p.op("dve", (lambda e, tps=tps, tb_=tb_, g8=g8: e.tensor_tensor(
                                    out=tb_[:, g8, :], in0=ps[g8 % 2][:, 0:384], in1=mask3[:, :], op=ALU.mult)),
                                    reads=[("ps", pb), (K_, "mask3")], writes=[(K_, "tsb", gh)])
                                p.op("act", (lambda e, wb_=wb_, g8=g8, pb=pb: e.activation(out=wb_[:, g8, :], in_=ps[2 + pb][:, 0:256], func=AF.Copy)),
                                     reads=[("ps", 2 + pb)], writes=[(K_, "wsb", gh)])
                            p.dma("sp", (lambda e, kc=kc, tb_=tb_: e.dma_start(out=T_scr[i, kc, :, :], in_=tb_[:, :, :].rearrange("p g f -> p (g f)"))),
                                  reads=[(K_, "tsb", gh)], writes=[("T_scr", i, kc)], dsem=p.dsem(f"pc_t{gh}"))
                            p.dma("sp", (lambda e, kc=kc, wb_=wb_: e.dma_start(out=W_scr[i, kc, :, :], in_=wb_[:, :, :].rearrange("p g f -> p (g f)"))),
                                  reads=[(K_, "wsb", gh)], writes=[("W_scr", i, kc)], dsem=p.dsem(f"pc_w{gh}"))

            def ssm_phase(l):
                i = l // 2
                nidx = 3 * l + 1
                p.barrier()
                with ExitStack() as ph:
                    B = {"id": nm("ssm")}
                    B["hT"] = sbp(ph, nm("hT"), (128, KC, STW), BF16)
                    K_ = B["id"]
                    hT = B["hT"]
                    U2 = sbp(ph, nm("U2"), (128, 64, NBK, NCH), BF16)
                    SH = sbp(ph, nm("SH"), (128, 2, 32, NCH + 1), BF16)
                    st_f = [sbp(ph, nm("stf"), (128, 2, 32), F32) for _ in range(2)]
                    wt = sbp(ph, nm("wt"), (128, 2, 32), F32)
                    sct1 = sbp(ph, nm("sct1"), (128, 2, 32), F32)
                    sct2 = sbp(ph, nm("sct2"), (128, 2, 32), F32)
                    hinit = sbp(ph, nm("hinit"), (128, 2, 32), F32)
                    usmp = sbp(ph, nm("usmp"), (128, KC, 16), BF16)
                    U2a = sbp(ph, nm("U2a"), (128, 64, 16), BF16)
                    U2b = sbp(ph, nm("U2b"), (128, 64, 16), BF16)
                    sin_b = sbp(ph, nm("sin_b"), (128, 2, 32, 16), BF16)
                    lr0 = sbp(ph, nm("lr0"), (128, 3, 32), F32)
                    p1 = ExitStack()
                    ph.enter_context(p1)
                    B["sq"] = [sbp(p1, nm("sq"), (128, STW), BF16) for _ in range(2)]
                    B["rstd"] = sbp(p1, nm("rstd"), (128, STW), F32)
                    B["rtmp"] = sbp(p1, nm("rtmp"), (128, STW), F32)
                    sin_f = sbp(p1, nm("sin_f"), (128, 2, 32, 16), F32)
                    nst = sbp(p1, nm("nst"), (128, 2, 32, 16), F32)
                    a1 = sbp(p1, nm("a1"), (128, 32, 16), F32)
                    a2 = sbp(p1, nm("a2"), (128, 32, 16), F32)
                    a3 = sbp(p1, nm("a3"), (128, 2, 32, 16), F32)
                    LR, LI, NLI = lamt[:, i, 2, :], lamt[:, i, 3, :], lamt[:, i, 4, :]
                    lam_re, lam_im = lamt[:, i, 0, :], lamt[:, i, 1, :]

                    p.op("dve", lambda e: e.tensor_scalar(out=lr0[:, 0, :], in0=LR, scalar1=flag[:, 1:2], scalar2=flag[:, 0:1], op0=ALU.mult, op1=ALU.add),
                         reads=["flag"], writes=[(K_, "lr0")])
                    p.op("dve", lambda e: e.tensor_scalar(out=lr0[:, 1, :], in0=LI, scalar1=flag[:, 1:2], scalar2=None, op0=ALU.mult),
                         reads=["flag", (K_, "lr0")], writes=[(K_, "lr0")])
                    p.op("dve", lambda e: e.tensor_scalar(out=lr0[:, 2, :], in0=NLI, scalar1=flag[:, 1:2], scalar2=None, op0=ALU.mult),
                         reads=["flag", (K_, "lr0")], writes=[(K_, "lr0")])
                    p.op("dve", lambda e: e.memset(U2a[:, :, :], 0.0), writes=[(K_, "U2a")])
                    p.op("dve", lambda e: e.memset(U2b[:, :, :], 0.0), writes=[(K_, "U2b")])
                    p.op("dve", lambda e: e.memset(SH[:, :, :, 0:1], 0.0), writes=[(K_, "SH", 0)])
                    p.dma("sp", (lambda e: e.dma_start(out=sin_f[:, :, :, :], in_=sst_in[i].rearrange("p (a g b) -> p a g b", a=2, g=32))),
                          reads=(), writes=[(K_, "sin_f")], dsem=p.dsem("sinload"))
                    p.op("act", lambda e: e.activation(out=sin_b[:, :, :, :], in_=sin_f[:, :, :, :], func=AF.Copy),
                         reads=[(K_, "sin_f")], writes=[(K_, "sin_b")])

                    if True:
                        Wm = [sbp(p1, nm("Wm"), (128, 8, 256), BF16) for _ in range(2)]
                        u_sm = [sbp(p1, nm("u_sm"), (128, 8, NBK, 65), BF16) for _ in range(2)]

                        def bsum(st, kc, Wmk, ncl, cg0, U2src, dst_fn, N, skip_sb0=False):
                            gh = kc // 4
                            hb = gh * 64
                            sps = pall[:, 2048:4096].rearrange("p (a b c) -> p a b c", a=2, b=8)
                            for g8 in range(8):
                                g = kc * 8 + g8
                                for ri in range(2):
                                    sbs = [1] if skip_sb0 else [0, 1]
                                    for sb_ in sbs:
                                        p.op("pe", (lambda e, g8=g8, ri=ri, sb_=sb_, g=g, sbs=sbs: e.matmul(
                                            sps[hb:hb + 64, ri, g8, 0:N], Wmk[:, g8, sb_ * 128 + ri * 64:sb_ * 128 + ri * 64 + 64],
                                            U2src(sb_, g), start=(sb_ == sbs[0]), stop=(sb_ == 1))),
                                            reads=[(K_, "Wm", kc % 2), (K_, "U2", kc, st), (K_, "U2b")],
                                            writes=[("ps", 4), ("ps", 5), ("ps", 6), ("ps", 7)], inc=(g8 == 7 and ri == 1 and sb_ == 1))
                            dst_fn(sps[hb:hb + 64, :, :, 0:N], hb)

                        for st in (0, 1):
                            c0 = st * STW
                            ncl, cg0 = NCL[st], CG0[st]
                            clo = 1 if st == 0 else 0
                            rmsnorm(st, nidx, B)
                            pend = None
                            for kc in range(KC):
                                def loader(slot, kc=kc):
                                    v_ = slot[:, 0:1024].rearrange("p (kc f) -> p kc f", kc=KC)
                                    return [(v_, ssm_win[i, :, kc * 128:(kc + 1) * 128].rearrange("(kc p) f -> p kc f", p=128))]
                                slot, wkey = wsA.next(loader)
                                wv = slot[:, 0:1024].rearrange("p (kc f) -> p kc f", kc=KC)
                                ub = u_sm[kc % 2]
                                for ti, (t0, t1) in enumerate(TILES):
                                    n = t1 - t0
                                    pb = (kc * 3 + ti) % 2
                                    for k2 in range(KC):
                                        p.op("pe", (lambda e, k2=k2, pb=pb, n=n, t0=t0, t1=t1, wv=wv: e.matmul(
                                            ps[pb][:, 0:n], wv[:, k2, :], hT[:, k2, t0:t1], start=(k2 == 0), stop=(k2 == KC - 1))),
                                            reads=[wkey, (K_, "hT", k2)], writes=[("ps", pb)], inc=(k2 == KC - 1))
                                    nmain = min(t1, NMAIN) - t0
                                    ca, cb = t0 // LCH, (t0 + nmain) // LCH
                                    p.op("act", (lambda e, pb=pb, nmain=nmain, ca=ca, cb=cb, ub=ub, clo=clo: e.activation(
                                        out=ub[:, :, :, clo + ca:clo + cb].rearrange("p s b c -> p c b s"),
                                        in_=ps[pb][:, 0:nmain].rearrange("p (c b s) -> p c b s", b=NBK, s=8), func=AF.Copy)),
                                        reads=[("ps", pb)], writes=[(K_, "u_sm", kc % 2)])
                                    if ti == 2:
                                        if st == 0:
                                            p.op("act", (lambda e, pb=pb, nmain=nmain, ub=ub: e.activation(
                                                out=ub[:, :, :, 0:1].rearrange("p s b c -> p c b s"),
                                                in_=ps[pb][:, nmain:nmain + 16].rearrange("p (c b s) -> p c b s", b=NBK, s=8), func=AF.Copy)),
                                                reads=[("ps", pb)], writes=[(K_, "u_sm", kc % 2)])
                                        else:
                                            p.op("act", (lambda e, pb=pb, nmain=nmain, kc=kc: e.activation(
                                                out=usmp[:, kc, :], in_=ps[pb][:, nmain:nmain + 16], func=AF.Copy)),
                                                reads=[("ps", pb)], writes=[(K_, "usmp")])
                                p.dma("sp", (lambda e, kc=kc, ub=ub: e.dma_start(
                                    out=scr_u[kc, :, :], in_=ub[:, :, :, :].rearrange("p s b c -> p (s b c)"))),
                                    reads=[(K_, "u_sm", kc % 2)], writes=[("scr_u", kc)], dsem=p.dsem(f"su_w{kc % 2}"))
                                for g8 in range(8):
                                    src = scr_u[kc, :, :].rearrange("(g c) (s x) -> g (c s) x", c=16, s=8)[g8].rearrange("q (b x) -> q b x", b=NBK)[:, :, 0:ncl]
                                    p.dma("sp", (lambda e, kc=kc, g8=g8, src=src, cg0=cg0, ncl=ncl: e.dma_start(
                                        out=U2[:, kc * 8 + g8, :, cg0:cg0 + ncl], in_=src)),
                                        reads=[("scr_u", kc)], writes=[(K_, "U2", kc, st)], dsem=p.dsem(f"su_r{kc}"))
                                p.dma("sp", (lambda e, kc=kc: e.dma_start(out=Wm[kc % 2][:, :, :], in_=W_scr[i, kc, :, :].rearrange("p (g f) -> p g f", g=8))),
                                      reads=[("W_scr", i, kc)], writes=[(K_, "Wm", kc % 2)], dsem=p.dsem(f"wm{kc % 2}"))

                                def do_b(kc=kc, st=st, ncl=ncl, cg0=cg0):
                                    def dst_fn(src_ps, hb, kc=kc):
                                        g32 = (kc % 4) * 8
                                        p.op("act", (lambda e: e.activation(
                                            out=SH[hb:hb + 64, :, g32:g32 + 8, 1 + cg0:1 + cg0 + ncl], in_=src_ps, func=AF.Copy)),
                                            reads=[("ps", 4), ("ps", 5), ("ps", 6), ("ps", 7)], writes=[(K_, "SH", 1 + st)])
                                    bsum(st, kc, Wm[kc % 2], ncl, cg0, (lambda sb_, g: U2[:, g, sb_, cg0:cg0 + ncl]), dst_fn, ncl)
                                if pend is not None:
                                    pend()
                                pend = do_b
                            pend()
                        p.dma("sp", (lambda e: e.dma_start(out=scr_su[:, :], in_=usmp[:, :, :].rearrange("p k b -> p (k b)"))),
                              reads=[(K_, "usmp")], writes=["scr_su"], dsem=p.dsem("ssu_w"))
                        srcs = scr_su[:, :].rearrange("(g c) (k b) -> c k g b", c=16, b=16)
                        p.dma("sp", (lambda e: e.dma_start(out=U2a[0:128:8, :, :].rearrange("c (k g) b -> c k g b", g=8), in_=srcs)),
                              reads=["scr_su", (K_, "U2a")], writes=[(K_, "U2a")], dsem=p.dsem("ssu_a"))
                        p.dma("sp", (lambda e: e.dma_start(out=U2b[7:128:8, :, :].rearrange("c (k g) b -> c k g b", g=8), in_=srcs)),
                              reads=["scr_su", (K_, "U2b")], writes=[(K_, "U2b")], dsem=p.dsem("ssu_b"))
                        for kc in range(KC):
                            p.dma("sp", (lambda e, kc=kc: e.dma_start(out=Wm[kc % 2][:, :, :], in_=W_scr[i, kc, :, :].rearrange("p (g f) -> p g f", g=8))),
                                  reads=[("W_scr", i, kc)], writes=[(K_, "Wm", kc % 2)], dsem=p.dsem(f"wm{kc % 2}"))

                            def dst_fn(src_ps, hb, kc=kc):
                                g32 = (kc % 4) * 8
                                p.op("act", (lambda e: e.activation(out=nst[hb:hb + 64, :, g32:g32 + 8, :], in_=src_ps, func=AF.Copy)),
                                     reads=[("ps", 4), ("ps", 5), ("ps", 6), ("ps", 7)], writes=[(K_, "nst")])
                            bsum(1, kc, Wm[kc % 2], 16, 0, (lambda sb_, g: U2b[:, g, :]), dst_fn, 16, skip_sb0=True)

                    if True:

                        def b16(t):
                            return bcast(t.unsqueeze(2), (128, 32, 16))

                        def cm(out, xr, xi, cr, ci, key_r):
                            p.op("dve", lambda e: e.tensor_tensor(out=a1[:, :, :], in0=xr, in1=b16(cr), op=ALU.mult), reads=key_r, writes=[(K_, "a1")])
                            p.op("dve", lambda e: e.tensor_tensor(out=a2[:, :, :], in0=xi, in1=b16(ci), op=ALU.mult), reads=key_r, writes=[(K_, "a2")])
                            p.op("dve", lambda e: e.tensor_tensor(out=out[:, 0, :, :], in0=a1[:, :, :], in1=a2[:, :, :], op=ALU.subtract),
                                 reads=[(K_, "a1"), (K_, "a2")], writes=[(K_, "cmo")])
                            p.op("dve", lambda e: e.tensor_tensor(out=a1[:, :, :], in0=xi, in1=b16(cr), op=ALU.mult), reads=key_r + [(K_, "cmo")], writes=[(K_, "a1")])
                            p.op("dve", lambda e: e.tensor_tensor(out=a2[:, :, :], in0=xr, in1=b16(ci), op=ALU.mult), reads=key_r + [(K_, "cmo")], writes=[(K_, "a2")])
                            p.op("dve", lambda e: e.tensor_tensor(out=out[:, 1, :, :], in0=a1[:, :, :], in1=a2[:, :, :], op=ALU.add),
                                 reads=[(K_, "a1"), (K_, "a2")], writes=[(K_, "cmo")])

                        cm(a3, nst[:, 0, :, :], nst[:, 1, :, :], LR, LI, [(K_, "nst")])
                        p.op("dve", lambda e: e.tensor_copy(out=nst[:, :, :, :], in_=a3[:, :, :, :]), reads=[(K_, "cmo")], writes=[(K_, "nst")])
                        cm(a3, sin_f[:, 0, :, :], sin_f[:, 1, :, :], lam_re, lam_im, [(K_, "sin_f"), (K_, "nst")])
                        p.op("dve", lambda e: e.tensor_tensor(out=nst[:, :, :, :], in0=nst[:, :, :, :], in1=a3[:, :, :, :], op=ALU.add),
                             reads=[(K_, "cmo"), (K_, "nst")], writes=[(K_, "nst")])
                        p.dma("sp", (lambda e: e.dma_start(out=sssm_out[i, :, :], in_=nst[:, :, :, :].rearrange("p a g b -> p (a g b)"))),
                              reads=[(K_, "nst")], writes=[("sssm", i)], dsem=p.dsem("sssm_st"))

                    def scan(init_ap, tag):
                        cur = 0
                        if init_ap is None:
                            p.op("dve", lambda e: e.memset(st_f[0][:, :, :], 0.0), reads=[], writes=[(K_, "stf", 0)])
                        else:
                            p.op("dve", lambda e: e.tensor_copy(out=st_f[0][:, :, :], in_=init_ap), reads=[(K_, "hinit")], writes=[(K_, "stf", 0)])
                            p.op("act", lambda e: e.activation(out=SH[:, :, :, 0], in_=init_ap, func=AF.Copy), reads=[(K_, "hinit")], writes=[(K_, "SH", 0)])
                        for cg in range(NCH):
                            s_, d_ = st_f[cur], st_f[1 - cur]
                            last = (cg == NCH - 1)
                            p.op("dve", (lambda e, s_=s_, cg=cg: e.tensor_tensor(out=wt[:, :, :], in0=s_[:, :, :], in1=SH[:, :, :, 1 + cg], op=ALU.add)),
                                 reads=[(K_, "stf", cur), (K_, "SH", 1), (K_, "SH", 2), (K_, "SHs", tag, cg)], writes=[(K_, "wt")])
                            LRc, LIc, NLIc = (lr0[:, 0, :], lr0[:, 1, :], lr0[:, 2, :]) if cg == 0 else (LR, LI, NLI)
                            p.op("dve", (lambda e, LRc=LRc: e.tensor_tensor(out=sct1[:, :, :], in0=wt[:, :, :], in1=bcast(LRc.unsqueeze(1), (128, 2, 32)), op=ALU.mult)),
                                 reads=[(K_, "wt"), (K_, "lr0")], writes=[(K_, "sct1")])
                            p.op("dve", (lambda e, NLIc=NLIc: e.tensor_tensor(out=sct2[:, 0, :], in0=wt[:, 1, :], in1=NLIc, op=ALU.mult)),
                                 reads=[(K_, "wt"), (K_, "lr0")], writes=[(K_, "sct2a")])
                            p.op("dve", (lambda e, LIc=LIc: e.tensor_tensor(out=sct2[:, 1, :], in0=wt[:, 0, :], in1=LIc, op=ALU.mult)),
                                 reads=[(K_, "wt"), (K_, "lr0")], writes=[(K_, "sct2b")])
                            p.op("dve", (lambda e, d_=d_: e.tensor_tensor(out=d_[:, :, :], in0=sct1[:, :, :], in1=sct2[:, :, :], op=ALU.add)),
                                 reads=[(K_, "sct1"), (K_, "sct2a"), (K_, "sct2b")], writes=[(K_, "stf", 1 - cur)])
                            if not last:
                                p.op("act", (lambda e, d_=d_, cg=cg: e.activation(out=SH[:, :, :, 1 + cg], in_=d_[:, :, :], func=AF.Copy)),
                                     reads=[(K_, "stf", 1 - cur)], writes=[(K_, "SHs", tag, cg)])
                            cur = 1 - cur
                        return st_f[cur], cur

                    def scan_final_only():
                        cur = 0
                        p.op("dve", lambda e: e.memset(st_f[0][:, :, :], 0.0), reads=[], writes=[(K_, "stf", 0)])
                        for cg in range(NCH):
                            s_, d_ = st_f[cur], st_f[1 - cur]
                            p.op("dve", (lambda e, s_=s_, cg=cg: e.tensor_tensor(out=wt[:, :, :], in0=s_[:, :, :], in1=SH[:, :, :, 1 + cg], op=ALU.add)),
                                 reads=[(K_, "stf", cur), (K_, "SH", 1), (K_, "SH", 2)], writes=[(K_, "wt")])
                            LRc, LIc, NLIc = (lr0[:, 0, :], lr0[:, 1, :], lr0[:, 2, :]) if cg == 0 else (LR, LI, NLI)
                            p.op("dve", (lambda e, LRc=LRc: e.tensor_tensor(out=sct1[:, :, :], in0=wt[:, :, :], in1=bcast(LRc.unsqueeze(1), (128, 2, 32)), op=ALU.mult)),
                                 reads=[(K_, "wt"), (K_, "lr0")], writes=[(K_, "sct1")])
                            p.op("dve", (lambda e, NLIc=NLIc: e.tensor_tensor(out=sct2[:, 0, :], in0=wt[:, 1, :], in1=NLIc, op=ALU.mult)),
                                 reads=[(K_, "wt"), (K_, "lr0")], writes=[(K_, "sct2a")])
                            p.op("dve", (lambda e, LIc=LIc: e.tensor_tensor(out=sct2[:, 1, :], in0=wt[:, 0, :], in1=LIc, op=ALU.mult)),
                                 reads=[(K_, "wt"), (K_, "lr0")], writes=[(K_, "sct2b")])
                            p.op("dve", (lambda e, d_=d_: e.tensor_tensor(out=d_[:, :, :], in0=sct1[:, :, :], in1=sct2[:, :, :], op=ALU.add)),
                                 reads=[(K_, "sct1"), (K_, "sct2a"), (K_, "sct2b")], writes=[(K_, "stf", 1 - cur)])
                            cur = 1 - cur
                        return st_f[cur], cur

                    p.op("dve", lambda e: e.tensor_scalar(out=SH[:, :, :, 1], in0=SH[:, :, :, 1], scalar1=flag[:, 1:2], scalar2=None, op0=ALU.mult),
                         reads=[(K_, "SH", 1), "flag"], writes=[(K_, "SH", 1)])
                    fin, fcur = scan_final_only()
                    p.dma("sp", (lambda e, fin=fin: e.dma_start(out=cc_in[:, :], in_=fin[:, :, :].rearrange("p a g -> p (a g)"))),
                          reads=[(K_, "stf", fcur)], writes=["cc_in"], dsem=p.dsem(f"ccin"))
                    if DEBUG.get("no_cc"):
                        p.dma("sp", lambda e: e.dma_start(out=cc_out[0:128, :], in_=cc_in[:, :]), reads=["cc_in"], writes=["cc_out"], dsem=p.dsem("ccfake"))
                    else:
                        p.dma("pool", (lambda e: e.collective_compute("AllGather", ALU.bypass, replica_groups=RG,
                                                                      ins=[cc_in.ap().opt()], outs=[cc_out.ap().opt()])),
                              reads=["cc_in"], writes=["cc_out"], dsem=p.dsem("ccsem", 1))
                    p.dma("sp", (lambda e: e.dma_start(out=hinit[:, :, :], in_=cc_out[0:128, :].rearrange("p (a g) -> p a g", a=2))),
                          reads=["cc_out"], writes=[(K_, "hinit")], dsem=p.dsem("ccrd"))
                    p.op("dve", lambda e: e.tensor_scalar(out=hinit[:, :, :], in0=hinit[:, :, :], scalar1=flag[:, 0:1], scalar2=None, op0=ALU.mult),
                         reads=[(K_, "hinit"), "flag"], writes=[(K_, "hinit")])
                    fin2, fcur2 = scan(hinit[:, :, :], "s2")
                    p.dma("sp", (lambda e, fin2=fin2: e.dma_start(out=pssm_out[i, :, :], in_=fin2[:, :, :].rearrange("p a g -> p (a g)"))),
                          reads=[(K_, "stf", fcur2)], writes=[("pssm", i)], dsem=p.dsem("pssm_st"))
                    p.barrier()
                    p1.close()

                    with ExitStack() as p4:
                        Tm = [sbp(p4, nm("Tm"), (128, 8, 384), BF16) for _ in range(1)]
                        Ym = [sbp(p4, nm("Ym"), (128, 2, 8, 256), BF16) for _ in range(1)]
                        G2 = [sbp(p4, nm("G2"), (128, 8, NBK, 65), BF16) for _ in range(2)]
                        e1 = sbp(p4, nm("e1"), (128, LCH * 65), F32)
                        e2 = sbp(p4, nm("e2"), (128, LCH * 65), F32)
                        gsm = sbp(p4, nm("gsm"), (128, 64, 16), BF16)
                        gTs = sbp(p4, nm("gTs"), (128, KC, 16), BF16)
                        sgm = [sbp(p4, nm("sgm"), (128, 400), F32) for _ in range(2)]
                        gT = hT
                        yps = pall[:, 2048:4096].rearrange("p (a b c) -> p a b c", a=8, b=2)

                        def gelu_chain(src_ps_view, u_view, d_view, shape, n_el, out_bf, rkeys, wkey, npart=128):
                            v1 = e1[0:npart, 0:n_el]
                            v2 = e2[0:npart, 0:n_el]

                            def rs(v_):
                                if len(shape) == 3:
                                    return v_.rearrange("p (a b c) -> p a b c", a=shape[0], b=shape[1])
                                return v_.rearrange("p (a b) -> p a b", a=shape[0])
                            p.op("dve", lambda e: e.tensor_tensor(out=rs(v1), in0=u_view, in1=d_view, op=ALU.mult), reads=rkeys, writes=[(K_, "e1")])
                            p.op("dve", lambda e: e.tensor_tensor(out=rs(v1), in0=rs(v1), in1=src_ps_view, op=ALU.add),
                                 reads=[(K_, "e1"), ("ps", 4), ("ps", 5), ("ps", 6), ("ps", 7)], writes=[(K_, "e1")])
                            p.op("dve", lambda e: e.tensor_tensor(out=v2, in0=v1, in1=v1, op=ALU.mult), reads=[(K_, "e1")], writes=[(K_, "e2")])
                            p.op("dve", lambda e: e.tensor_scalar(out=v2, in0=v2, scalar1=0.044715, scalar2=1.0, op0=ALU.mult, op1=ALU.add),
                                 reads=[(K_, "e2")], writes=[(K_, "e2")])
                            p.op("dve", lambda e: e.tensor_tensor(out=v2, in0=v2, in1=v1, op=ALU.mult), reads=[(K_, "e2"), (K_, "e1")], writes=[(K_, "e2")])
                            p.op("act", lambda e: e.activation(out=v2, in_=v2, func=AF.Sigmoid, scale=1.5957691216057308),
                                 reads=[(K_, "e2")], writes=[(K_, "e2")])
                            p.op("dve", lambda e: e.tensor_tensor(out=out_bf, in0=rs(v2), in1=rs(v1), op=ALU.mult),
                                 reads=[(K_, "e2"), (K_, "e1")], writes=[wkey])

                        def ymat(kc, Tmk, Ymk, N, u_fn, h_fn, only_ib0=False):
                            gh = kc // 4
                            hb = gh * 64
                            for g8 in range(8):
                                g = kc * 8 + g8
                                g32 = g % 32
                                for ib_ in ([0] if only_ib0 else [0, 1]):
                                    mms = []
                                    for sb_ in range(ib_ + 1):
                                        blk = {(0, 0): 0, (0, 1): 1, (1, 1): 2}[(sb_, ib_)]
                                        mms.append((Tmk[:, g8, blk * 128:(blk + 1) * 128], u_fn(sb_, g)))
                                    mms.append((Ymk[hb:hb + 64, 0, g8, ib_ * 128:(ib_ + 1) * 128], h_fn(hb, 0, g32)))
                                    mms.append((Ymk[hb:hb + 64, 1, g8, ib_ * 128:(ib_ + 1) * 128], h_fn(hb, 1, g32)))
                                    for mi, (lh, rh) in enumerate(mms):
                                        p.op("pe", (lambda e, lh=lh, rh=rh, mi=mi, nm_=len(mms), ib_=ib_, g8=g8: e.matmul(
                                            yps[:, g8, ib_, 0:N], lh, rh, start=(mi == 0), stop=(mi == nm_ - 1))),
                                            reads=[(K_, "Tm", 0), (K_, "Ym"), (K_, "U2a"), (K_, "sin_b"), (K_, "SH", 0), (K_, "SH", 1), (K_, "SH", 2)]
                                            + [(K_, "SHs", "s2", c_) for c_ in (0, NCH - 2)] + [(K_, "U2", kc, 0), (K_, "U2", kc, 1)],
                                            writes=[("ps", 4), ("ps", 5), ("ps", 6), ("ps", 7)],
                                            inc=(g8 == 7 and mi == len(mms) - 1 and (only_ib0 or ib_ == 1)))

                        def load_mats(kc):
                            gh = kc // 4
                            hb = gh * 64
                            g32 = (kc % 4) * 8
                            p.dma("sp", (lambda e: e.dma_start(out=Tm[0][:, :, :], in_=T_scr[i, kc, :, :].rearrange("p (g f) -> p g f", g=8))),
                                  reads=[("T_scr", i, kc)], writes=[(K_, "Tm", 0)], dsem=p.dsem("tm0"))
                            for ri in range(2):
                                p.dma("sp", (lambda e, ri=ri: e.dma_start(
                                    out=Ym[0][hb:hb + 64, ri, :, :],
                                    in_=Y_scr[i, ri, hb:hb + 64, g32 * 256:(g32 + 8) * 256].rearrange("p (g f) -> p g f", g=8))),
                                    reads=[("Y_scr", i, ri, kc % 4)], writes=[(K_, "Ym")], dsem=p.dsem(f"ym"))

                        for st in (0, 1):
                            c0 = st * STW
                            ncl, cg0 = NCL[st], CG0[st]
                            for kc in range(KC):
                                load_mats(kc)
                                ymat(kc, Tm[0], Ym[0], ncl,
                                     (lambda sb_, g: U2[:, g, sb_, cg0:cg0 + ncl]),
                                     (lambda hb, ri, g32: SH[hb:hb + 64, ri, g32, cg0:cg0 + ncl]))
                                gb_ = G2[kc % 2]
                                gelu_chain(yps[:, :, :, 0:ncl],
                                           U2[:, kc * 8:(kc + 1) * 8, :, cg0:cg0 + ncl],
                                           bcast(dsh[:, i, kc * 8:(kc + 1) * 8].unsqueeze(2).unsqueeze(3), (128, 8, 2, ncl)),
                                           (8, 2, ncl), 16 * ncl,
                                           gb_[:, :, :, 0:ncl],
                                           [(K_, "U2", kc, st), "dsh"], (K_, "G2", kc % 2))
                                p.dma("sp", (lambda e, kc=kc, gb_=gb_: e.dma_start(
                                    out=scr_y[kc, :, :], in_=gb_[:, :, :, :].rearrange("p g b c -> p (g b c)"))),
                                    reads=[(K_, "G2", kc % 2)], writes=[("scr_y", kc)], dsem=p.dsem(f"sy_w{kc % 2}"))
                                for g8 in range(8):
                                    for ib_ in range(NBK):
                                        src = scr_y[kc, :, :].rearrange("(c i) (g b x) -> g b c i x", i=8, g=8, b=NBK)[g8, ib_][:, :, 0:ncl]
                                        p.dma("sp", (lambda e, kc=kc, g8=g8, ib_=ib_, src=src, ncl=ncl: e.dma_start(
                                            out=gT[g8 * 16:(g8 + 1) * 16, kc, :].rearrange("c (b i x) -> c b i x", b=NBK, i=8)[:, ib_, :, 0:ncl], in_=src)),
                                            reads=[("scr_y", kc)], writes=[(K_, "hT", kc)], dsem=p.dsem(f"sy_r{kc}"))
                            if st == 1:
                                for kc in range(KC):
                                    load_mats(kc)
                                    ymat(kc, Tm[0], Ym[0], 16,
                                         (lambda sb_, g: U2a[:, g, :]),
                                         (lambda hb, ri, g32: sin_b[hb:hb + 64, ri, g32, :]), only_ib0=True)
                                    gelu_chain(yps[:, :, 0, 0:16],
                                               U2a[:, kc * 8:(kc + 1) * 8, :],
                                               bcast(dsh[:, i, kc * 8:(kc + 1) * 8].unsqueeze(2), (128, 8, 16)),
                                               (8, 16), 128, gsm[:, kc * 8:(kc + 1) * 8, :], [(K_, "U2a"), "dsh"], (K_, "gsm"))
                                p.dma("sp", (lambda e: e.dma_start(out=scr_sy[:, :], in_=gsm[0:128:8, :, :].rearrange("c g b -> c (g b)"))),
                                      reads=[(K_, "gsm")], writes=["scr_sy"], dsem=p.dsem("ssy_w"))
                                for g8 in range(8):
                                    p.dma("sp", (lambda e, g8=g8: e.dma_start(out=gTs[g8 * 16:(g8 + 1) * 16, :, :],
                                                                       in_=scr_sy[:, :].rearrange("c (k g b) -> g c k b", g=8, b=16)[g8])),
                                          reads=["scr_sy"], writes=[(K_, "gTs")], dsem=p.dsem("ssy_r"))
                            tiles = [(s0 * 65, s1 * 65, s0, s1) for (s0, s1) in STILES]
                            if st == 1:
                                tiles.append((None, None, None, None))
                            step = 0
                            for dc in range(KC):
                                def loaderg(slot, dc=dc):
                                    v_ = slot[:, :].rearrange("p (kc two f) -> p kc two f", kc=KC, two=2)
                                    return [
                                        (v_[:, :, 0, :], ssm_wglu[i, :, dc * 128:(dc + 1) * 128].rearrange("(kc p) f -> p kc f", p=128)),
                                        (v_[:, :, 1, :], ssm_wglu[i, :, D + dc * 128:D + (dc + 1) * 128].rearrange("(kc p) f -> p kc f", p=128)),
                                    ]
                                slot, wkey = wsA.next(loaderg)
                                wv = slot[:, :].rearrange("p (kc two f) -> p kc two f", kc=KC, two=2)
                                for (q0, q1, s0, s1) in tiles:
                                    smp = q0 is None
                                    n = 16 if smp else q1 - q0
                                    pb = step % 2
                                    step += 1
                                    for half in range(2):
                                        for k2 in range(KC):
                                            rhs = gTs[:, k2, :] if smp else gT[:, k2, q0:q1]
                                            p.op("pe", (lambda e, k2=k2, half=half, pb=pb, n=n, rhs=rhs, wv=wv: e.matmul(
                                                ps[2 * half + pb][:, 0:n], wv[:, k2, half, :], rhs, start=(k2 == 0), stop=(k2 == KC - 1))),
                                                reads=[wkey, (K_, "hT", k2), (K_, "gTs")], writes=[("ps", 2 * half + pb)], inc=(k2 == KC - 1))
                                    sg_ = sgm[pb]
                                    p.op("act", (lambda e, pb=pb, n=n, sg_=sg_: e.activation(out=sg_[:, 0:n], in_=ps[2 + pb][:, 0:n], func=AF.Sigmoid)),
                                         reads=[("ps", 2 + pb)], writes=[(K_, "sgm", pb)])
                                    p.op("dve", (lambda e, pb=pb, n=n, sg_=sg_: e.tensor_tensor(out=sg_[:, 0:n], in0=sg_[:, 0:n], in1=ps[pb][:, 0:n], op=ALU.mult)),
                                         reads=[(K_, "sgm", pb), ("ps", pb)], writes=[(K_, "sgm", pb)])
                                    if smp:
                                        p.op("dve", (lambda e, dc=dc, sg_=sg_: e.tensor_tensor(
                                            out=xT[:, dc, 2064:2080], in0=xT[:, dc, 2064:2080], in1=sg_[:, 0:16], op=ALU.add)),
                                            reads=[(K_, "sgm", pb), ("xT", dc, 1)], writes=[("xT", dc, 1)])
                                    else:
                                        ns = s1 - s0
                                        clo = 1 if st == 0 else 0
                                        nmc = ncl - clo
                                        xv = xT[:, dc, c0 + s0:c0 + s0 + 1024].rearrange("p (c s) -> p s c", s=LCH)[:, 0:ns, :]
                                        tv = sg_[:, 0:n].rearrange("p (s c) -> p s c", c=65)[:, :, clo:clo + 64]
                                        p.op("dve", (lambda e, xv=xv, tv=tv: e.tensor_tensor(out=xv, in0=xv, in1=tv, op=ALU.add)),
                                             reads=[(K_, "sgm", pb), ("xT", dc, st)], writes=[("xT", dc, st)])
                                        if st == 0:
                                            xp_ = xT[:, dc, 1024 + s0:1024 + s1]
                                            tp_ = sg_[:, 0:n].rearrange("p (s c) -> p s c", c=65)[:, :, 0]
                                            p.op("dve", (lambda e, xp_=xp_, tp_=tp_: e.tensor_tensor(out=xp_, in0=xp_, in1=tp_, op=ALU.add)),
                                                 reads=[(K_, "sgm", pb), ("xT", dc, st)], writes=[("xT", dc, st)])

            def ab_phase(l):
                i = l // 2
                nidx = 3 * l + 1
                p.barrier()
                NEGSC = 0.125
                with ExitStack() as ph:
                    K_ = nm("ab")
                    B = {"id": K_}
                    B["hT"] = sbp(ph, nm("hT"), (128, KC, STW), BF16)
                    hT = B["hT"]
                    qT = sbp(ph, nm("qT"), (128, 4, STW), BF16)
                    kT = sbp(ph, nm("kT"), (128, 2, 128 + STW), BF16)
                    vtok = sbp(ph, nm("vtok"), (128, 10, 128), BF16)
                    zT = sbp(ph, nm("zT"), (128, 4, 30 + STW), BF16)
                    zpre = sbp(ph, nm("zpre"), (128, 4, 46), BF16)
                    aoT = sbp(ph, nm("aoT"), (128, 4, STW), BF16)
                    coT = sbp(ph, nm("coT"), (128, 4, STW), BF16)
                    ropeT = sbp(ph, nm("ropeT"), (128, 2, STW), F32)
                    abc = sbp(ph, nm("abc"), (128, 140), F32)
                    esink = sbp(ph, nm("esink"), (128, 4), F32)
                    rmat = sbp(ph, nm("rmat"), (128, 128), BF16)
                    b2m = sbp(ph, nm("b2m"), (128, 128), BF16)
                    onesf = sbp(ph, nm("onesf"), (128, 128), F32)
                    amask = sbp(ph, nm("amask"), (128, 4, 128), BF16)
                    rpk = sbp(ph, nm("rpk"), (128, 512), BF16)
                    ksf = sbp(ph, nm("ksf"), (128, 2, 16), F32)
                    vsf = sbp(ph, nm("vsf"), (16, 128), F32)
                    vsb = sbp(ph, nm("vsb"), (16, 128), BF16)
                    cst = sbp(ph, nm("cst"), (128, 768), F32)
                    convw = abc[:, 0:124].rearrange("p (c j) -> p c j", c=4)
                    convb, lng, lnb = abc[:, 124:128], abc[:, 128:132], abc[:, 132:136]

                    p.dma("sp", lambda e: e.dma_start(out=abc[:, :], in_=abc_in[i, :, :]), reads=(), writes=[(K_, "abc")], dsem=p.dsem("abc_l"))
                    p.dma("sp", lambda e: e.dma_start(out=cst[:, 0:128], in_=rmat_in[:, :]), reads=(), writes=[(K_, "cst0")], dsem=p.dsem("cst0"))
                    p.dma("sp", lambda e: e.dma_start(out=cst[:, 128:256], in_=b2_in[:, :]), reads=(), writes=[(K_, "cst1")], dsem=p.dsem("cst1"))
                    p.op("dve", lambda e: e.tensor_copy(out=rmat[:, :], in_=cst[:, 0:128]), reads=[(K_, "cst0")], writes=[(K_, "rmat")])
                    p.op("dve", lambda e: e.tensor_copy(out=b2m[:, :], in_=cst[:, 128:256]), reads=[(K_, "cst1")], writes=[(K_, "b2m")])
                    p.op("dve", lambda e: e.memset(onesf[:, :], 1.0 / 512), writes=[(K_, "onesf")])
                    p.op("act", lambda e: e.activation(out=esink[:, :], in_=abc[:, 136:140], func=AF.Exp), reads=[(K_, "abc")], writes=[(K_, "esink")])
                    p.dma("sp", lambda e: e.dma_start(out=cst[:, 256:768], in_=amask_in[:, :]), reads=(), writes=[(K_, "cst2")], dsem=p.dsem("cst2"))
                    p.op("dve", lambda e: e.tensor_copy(out=amask[:, :, :].rearrange("p a b -> p (a b)"), in_=cst[:, 256:768]), reads=[(K_, "cst2")], writes=[(K_, "amask")])
                    p.op("dve", lambda e: e.memset(zpre[:, :, 0:30], 0.0), writes=[(K_, "zpre")])
                    p.op("dve", lambda e: e.memset(vtok[:, 9, :], 0.0), writes=[(K_, "vtok", 9)])

                    if DEBUG.get("ab_stop") == 101:
                        p.barrier()
                        return

                    def load_rope(st):
                        p.dma("sp", (lambda e: e.dma_start(out=ropeT[:, :, :], in_=rope_in[st].rearrange("a p n -> p a n"))),
                              reads=(), writes=[(K_, "rope")], dsem=p.dsem("rope_l"))

                    def project(cols, hsrc, hkeys, rope_cols, dst, tag, with_q=True, sample_cols=None):
                        T = dst["tmp"]
                        fills = []
                        if with_q:
                            fills += [("q", 0), ("q", 2)]
                        fills += [("k", 0)] + [("z", c) for c in range(4)] + [("v", 0)]
                        if DEBUG.get("ab_fills"):
                            fills = [f_ for f_ in fills if f_[0] in DEBUG["ab_fills"]]
                        step = [0]
                        for kind, c in fills:
                            if kind == "q":
                                def loader(slot, c=c):
                                    v_ = slot[:, :].rearrange("p (kc f) -> p kc f", kc=KC)
                                    return [(v_, ab_w_in[i, :, c * 128:(c + 2) * 128].rearrange("(kc p) f -> p kc f", p=128))]
                            elif kind == "k":
                                def loader(slot):
                                    if DEBUG.get("kload") == 1:
                                        v2_ = slot[:, :].rearrange("p (kc f) -> p kc f", kc=KC)
                                        return [(v2_, ab_w_in[i, :, 512:768].rearrange("(kc p) f -> p kc f", p=128))]
                                    v_ = slot[:, :].rearrange("p (kc a f) -> p kc a f", kc=KC, a=4)
                                    return [(v_[:, :, a, :], ab_w_in[i, :, 512 + 64 * (a // 2):512 + 64 * (a // 2) + 64].rearrange("(kc p) f -> p kc f", p=128))
                                            for a in range(4)]
                            elif kind == "z":
                                def loader(slot, c=c):
                                    v_ = slot[:, :].rearrange("p (kc a f) -> p kc a f", kc=KC, a=2)
                                    return [(v_[:, :, 0, :], ab_w_in[i, :, 768 + c * 128:768 + (c + 1) * 128].rearrange("(kc p) f -> p kc f", p=128)),
                                            (v_[:, :, 1, :], ab_w_in[i, :, 1280 + c * 128:1280 + (c + 1) * 128].rearrange("(kc p) f -> p kc f", p=128))]
                            else:
                                def loader(slot):
                                    v_ = slot[:, 0:1024].rearrange("p (kc f) -> p kc f", kc=KC)
                                    return [(v_, ab_w_in[i, :, 640:768].rearrange("(kc p) f -> p kc f", p=128))]
                            slot, wkey = wsA.next(loader)
                            wv = slot[:, :].rearrange("p (kc a f) -> p kc a f", kc=KC, a=2)
                            if kind == "v":
                                wvv = slot[:, 0:1024].rearrange("p (kc f) -> p kc f", kc=KC)
                                for (b0, nb, dkey, dfn) in dst["vblocks"]:
                                    pb = 4 + (step[0] % 2)
                                    step[0] += 1
                                    for k2 in range(KC):
                                        p.op("pe", (lambda e, k2=k2, pb=pb, b0=b0, nb=nb, wvv=wvv: e.matmul(
                                            ps[pb][0:nb, 0:128], hsrc[:, k2, b0:b0 + nb], wvv[:, k2, :], start=(k2 == 0), stop=(k2 == KC - 1))),
                                            reads=[wkey] + hkeys, writes=[("ps", pb)], inc=(k2 == KC - 1))
                                    dfn(ps[pb][0:nb, 0:128], pb)
                                continue
                            for (t0, t1) in cols:
                                n = t1 - t0
                                pb = step[0] % 2
                                step[0] += 1
                                for half in range(2):
                                    for k2 in range(KC):
                                        p.op("pe", (lambda e, k2=k2, half=half, pb=pb, n=n, t0=t0, t1=t1, wv=wv: e.matmul(
                                            ps[2 * half + pb][:, 0:n], wv[:, k2, half, :], hsrc[:, k2, t0:t1], start=(k2 == 0), stop=(k2 == KC - 1))),
                                            reads=[wkey] + hkeys, writes=[("ps", 2 * half + pb)], inc=(k2 == KC - 1))
                                if kind == "z":
                                    sg_ = T["sgz"][pb]
                                    p.op("act", (lambda e, pb=pb, n=n, sg_=sg_: e.activation(out=sg_[:, 0:n], in_=ps[2 + pb][:, 0:n], func=AF.Sigmoid)),
                                         reads=[("ps", 2 + pb)], writes=[(K_, tag, "sgz", pb)])
                                    dst["z"](c, t0, t1, ps[pb][:, 0:n], sg_[:, 0:n], [("ps", pb), (K_, tag, "sgz", pb)])
                                else:
                                    for half in range(2):
                                        src = ps[2 * half + pb]
                                        qr = T["qraw"][half]
                                        cosv, sinv = rope_cols(t0, t1)
                                        if DEBUG.get("rope_skip") != 1:
                                            p.op("dve", (lambda e, src=src, qr=qr, n=n: e.tensor_copy(out=qr[:, 0:n], in_=src[:, 0:n])),
                                                 reads=[("ps", 2 * half + pb)], writes=[(K_, tag, "qraw", half)])
                                        p.op("dve", (lambda e, src=src, n=n, half=half, cosv=cosv: e.tensor_tensor(
                                            out=T["rt1"][half][:, 0:n], in0=src[:, 0:n], in1=cosv, op=ALU.mult)),
                                            reads=[("ps", 2 * half + pb), (K_, "rope")], writes=[(K_, tag, "rt1", half)])
                                        rb = 6 + half
                                        if DEBUG.get("rope_skip") == 1:
                                            rb = 2 * half + pb
                                        else:
                                            p.op("pe", (lambda e, qr=qr, n=n, rb=rb: e.matmul(ps[rb][:, 0:n], rmat[:, :], qr[:, 0:n], start=True, stop=True)),
                                                 reads=[(K_, tag, "qraw", half), (K_, "rmat")], writes=[("ps", rb)])
                                        p.op("dve", (lambda e, n=n, half=half, rb=rb, sinv=sinv: e.tensor_tensor(
                                            out=T["rt2"][half][:, 0:n], in0=ps[rb][:, 0:n], in1=sinv, op=ALU.mult)),
                                            reads=[("ps", rb), (K_, "rope")], writes=[(K_, tag, "rt2", half)])
                                        dst[kind](c + half, t0, t1, T["rt1"][half][:, 0:n], T["rt2"][half][:, 0:n],
                                                  [(K_, tag, "rt1", half), (K_, tag, "rt2", half)])

                    def tmp_bufs(stack):
                        T = {}
                        T["qraw"] = [sbp(stack, nm("qraw"), (128, 352), BF16) for _ in range(2)]
                        T["rt1"] = [sbp(stack, nm("rt1"), (128, 352), F32) for _ in range(2)]
                        T["rt2"] = [sbp(stack, nm("rt2"), (128, 352), F32) for _ in range(2)]
                        T["sgz"] = [sbp(stack, nm("sgz"), (128, 352), F32) for _ in range(2)]
                        return T

                    with ExitStack() as sa:
                        Bm = {"id": K_ + "m"}
                        Bm["sq"] = [sbp(sa, nm("sq"), (128, STW), BF16) for _ in range(2)]
                        Bm["rstd"] = sbp(sa, nm("rstd"), (128, STW), F32)
                        Bm["rtmp"] = sbp(sa, nm("rtmp"), (128, STW), F32)
                        Bm["hT"] = hT
                        T = tmp_bufs(sa)
                        xpk = sbp(sa, nm("xpk"), (128, 512), BF16)
                        p.op("dve", lambda e: e.memset(xpk[:, 504:512], 0.0), writes=[(K_, "xpk")])
                        kbf = sbp(sa, nm("kbf"), (128, 2, 128), F32)
                        vbf = sbp(sa, nm("vbf"), (128, 128), F32)
                        zbf = sbp(sa, nm("zbf"), (128, 4, 128), F32)
                        load_rope(1)
                        rmsnorm(1, nidx, Bm)
                        if DEBUG.get("ab_stop") == 102:
                            p.barrier()
                            return
                        bc0 = 896
                        hk = [(Bm["id"], "hT", k2) for k2 in range(KC)]

                        def d_k(c, t0, t1, a, b, rk):
                            p.op("dve", lambda e: e.tensor_tensor(out=kbf[:, c, :], in0=a, in1=b, op=ALU.add), reads=rk, writes=[(K_, "kbf", c)])
                            if DEBUG.get("dk_skip") != 1:
                                p.op("dve", lambda e: e.tensor_copy(out=xpk[:, c * 128:(c + 1) * 128], in_=kbf[:, c, :]),
                                     reads=[(K_, "kbf", c)], writes=[(K_, "xpk")])

                        def d_z(c, t0, t1, zv_ps, sg_, rk):
                            p.op("dve", lambda e: e.tensor_tensor(out=zbf[:, c, :], in0=zv_ps, in1=sg_, op=ALU.mult), reads=rk, writes=[(K_, "zbf", c)])
                            p.op("act", lambda e: e.activation(out=xpk[:, 384 + 30 * c:384 + 30 * (c + 1)], in_=zbf[:, c, 98:128], func=AF.Copy),
                                 reads=[(K_, "zbf", c)], writes=[(K_, "xpk")])

                        def d_v(src, pb):
                            p.op("act", lambda e: e.activation(out=vbf[:, :], in_=src, func=AF.Copy), reads=[("ps", pb)], writes=[(K_, "vbf")])
                            p.op("dve", lambda e: e.tensor_copy(out=xpk[:, 256:384], in_=vbf[:, :]), reads=[(K_, "vbf")], writes=[(K_, "xpk")])

                        project([(bc0, bc0 + 128)], hT, hk,
                                (lambda t0, t1: (ropeT[:, 0, t0:t1], ropeT[:, 1, t0:t1])),
                                {"tmp": T, "k": d_k, "z": d_z, "vblocks": [(bc0, 128, None, d_v)]}, "mini", with_q=False)
                        if DEBUG.get("ab_stop") == 103:
                            p.barrier()
                            return
                        p.dma("sp", lambda e: e.dma_start(out=pwk_out[i, :, :, :], in_=kbf[0:64, :, :]), reads=[(K_, "kbf", 0), (K_, "kbf", 1)],
                              writes=[("pwk", i)], dsem=p.dsem("pwk_s"))
                        p.dma("sp", lambda e: e.dma_start(out=pwv_out[i, :, :], in_=vbf[:, :]), reads=[(K_, "vbf")], writes=[("pwv", i)], dsem=p.dsem("pwv_s"))
                        p.dma("sp", lambda e: e.dma_start(out=pconv_out[i, :, :, :], in_=zbf[:, :, 98:128]), reads=[(K_, "zbf", c) for c in range(4)],
                              writes=[("pconv", i)], dsem=p.dsem("pconv_s"))
                        if DEBUG.get("ab_stop") == 104:
                            p.barrier()
                            return
                        p.dma("sp", lambda e: e.dma_start(out=ccab_in[:, :], in_=xpk[:, :].bitcast(F32)), reads=[(K_, "xpk")], writes=["ccab_in"], dsem=p.dsem("ccab_w"))
                        if DEBUG.get("no_cc"):
                            p.dma("sp", lambda e: e.dma_start(out=ccab_out[0:128, :], in_=ccab_in[:, :]), reads=["ccab_in"], writes=["ccab_out"], dsem=p.dsem("ccfake2"))
                        else:
                            p.dma("pool", (lambda e: e.collective_compute("AllGather", ALU.bypass, replica_groups=RG,
                                                                          ins=[ccab_in.ap().opt()], outs=[ccab_out.ap().opt()])),
                                  reads=["ccab_in"], writes=["ccab_out"], dsem=p.dsem("ccsem2", 1))
                        p.dma("sp", lambda e: e.dma_start(out=rpk[:, :].bitcast(F32), in_=ccab_out[0:128, :]), reads=["ccab_out"], writes=[(K_, "rpk")], dsem=p.dsem("ccab_r"))
                        p.barrier()
                        if DEBUG.get("ab_stop") == 1:
                            return

                    def do_st(st):
                        c0 = st * STW
                        with ExitStack() as sa:
                            Bn = {"id": K_ + f"n{st}"}
                            Bn["sq"] = [sbp(sa, nm("sq"), (128, STW), BF16) for _ in range(2)]
                            Bn["rstd"] = sbp(sa, nm("rstd"), (128, STW), F32)
                            Bn["rtmp"] = sbp(sa, nm("rtmp"), (128, STW), F32)
                            Bn["hT"] = hT
                            T = tmp_bufs(sa)
                            load_rope(st)
                            rmsnorm(st, nidx, Bn)
                            hk = [(Bn["id"], "hT", k2) for k2 in range(KC)]
                            if st == 0:
                                p.op("dve", lambda e: e.tensor_copy(out=kT[:, :, 0:128], in_=rpk[:, 0:256].rearrange("p (a b) -> p a b", a=2)),
                                     reads=[(K_, "rpk")], writes=[(K_, "kT", "ctx")])
                                p.op("dve", lambda e: e.tensor_copy(out=vtok[:, 0, :], in_=rpk[:, 256:384]), reads=[(K_, "rpk")], writes=[(K_, "vtok", 0)])
                            else:
                                p.op("dve", lambda e: e.tensor_copy(out=kT[:, :, 0:128], in_=kT[:, :, 128 + 896:128 + 1024]),
                                     reads=[(K_, "kT", 7)], writes=[(K_, "kT", "ctx")])
                                p.op("dve", lambda e: e.tensor_copy(out=vtok[:, 0, :], in_=vtok[:, 8, :]), reads=[(K_, "vtok", 8)], writes=[(K_, "vtok", 0)])
                                p.op("dve", lambda e: e.tensor_copy(out=zT[:, :, 0:30], in_=zT[:, :, 1024:1054]), reads=[(K_, "zT", c, 2) for c in range(4)],
                                     writes=[(K_, "zT", "ctx")])

                            def d_q(c, t0, t1, a, b, rk):
                                p.op("dve", lambda e: e.tensor_tensor(out=qT[:, c, t0:t1], in0=a, in1=b, op=ALU.add), reads=rk, writes=[(K_, "qT", c, t0)])

                            def d_k(c, t0, t1, a, b, rk):
                                p.op("dve", lambda e: e.tensor_tensor(out=kT[:, c, 128 + t0:128 + t1], in0=a, in1=b, op=ALU.add), reads=rk,
                                     writes=[(K_, "kT", t0)] + [(K_, "kT", bb) for bb in range(t0 // 128, (t1 + 127) // 128)])
                                if st == 1 and t1 == STW:
                                    nn = t1 - t0
                                    p.op("dve", lambda e: e.tensor_tensor(out=ksf[:, c, :], in0=a[:, nn - 16:nn], in1=b[:, nn - 16:nn], op=ALU.add), reads=rk,
                                         writes=[(K_, "ksf", c)])

                            def d_z(c, t0, t1, zv_ps, sg_, rk):
                                ti = [t[0] for t in TILES].index(t0)
                                p.op("dve", lambda e: e.tensor_tensor(out=zT[:, c, 30 + t0:30 + t1], in0=zv_ps, in1=sg_, op=ALU.mult), reads=rk,
                                     writes=[(K_, "zT", c, ti)])
                                if st == 0 and t1 == STW:
                                    p.op("act", lambda e: e.activation(out=zpre[:, c, 30:46], in_=zT[:, c, 30 + 1024:30 + 1040], func=AF.Copy),
                                         reads=[(K_, "zT", c, ti)], writes=[(K_, "zpre")])

                            vblocks = []
                            for bi_ in range(8):
                                def dfn(src, pb, bi_=bi_):
                                    p.op("act", lambda e: e.activation(out=vtok[:, 1 + bi_, :], in_=src, func=AF.Copy), reads=[("ps", pb)], writes=[(K_, "vtok", 1 + bi_)])
                                vblocks.append((bi_ * 128, 128, None, dfn))

                            def dfn_x(src, pb):
                                if st == 0:
                                    p.op("act", lambda e: e.activation(out=vtok[0:16, 9, :], in_=src, func=AF.Copy), reads=[("ps", pb)], writes=[(K_, "vtok", 9)])
                                else:
                                    p.op("act", lambda e: e.activation(out=vsf[:, :], in_=src, func=AF.Copy), reads=[("ps", pb)], writes=[(K_, "vsf")])
                                    p.op("dve", lambda e: e.tensor_copy(out=vsb[:, :], in_=vsf[:, :]), reads=[(K_, "vsf")], writes=[(K_, "vsb")])
                            vblocks.append((1024, 16, None, dfn_x))
                            project(TILES, hT, hk, (lambda t0, t1: (ropeT[:, 0, t0:t1], ropeT[:, 1, t0:t1])),
                                    {"tmp": T, "q": d_q, "k": d_k, "z": d_z, "vblocks": vblocks}, f"st{st}")
                            if st == 0:
                                p.op("dve", lambda e: e.tensor_scalar(out=zT[:, :, 0:30], in0=rpk[:, 384:504].rearrange("p (c j) -> p c j", c=4),
                                                                      scalar1=flag[:, 0:1], scalar2=None, op0=ALU.mult),
                                     reads=[(K_, "rpk"), "flag"], writes=[(K_, "zT", "ctx")])
                                p.op("dve", lambda e: e.scalar_tensor_tensor(out=zT[:, :, 14:30], in0=zpre[:, :, 30:46], scalar=flag[:, 1:2], in1=zT[:, :, 14:30],
                                                                             op0=ALU.mult, op1=ALU.add),
                                     reads=[(K_, "zpre"), (K_, "zT", "ctx"), "flag"], writes=[(K_, "zT", "ctx")])
                            p.barrier()
                            if DEBUG.get("ab_stop") == 2 + 10 * st:
                                return True

                        with ExitStack() as sb_:
                            ysb = sbp(sb_, nm("ysb"), (128, 4, 352), F32)
                            ysq = sbp(sb_, nm("ysq"), (128, 352), F32)
                            musb = sbp(sb_, nm("musb"), (128, 352), F32)
                            varb = sbp(sb_, nm("varb"), (128, 352), F32)
                            rsb = sbp(sb_, nm("rsb"), (128, 352), F32)
                            lt = [sbp(sb_, nm("lt"), (128, 352), F32) for _ in range(2)]
                            dg = [sbp(sb_, nm("dg"), (128, 31, 128), BF16) for _ in range(2)]
                            pT = [sbp(sb_, nm("pT"), (128, 6, 128), BF16) for _ in range(2)]
                            dn = [sbp(sb_, nm("dn"), (128, 128), F32) for _ in range(2)]
                            if st == 1:
                                zs = sbp(sb_, nm("zs"), (128, 4, 16, 31), F32)
                                zsm = sbp(sb_, nm("zsm"), (128, 16, 31), F32)
                                p.dma("sp", lambda e: e.dma_start(out=zs[:, :, :, 0:30], in_=sconv_in[i, :, :, :, :]), reads=(), writes=[(K_, "zs")], dsem=p.dsem("zs_l"))
                                for c in range(4):
                                    p.op("act", (lambda e, c=c: e.activation(out=zs[:, c, :, 30], in_=zT[:, c, 30 + 1024:30 + 1040], func=AF.Copy)),
                                         reads=[(K_, "zs"), (K_, "zT", c, 2)], writes=[(K_, "zs")])
                                p.dma("sp", lambda e: e.dma_start(out=sconv_out[i, :, :, :, :], in_=zs[:, :, :, 1:31]), reads=[(K_, "zs")], writes=[("sconv", i)], dsem=p.dsem("zs_s"))
                            for ti, (t0, t1) in enumerate(TILES):
                                nmain = min(t1, NMAIN) - t0
                                n = t1 - t0
                                for c in range(4):
                                    if ti == 0 or True:
                                        d_ = dg[c % 2]
                                        for j in range(31):
                                            eng = "act" if j % 2 else "dve"
                                            if eng == "dve":
                                                p.op("dve", (lambda e, d_=d_, c=c, j=j: e.tensor_scalar(out=d_[:, j, :], in0=ident_bf[:, :], scalar1=convw[:, c, j:j + 1],
                                                                                                   scalar2=None, op0=ALU.mult)),
                                                     reads=["ident", (K_, "abc")], writes=[(K_, "dg", c % 2, j)])
                                            else:
                                                p.op("act", (lambda e, d_=d_, c=c, j=j: e.activation(out=d_[:, j, :], in_=ident_bf[:, :], func=AF.Copy, scale=convw[:, c, j:j + 1])),
                                                     reads=["ident", (K_, "abc")], writes=[(K_, "dg", c % 2, j)])
                                    for j in range(31):
                                        p.op("pe", (lambda e, c=c, j=j, t0=t0, nmain=nmain: e.matmul(
                                            ps[c][:, 0:nmain], dg[c % 2][:, j, :], zT[:, c, t0 + j:t0 + j + nmain], start=(j == 0), stop=(j == 30))),
                                            reads=[(K_, "dg", c % 2, j), (K_, "zT", "ctx")] + [(K_, "zT", c, tt_) for tt_ in range(3)],
                                            writes=[("ps", c)], inc=(j == 30))
                                    if ti == 2 and st == 0:
                                        for j in range(31):
                                            p.op("pe", (lambda e, c=c, j=j, nmain=nmain: e.matmul(
                                                ps[c][:, nmain:nmain + 16], dg[c % 2][:, j, :], zpre[:, c, j:j + 16], start=(j == 0), stop=(j == 30))),
                                                reads=[(K_, "dg", c % 2, j), (K_, "zpre")], writes=[("ps", c)], inc=(j == 30))
                                    ncv = n if (ti < 2 or st == 0) else nmain
                                    p.op("act", (lambda e, c=c, ncv=ncv: e.activation(out=ysb[:, c, 0:ncv], in_=ps[c][:, 0:ncv], func=AF.Identity, bias=convb[:, c:c + 1])),
                                         reads=[("ps", c), (K_, "abc")], writes=[(K_, "ysb", c)])
                                    if ti == 2 and st == 1:
                                        p.op("dve", (lambda e, c=c: e.tensor_tensor(out=zsm[:, :, :], in0=zs[:, c, :, :],
                                                                                   in1=bcast(convw[:, c, :].unsqueeze(1), (128, 16, 31)), op=ALU.mult)),
                                             reads=[(K_, "zs"), (K_, "abc")], writes=[(K_, "zsm")])
                                        p.op("dve", (lambda e, c=c, nmain=nmain: e.tensor_reduce(out=ysb[:, c, nmain:nmain + 16], in_=zsm[:, :, :],
                                                                                               axis=mybir.AxisListType.X, op=ALU.add)),
                                             reads=[(K_, "zsm"), (K_, "ysb", c)], writes=[(K_, "ysb", c)])
                                        p.op("dve", (lambda e, c=c, nmain=nmain: e.tensor_scalar(out=ysb[:, c, nmain:nmain + 16], in0=ysb[:, c, nmain:nmain + 16],
                                                                                               scalar1=convb[:, c:c + 1], scalar2=None, op0=ALU.add)),
                                             reads=[(K_, "ysb", c), (K_, "abc")], writes=[(K_, "ysb", c)])
                                for c in range(4):
                                    p.op("pe", (lambda e, c=c, n=n: e.matmul(ps[4][:, 0:n], onesf[:, :], ysb[:, c, 0:n], start=(c == 0), stop=(c == 3))),
                                         reads=[(K_, "ysb", c), (K_, "onesf")], writes=[("ps", 4)], inc=(c == 3))
                                for c in range(4):
                                    p.op("act", (lambda e, c=c, n=n: e.activation(out=ysq[:, 0:n], in_=ysb[:, c, 0:n], func=AF.Square)),
                                         reads=[(K_, "ysb", c)], writes=[(K_, "ysq")])
                                    p.op("pe", (lambda e, c=c, n=n: e.matmul(ps[5][:, 0:n], onesf[:, :], ysq[:, 0:n], start=(c == 0), stop=(c == 3))),
                                         reads=[(K_, "ysq"), (K_, "onesf")], writes=[("ps", 5)], inc=True)
                                p.op("act", (lambda e, n=n: e.activation(out=musb[:, 0:n], in_=ps[4][:, 0:n], func=AF.Copy)), reads=[("ps", 4)], writes=[(K_, "musb")])
                                p.op("dve", (lambda e, n=n: e.tensor_tensor(out=varb[:, 0:n], in0=musb[:, 0:n], in1=musb[:, 0:n], op=ALU.mult)),
                                     reads=[(K_, "musb")], writes=[(K_, "varb")])
                                p.op("dve", (lambda e, n=n: e.tensor_tensor(out=varb[:, 0:n], in0=ps[5][:, 0:n], in1=varb[:, 0:n], op=ALU.subtract)),
                                     reads=[("ps", 5), (K_, "varb")], writes=[(K_, "varb")])
                                p.op("act", (lambda e, n=n: e.activation(out=varb[:, 0:n], in_=varb[:, 0:n], func=AF.Sqrt, bias=EPS)),
                                     reads=[(K_, "varb")], writes=[(K_, "varb")])
                                p.op("dve", (lambda e, n=n: e.reciprocal(out=rsb[:, 0:n], in_=varb[:, 0:n])), reads=[(K_, "varb")], writes=[(K_, "rsb")])
                                for c in range(4):
                                    l_ = lt[c % 2]
                                    p.op("dve", (lambda e, c=c, n=n, l_=l_: e.tensor_tensor(out=l_[:, 0:n], in0=ysb[:, c, 0:n], in1=musb[:, 0:n], op=ALU.subtract)),
                                         reads=[(K_, "ysb", c), (K_, "musb")], writes=[(K_, "lt", c % 2)])
                                    p.op("dve", (lambda e, c=c, n=n, l_=l_: e.tensor_tensor(out=l_[:, 0:n], in0=l_[:, 0:n], in1=rsb[:, 0:n], op=ALU.mult)),
                                         reads=[(K_, "lt", c % 2), (K_, "rsb")], writes=[(K_, "lt", c % 2)])
                                    p.op("dve", (lambda e, c=c, n=n, l_=l_: e.tensor_scalar(out=l_[:, 0:n], in0=l_[:, 0:n], scalar1=lng[:, c:c + 1], scalar2=lnb[:, c:c + 1],
                                                                                         op0=ALU.mult, op1=ALU.add)),
                                         reads=[(K_, "lt", c % 2), (K_, "abc")], writes=[(K_, "lt", c % 2)])
                                    p.op("act", (lambda e, c=c, n=n, l_=l_, t0=t0, t1=t1: e.activation(out=coT[:, c, t0:t1], in_=l_[:, 0:n], func=AF.Silu)),
                                         reads=[(K_, "lt", c % 2)], writes=[(K_, "coT", c, ti)])

                            if DEBUG.get("ab_stop") == 3 + 10 * st:
                                p.barrier()
                                return True
                            def attend(qc0, nq, kblocks, step):
                                for c in range(4):
                                    par = (step * 4 + c) % 2
                                    sbank = ps[2 * par][:, :]
                                    kvh = c // 2
                                    slots = []
                                    sl = 0
                                    for hh in range(2):
                                        hb = hh * 64
                                        for (ko, nk, vi, mi) in kblocks:
                                            bank = 2 * par + (sl // 4)
                                            so = (sl % 4) * 128
                                            p.op("pe", (lambda e, bank=bank, so=so, nk=nk, hb=hb, ko=ko, c=c, kvh=kvh: e.matmul(
                                                ps[bank][0:nk, so:so + nq], kT[hb:hb + 64, kvh, ko:ko + nk], qT[hb:hb + 64, c, qc0:qc0 + nq], start=True, stop=False)),
                                                reads=[(K_, "kT", "ctx")] + [(K_, "kT", bb) for bb in range(9)] + [(K_, "qT", c, tt_[0]) for tt_ in TILES],
                                                writes=[("ps", bank)], inc=False)
                                            p.op("pe", (lambda e, bank=bank, so=so, nk=nk, mi=mi: e.matmul(
                                                ps[bank][0:nk, so:so + nq], ident_bf[0:nk, 0:nk], amask[0:nk, mi, 0:nq], start=False, stop=True)),
                                                reads=["ident", (K_, "amask")], writes=[("ps", bank)], inc=True)
                                            p.op("act", (lambda e, bank=bank, so=so, nk=nk, sl=sl, par=par: e.activation(
                                                out=pT[par][0:nk, sl, 0:nq], in_=ps[bank][0:nk, so:so + nq], func=AF.Exp, scale=NEGSC)),
                                                reads=[("ps", bank)], writes=[(K_, "pT", par, sl)])
                                            slots.append((sl, hb, nk, vi))
                                            sl += 1
                                    ob = 4 + par
                                    for hh in range(2):
                                        hb = hh * 64
                                        mine = [s_ for s_ in slots if s_[1] == hb]
                                        for idx, (sl_, _, nk, vi) in enumerate(mine):
                                            p.op("pe", (lambda e, ob=ob, hb=hb, nk=nk, vi=vi, sl_=sl_, kvh=kvh, idx=idx, nm_=len(mine), par=par: e.matmul(
                                                ps[ob][hb:hb + 64, 0:nq], vtok[0:nk, vi, kvh * 64:kvh * 64 + 64], pT[par][0:nk, sl_, 0:nq],
                                                start=(idx == 0), stop=(idx == nm_ - 1))),
                                                reads=[(K_, "pT", par, sl_), (K_, "vtok", vi)], writes=[("ps", ob)], inc=False)
                                        for idx, (sl_, _, nk, vi) in enumerate(mine):
                                            p.op("pe", (lambda e, ob=ob, hb=hb, nk=nk, sl_=sl_, idx=idx, nm_=len(mine), par=par: e.matmul(
                                                ps[ob][hb:hb + 64, 128:128 + nq], ones_bf[0:nk, 0:64], pT[par][0:nk, sl_, 0:nq],
                                                start=(idx == 0), stop=(idx == nm_ - 1))),
                                                reads=[(K_, "pT", par, sl_), "ones"], writes=[("ps", ob)], inc=(hh == 1 and idx == len(mine) - 1))
                                    d_ = dn[par]
                                    p.op("dve", (lambda e, ob=ob, d_=d_, c=c: e.tensor_scalar(out=d_[:, 0:nq], in0=ps[ob][:, 128:128 + nq], scalar1=esink[:, c:c + 1],
                                                                                         scalar2=None, op0=ALU.add)),
                                         reads=[("ps", ob), (K_, "esink")], writes=[(K_, "dn", par)])
                                    p.op("dve", (lambda e, d_=d_: e.reciprocal(out=d_[:, 0:nq], in_=d_[:, 0:nq])), reads=[(K_, "dn", par)], writes=[(K_, "dn", par)])
                                    p.op("dve", (lambda e, ob=ob, d_=d_, c=c: e.tensor_tensor(out=aoT[:, c, qc0:qc0 + nq], in0=ps[ob][:, 0:nq], in1=d_[:, 0:nq], op=ALU.mult)),
                                         reads=[("ps", ob), (K_, "dn", par)], writes=[(K_, "aoT", c, qc0)])

                            stp = 0
                            for bi_ in range(8):
                                kbl = []
                                if bi_ == 0 and st == 0:
                                    kbl.append((0, 128, 0, 2))
                                    kbl.append((128 + 1024, 16, 9, 3))
                                else:
                                    kbl.append((128 + (bi_ - 1) * 128, 128, bi_, 1))
                                kbl.append((128 + bi_ * 128, 128, 1 + bi_, 0))
                                attend(bi_ * 128, 128, kbl, stp)
                                stp += 1
                            if st == 0:
                                attend(1024, 16, [(128 + 1024, 16, 9, 0)], stp)
                                stp += 1
                            p.barrier()
                            if DEBUG.get("ab_stop") == 4 + 10 * st:
                                return True

                        if st == 1:
                            with ExitStack() as sc_:
                                KTs = sbp(sc_, nm("KTs"), (128, 2, 16, 128), BF16)
                                Vs = sbp(sc_, nm("Vs"), (128, 16, 128), BF16)
                                PTs = sbp(sc_, nm("PTs"), (128, 128), BF16)
                                prod = sbp(sc_, nm("prod"), (128, 4, 16), BF16)
                                pnew = sbp(sc_, nm("pnew"), (128, 4, 16), F32)
                                vdT = sbp(sc_, nm("vdT"), (128, 2, 16), F32)
                                o1 = sbp(sc_, nm("o1"), (128, 4, 16), F32)
                                d1 = sbp(sc_, nm("d1"), (128, 4, 16), F32)
                                kd = sbp(sc_, nm("kd"), (128, 2, 16), BF16)
                                p.op("dve", lambda e: e.memset(fence_t[:, 4:8], 0.0), writes=["fenceC"])
                                for kv_ in range(2):
                                    p.dma("pool", (lambda e, kv_=kv_: e.dma_start(out=KTs[:, kv_, :, :], in_=ckt_in[i, :, kv_, :, :])),
                                          reads=["fenceC"], writes=[(K_, "KTs")], dsem=p.dsem("kts_l"))
                                p.dma("pool", lambda e: e.dma_start(out=Vs[:, :, :], in_=cv_in[i, :, :, :]), reads=["fenceC"], writes=[(K_, "Vs")], dsem=p.dsem("vs_l"))
                                p.dma("sp", lambda e: e.dma_start(out=swk_out[i, :, 0:127, :], in_=cknat_in[i, :, 1:128, :]), reads=(), writes=[("swk", i)], dsem=p.dsem("swk_c"))
                                p.dma("sp", lambda e: e.dma_start(out=swv_out[i, :, 0:127, :], in_=cvnat_in[i, :, 1:128, :]), reads=(), writes=[("swv", i)], dsem=p.dsem("swv_c"))
                                p.dma("sp", lambda e: e.dma_start(out=swv_out[i, :, 127, :], in_=vsf[:, :]), reads=[(K_, "vsf")], writes=[("swv2", i)], dsem=p.dsem("swv_n"))
                                for kv_ in range(2):
                                    p.dma("sp", (lambda e, kv_=kv_: e.dma_start(out=swk_out[i, :, 127, kv_ * 64:(kv_ + 1) * 64].rearrange("b d -> d b"), in_=ksf[0:64, kv_, :],
                                                                              allow_slow_non_contiguous=True)),
                                          reads=[(K_, "ksf", 0), (K_, "ksf", 1)], writes=[("swk2", i, kv_)], dsem=p.dsem("swk_n"))
                                sq0 = 1024
                                for b in range(16):
                                    for h in range(8):
                                        hb = (h % 2) * 64
                                        p.op("pe", (lambda e, b=b, h=h, hb=hb: e.matmul(
                                            ps[0][:, b * 8 + h:b * 8 + h + 1], KTs[hb:hb + 64, h // 4, b, :], qT[hb:hb + 64, h // 2, sq0 + b:sq0 + b + 1],
                                            start=True, stop=True)),
                                            reads=[(K_, "KTs")] + [(K_, "qT", c, TILES[2][0]) for c in range(4)], writes=[("ps", 0)], inc=(b == 15 and h == 7))
                                p.op("act", lambda e: e.activation(out=PTs[:, :], in_=ps[0][:, 0:128], func=AF.Exp, scale=NEGSC), reads=[("ps", 0)], writes=[(K_, "PTs")])
                                osps = ps[4][:, 0:64].rearrange("p (c b) -> p c b", c=4)
                                dsps = ps[4][:, 64:128].rearrange("p (c b) -> p c b", c=4)
                                for b in range(16):
                                    for h in range(8):
                                        hb = (h % 2) * 64
                                        p.op("pe", (lambda e, b=b, h=h, hb=hb: e.matmul(
                                            osps[hb:hb + 64, h // 2, b:b + 1], Vs[:, b, (h // 4) * 64:(h // 4) * 64 + 64], PTs[:, b * 8 + h:b * 8 + h + 1],
                                            start=True, stop=True)), reads=[(K_, "PTs"), (K_, "Vs")], writes=[("ps", 4)], inc=False)
                                        p.op("pe", (lambda e, b=b, h=h, hb=hb: e.matmul(
                                            dsps[hb:hb + 64, h // 2, b:b + 1], ones_bf[:, 0:64], PTs[:, b * 8 + h:b * 8 + h + 1],
                                            start=True, stop=True)), reads=[(K_, "PTs"), "ones"], writes=[("ps", 4)], inc=(b == 15 and h == 7))
                                p.op("act", lambda e: e.activation(out=kd[:, :, :], in_=ksf[:, :, :], func=AF.Copy), reads=[(K_, "ksf", 0), (K_, "ksf", 1)], writes=[(K_, "kd")])
                                for c in range(4):
                                    p.op("dve", (lambda e, c=c: e.tensor_tensor(out=prod[:, c, :], in0=qT[:, c, sq0:sq0 + 16], in1=kd[:, c // 2, :], op=ALU.mult)),
                                         reads=[(K_, "kd"), (K_, "qT", c, TILES[2][0])], writes=[(K_, "prod", c)])
                                    p.op("pe", (lambda e, c=c: e.matmul(ps[5][:, c * 16:(c + 1) * 16], b2m[:, :], prod[:, c, :], start=True, stop=True)),
                                         reads=[(K_, "prod", c), (K_, "b2m")], writes=[("ps", 5)], inc=(c == 3))
                                p.op("act", lambda e: e.activation(out=pnew[:, :, :], in_=ps[5][:, 0:64].rearrange("p (c b) -> p c b", c=4), func=AF.Exp, scale=NEGSC),
                                     reads=[("ps", 5)], writes=[(K_, "pnew")])
                                for kv_ in range(2):
                                    for dup in range(2):
                                        p.op("pe", (lambda e, kv_=kv_, dup=dup: e.matmul(
                                            ps[6][dup * 64:dup * 64 + 64, kv_ * 16:(kv_ + 1) * 16], vsb[0:16, kv_ * 64:kv_ * 64 + 64], ident_bf[0:16, 0:16],
                                            start=True, stop=True)), reads=[(K_, "vsb"), "ident"], writes=[("ps", 6)], inc=(kv_ == 1 and dup == 1))
                                p.op("act", lambda e: e.activation(out=vdT[:, :, :], in_=ps[6][:, 0:32].rearrange("p (k b) -> p k b", k=2), func=AF.Copy),
                                     reads=[("ps", 6)], writes=[(K_, "vdT")])
                                for c in range(4):
                                    p.op("dve", (lambda e, c=c: e.tensor_tensor(out=o1[:, c, :], in0=pnew[:, c, :], in1=vdT[:, c // 2, :], op=ALU.mult)),
                                         reads=[(K_, "pnew"), (K_, "vdT")], writes=[(K_, "o1", c)])
                                    p.op("dve", (lambda e, c=c: e.tensor_tensor(out=o1[:, c, :], in0=o1[:, c, :], in1=osps[:, c, :], op=ALU.add)),
                                         reads=[(K_, "o1", c), ("ps", 4)], writes=[(K_, "o1", c)])
                                    p.op("dve", (lambda e, c=c: e.tensor_tensor(out=d1[:, c, :], in0=pnew[:, c, :], in1=dsps[:, c, :], op=ALU.add)),
                                         reads=[(K_, "pnew"), ("ps", 4)], writes=[(K_, "d1", c)])
                                    p.op("dve", (lambda e, c=c: e.tensor_scalar(out=d1[:, c, :], in0=d1[:, c, :], scalar1=esink[:, c:c + 1], scalar2=None, op0=ALU.add)),
                                         reads=[(K_, "d1", c), (K_, "esink")], writes=[(K_, "d1", c)])
                                    p.op("dve", (lambda e, c=c: e.reciprocal(out=d1[:, c, :], in_=d1[:, c, :])), reads=[(K_, "d1", c)], writes=[(K_, "d1", c)])
                                    p.op("dve", (lambda e, c=c: e.tensor_tensor(out=aoT[:, c, sq0:sq0 + 16], in0=o1[:, c, :], in1=d1[:, c, :], op=ALU.mult)),
                                         reads=[(K_, "o1", c), (K_, "d1", c)], writes=[(K_, "aoT", c, sq0)])
                                p.barrier()

                        step = 0
                        for dc in range(KC):
                            def loadero(slot, dc=dc):
                                v_ = slot[:, 0:1024].rearrange("p (kc f) -> p kc f", kc=KC)
                                return [(v_, ab_w_out[i, :, dc * 128:(dc + 1) * 128].rearrange("(kc p) f -> p kc f", p=128))]
                            slot, wkey = wsA.next(loadero)
                            wv = slot[:, 0:1024].rearrange("p (kc f) -> p kc f", kc=KC)
                            for ti, (t0, t1) in enumerate(TILES):
                                n = t1 - t0
                                pb = 6 + (step % 2)
                                step += 1
                                for k2 in range(KC):
                                    rhs = aoT[:, k2, t0:t1] if k2 < 4 else coT[:, k2 - 4, t0:t1]
                                    p.op("pe", (lambda e, k2=k2, pb=pb, n=n, rhs=rhs, wv=wv: e.matmul(
                                        ps[pb][:, 0:n], wv[:, k2, :], rhs, start=(k2 == 0), stop=(k2 == KC - 1))),
                                        reads=[wkey] + [(K_, "aoT", c, q_) for c in range(4) for q_ in list(range(0, 1024, 128)) + [1024]]
                                        + [(K_, "coT", c, ti) for c in range(4)], writes=[("ps", pb)], inc=(k2 == KC - 1))
                                p.op("dve", (lambda e, pb=pb, n=n, dc=dc, t0=t0, t1=t1, c0=c0: e.tensor_tensor(
                                    out=xT[:, dc, c0 + t0:c0 + t1], in0=xT[:, dc, c0 + t0:c0 + t1], in1=ps[pb][:, 0:n], op=ALU.add)),
                                    reads=[("ps", pb), ("xT", dc, st)], writes=[("xT", dc, st)])
                        p.barrier()
                        return False

                    for st_ in (0, 1):
                        if do_st(st_):
                            return

            mode = DEBUG.get("mode")
            if mode == "ab":
                ab_phase(0)
                final_phase()
                p.final_wait("sp")
                return
            if mode == "ssm":
                ssm_precompute(0)
                ssm_phase(1)
                final_phase()
                p.final_wait("sp")
                return
            for i in range(2):
                ssm_precompute(i)
            sub = 0
            stop = DEBUG["stop_after"]
            done = False
            for l in range(DEPTH):
                for which in (0, 1, 2):
                    if which == 1:
                        if l % 2 == 1:
                            ssm_phase(l)
                        else:
                            ab_phase(l)
                    else:
                        ffn_phase(l, 0 if which == 0 else 1, 3 * l + which)
                    if stop is not None and sub == stop:
                        done = True
                        break
                    sub += 1
                if done:
                    break
            final_phase()
            p.final_wait("sp")

        pd = Prog(dry=True)
        wsA = WeightStream(pd, "wA", wA, 3)
        wsB = WeightStream(pd, "wB", [None, None], 1)
        emit(pd, wsA, wsB)
        p = Prog(dry=False)
        wsA.p = p
        wsB.p = p
        wsA.reset_for_real()
        wsB.reset_for_real()
        emit(p, wsA, wsB)
        p.replay(nc, es)
    return nc


_CACHE = {}


def _host_inputs(inputs):
    f = lambda k: np.asarray(inputs[k], np.float32)
    x_prompt, x_sample, meta = f("x_prompt"), f("x_sample"), f("meta_tokens")
    ng = np.concatenate([f("norm_g").reshape(12, D), f("final_norm_g")[None]], 0)
    gains = np.ascontiguousarray(ng.reshape(13, KC, 128).transpose(2, 0, 1).reshape(128, 13 * KC))
    ident = np.eye(128, dtype=np.float32)
    r = np.arange(128)
    mc = (r[None, :] % 8 >= r[:, None] % 8).astype(np.float32)
    mask3 = np.ascontiguousarray(np.concatenate([mc, np.ones((128, 128), np.float32), mc], 1))

    def gp(a):
        sh = a.shape
        a = a.reshape(2, 32, 64, *sh[2:])
        a = np.moveaxis(a, 2, 1)
        return a.reshape(128, 32, *sh[2:])

    a_re, a_im, ldt = f("ssm_a_re"), f("ssm_a_im"), f("ssm_log_dt")
    small = np.stack([np.concatenate([gp(a_re[i]), gp(a_im[i]), gp(np.broadcast_to(ldt[i][:, None], (64, 64)))], 1) for i in range(2)])
    b_re, b_im, c_re, c_im = f("ssm_b_re"), f("ssm_b_im"), f("ssm_c_re"), f("ssm_c_im")
    ssm_b = np.stack([np.stack([gp(b_re[i]), gp(b_im[i])], 1).reshape(128, 1024) for i in range(2)])
    ssm_c = np.stack([np.stack([gp(c_re[i].transpose(0, 2, 1)), gp(c_im[i].transpose(0, 2, 1))], 1).reshape(128, 1024) for i in range(2)])
    sd = f("ssm_d")
    dsh = np.stack([np.ascontiguousarray(np.repeat(sd[i].reshape(64, 16).T, 8, axis=0)) for i in range(2)])
    common = {"gains": gains, "ident": ident, "mask3": mask3,
              "ssm_small": np.ascontiguousarray(small.reshape(2, 128, 96)), "ssm_b": np.ascontiguousarray(ssm_b),
              "ssm_c": np.ascontiguousarray(ssm_c), "ssm_dsh": np.ascontiguousarray(dsh)}
    for k in ("ffn1_w_gu", "ffn2_w_gu", "ffn1_w_down", "ffn2_w_down", "ssm_w_in", "ssm_w_glu", "ab_w_in", "ab_w_out"):
        common[k] = f(k)
    cw, cb, lg, lb, sk = f("conv_w"), f("conv_b"), f("conv_ln_g"), f("conv_ln_b"), f("attn_sink")
    abc = np.zeros((2, 128, 140), np.float32)
    for i in range(2):
        abc[i, :, 0:124] = cw[i].reshape(31, 4, 128).transpose(2, 1, 0).reshape(128, 124)
        abc[i, :, 124:128] = cb[i].reshape(4, 128).T
        abc[i, :, 128:132] = lg[i].reshape(4, 128).T
        abc[i, :, 132:136] = lb[i].reshape(4, 128).T
        abc[i, :, 136:140] = np.repeat(sk[i].reshape(4, 2), 64, axis=1).T
    common["abc"] = abc
    rmat = np.zeros((128, 128), np.float32)
    for hb in (0, 64):
        for dd in range(8):
            rmat[hb + dd + 8, hb + dd] = -1.0
            rmat[hb + dd, hb + dd + 8] = 1.0
    common["rmat"] = rmat
    b2 = np.zeros((128, 128), np.float32)
    b2[0:64, 0:64] = 1.0
    b2[64:128, 64:128] = 1.0
    common["b2"] = b2
    NEG = -240000.0
    sidx = np.arange(128)[:, None]
    qidx = np.arange(128)[None, :]
    m_own = np.where(sidx <= qidx, 0.0, NEG).astype(np.float32)
    m_prev = np.where(sidx >= qidx, 0.0, NEG).astype(np.float32)
    m_neg = np.full((128, 128), NEG, np.float32)
    m_pre = np.where((sidx >= qidx - 112) & (sidx < 16), 0.0, NEG).astype(np.float32)
    inv = (np.float32(500000.0) ** (-np.arange(8, dtype=np.float32) * np.float32(2.0) / np.float32(16.0))).astype(np.float32)
    ck, cv_, sc_in = f("cache_win_k"), f("cache_win_v"), f("state_conv")
    s_re, s_im = f("state_ssm_re"), f("state_ssm_im")
    in_maps = []
    for rr in range(8):
        seq, par = rr // 2, rr % 2
        cols = np.zeros((NCOL, D), np.float32)
        main = x_prompt[seq, par * 2048:(par + 1) * 2048]
        cols[0:1024] = main[0:1024]
        cols[1040:2064] = main[1024:2048]
        if par == 0:
            cols[1024:1040] = meta
        cols[2064:2080] = x_sample[16 * rr:16 * rr + 16, 0]
        m = dict(common)
        m["xT"] = np.ascontiguousarray(cols.T)
        fl = np.zeros((128, 2), np.float32)
        fl[:, 0] = float(par)
        fl[:, 1] = 1.0 - float(par)
        m["flag"] = fl
        sst = np.zeros((2, 128, 2, 32, 16), np.float32)
        for i in range(2):
            for ri, sarr in enumerate((s_re, s_im)):
                blk = sarr[i, 16 * rr:16 * rr + 16]
                sst[i, :, ri] = gp(blk.transpose(1, 2, 0))
        m["sst_in"] = np.ascontiguousarray(sst.reshape(2, 128, 1024))
        m["amask"] = np.ascontiguousarray(np.stack([m_own, m_prev, m_prev if par == 1 else m_neg, m_pre if par == 0 else m_neg], 1).reshape(128, 512))
        rope = np.zeros((2, 2, 128, STW), np.float32)
        for st in range(2):
            pos = np.zeros(STW, np.float32)
            pos[0:1024] = 16 + 2048 * par + 1024 * st + np.arange(1024)
            pos[1024:1040] = np.arange(16) if st == 0 else 8192
            ang = (pos[:, None] * inv[None, :]).astype(np.float32)
            cs, sn = np.cos(ang).astype(np.float32), np.sin(ang).astype(np.float32)
            rope[st, 0] = 1.0
            for hb in (0, 64):
                for dd in range(16):
                    rope[st, 0, hb + dd] = cs[:, dd % 8]
                    rope[st, 1, hb + dd] = sn[:, dd % 8]
        m["rope"] = rope
        ckb = ck[:, 16 * rr:16 * rr + 16]
        cvb = cv_[:, 16 * rr:16 * rr + 16]
        kt = ckb.transpose(0, 4, 3, 1, 2)
        m["ckt"] = np.ascontiguousarray(np.concatenate([kt, kt], 1))
        m["cv"] = np.ascontiguousarray(cvb.reshape(2, 16, 128, 128).transpose(0, 2, 1, 3))
        m["cknat"] = np.ascontiguousarray(ckb.reshape(2, 16, 128, 128))
        m["cvnat"] = np.ascontiguousarray(cvb.reshape(2, 16, 128, 128))
        scb = sc_in[:, 16 * rr:16 * rr + 16]
        m["sconv_in"] = np.ascontiguousarray(scb.reshape(2, 16, 30, 4, 128).transpose(0, 4, 3, 1, 2))
        in_maps.append(m)
    return in_maps


def _ungp(a):
    sh = a.shape
    a = a.reshape(2, 64, 32, *sh[2:])
    a = np.moveaxis(a, 1, 2)
    return a.reshape(64, 64, *sh[2:])


def kernel(**inputs):
    ncores = 8
    key = str(sorted(DEBUG.items()))
    if key not in _CACHE:
        _CACHE[key] = build_program()
    nc = _CACHE[key]
    in_maps = _host_inputs(inputs)
    res = run_bass_kernel_spmd(nc, in_maps, core_ids=list(range(ncores)))
    kernel.last_res = res
    y_prompt = np.zeros((4, 4096, D), np.float32)
    y_sample = np.zeros((128, 1, D), np.float32)
    p_re = np.zeros((2, 4, 64, 64), np.float32)
    p_im = np.zeros((2, 4, 64, 64), np.float32)
    s_re = np.zeros((2, 128, 64, 64), np.float32)
    s_im = np.zeros((2, 128, 64, 64), np.float32)
    for r in range(ncores):
        seq, par = r // 2, r % 2
        out = res.results[r]
        y = np.asarray(out["yT"]).T
        y_prompt[seq, par * 2048:par * 2048 + 1024] = y[0:1024]
        y_prompt[seq, par * 2048 + 1024:(par + 1) * 2048] = y[1040:2064]
        y_sample[16 * r:16 * r + 16, 0] = y[2064:2080]
        ps_ = np.asarray(out["pssm"]).reshape(2, 128, 2, 32)
        ss_ = np.asarray(out["sssm"]).reshape(2, 128, 2, 32, 16)
        for i in range(2):
            if par == 1:
                p_re[i, seq] = _ungp(ps_[i, :, 0])
                p_im[i, seq] = _ungp(ps_[i, :, 1])
            s_re[i, 16 * r:16 * r + 16] = _ungp(ss_[i, :, 0]).transpose(2, 0, 1)
            s_im[i, 16 * r:16 * r + 16] = _ungp(ss_[i, :, 1]).transpose(2, 0, 1)
    p_conv = np.zeros((2, 4, 30, 512), np.float32)
    p_k = np.zeros((2, 4, 128, 2, 64), np.float32)
    p_v = np.zeros((2, 4, 128, 2, 64), np.float32)
    s_conv = np.zeros((2, 128, 30, 512), np.float32)
    s_k = np.zeros((2, 128, 128, 2, 64), np.float32)
    s_v = np.zeros((2, 128, 128, 2, 64), np.float32)
    for r in range(ncores):
        seq, par = r // 2, r % 2
        out = res.results[r]
        if "pwk" not in out:
            break
        if par == 1:
            p_k[:, seq] = np.asarray(out["pwk"]).transpose(0, 3, 2, 1)
            p_v[:, seq] = np.asarray(out["pwv"]).reshape(2, 128, 2, 64)
            p_conv[:, seq] = np.asarray(out["pconv"]).transpose(0, 3, 2, 1).reshape(2, 30, 512)
        s_k[:, 16 * r:16 * r + 16] = np.asarray(out["swk"]).reshape(2, 16, 128, 2, 64)
        s_v[:, 16 * r:16 * r + 16] = np.asarray(out["swv"]).reshape(2, 16, 128, 2, 64)
        s_conv[:, 16 * r:16 * r + 16] = np.asarray(out["sconv"]).transpose(0, 3, 4, 2, 1).reshape(2, 16, 30, 512)
    kernel.extra = dict(p_re=p_re, p_im=p_im, s_re=s_re, s_im=s_im)
    return (y_prompt, y_sample, p_conv, p_k, p_v, p_re, p_im, s_conv, s_k, s_v, s_re, s_im)
```

```python
import numpy as np
from contextlib import ExitStack
import concourse.bass as bass
import concourse.mybir as mybir
from concourse.bass_utils import run_bass_kernel_spmd

F32 = mybir.dt.float32
BF16 = mybir.dt.bfloat16
ALU = mybir.AluOpType
AF = mybir.ActivationFunctionType

D = 1024
KC = 8
DFF = 2816
NJ = 22
DEPTH = 4
NCOL = 2080
STW = 1040
NMAIN = 1024
TILES = [(0, 352), (352, 704), (704, 1040)]
EPS = 1e-6
LCH = 16
NBK = 2
NCH = 129
NCL = (65, 64)
CG0 = (0, 65)
STILES = [(0, 6), (6, 11), (11, 16)]

DEBUG = {"stop_after": None}


class DSem:
    def __init__(self, name, step=16):
        self.name = name
        self.total = 0
        self.step = step


class Prog:
    ENG = ("pe", "act", "dve", "pool", "sp")

    def __init__(self, dry=False):
        self.dry = dry
        self.ops = {e: [] for e in self.ENG}
        self.count = {e: 0 for e in self.ENG}
        self.waited = {e: {} for e in self.ENG}
        self.res = {}
        self.dsems = {}
        self.dirty_dsems = set()

    def dsem(self, name, step=16):
        if name not in self.dsems:
            self.dsems[name] = DSem("d_" + name, step)
        return self.dsems[name]

    def _deps(self, eng, reads, writes):
        deps = {}

        def add(tok):
            if tok is None:
                return
            k, v = tok
            if deps.get(k, 0) < v:
                deps[k] = v

        for k in reads:
            st = self.res.get(k)
            if st:
                add(st[0])
        for k in writes:
            st = self.res.get(k)
            if st:
                add(st[0])
                for kk, vv in st[1].items():
                    add((kk, vv))
        waits = []
        for k, v in deps.items():
            if k == eng and eng == "pe":
                continue
            if self.waited[eng].get(k, 0) >= v:
                continue
            self.waited[eng][k] = v
            waits.append((k, v))
        return waits

    def _commit(self, tok, reads, writes):
        for k in reads:
            st = self.res.setdefault(k, [None, {}])
            if st[1].get(tok[0], 0) < tok[1]:
                st[1][tok[0]] = tok[1]
        for k in writes:
            self.res[k] = [tok, {}]

    def op(self, eng, fn, reads=(), writes=(), inc=True):
        if self.dry:
            return
        waits = self._deps(eng, reads, writes)
        if inc:
            self.count[eng] += 1
            tok = (eng, self.count[eng])
        else:
            tok = (eng, self.count[eng] + 1)
        self.ops[eng].append((waits, fn, "inc" if inc else None, None))
        self._commit(tok, reads, writes)

    def dma(self, eng, fn, reads, writes, dsem):
        if self.dry:
            return
        waits = self._deps(eng, reads, writes)
        dsem.total += dsem.step
        tok = (dsem.name, dsem.total)
        self.ops[eng].append((waits, fn, "dma", dsem))
        self.dirty_dsems.add(dsem.name)
        self._commit(tok, reads, writes)

    def barrier(self, engines=("pe", "act", "dve", "sp")):
        if self.dry:
            return
        toks = [(e, self.count[e]) for e in engines if self.count[e] > 0]
        for n in sorted(self.dirty_dsems):
            ds = [d for d in self.dsems.values() if d.name == n][0]
            toks.append((ds.name, ds.total))
        self.dirty_dsems = set()
        for e in engines:
            waits = []
            for k, v in toks:
                if k == e and e == "pe":
                    continue
                if self.waited[e].get(k, 0) >= v:
                    continue
                self.waited[e][k] = v
                waits.append((k, v))
            if waits:
                self.ops[e].append((waits, None, None, None))

    def final_wait(self, eng="sp"):
        if self.dry:
            return
        waits = []
        for d in self.dsems.values():
            if d.total > 0 and self.waited[eng].get(d.name, 0) < d.total:
                waits.append((d.name, d.total))
        for e in self.ENG:
            if e != eng and self.count[e] > 0:
                waits.append((e, self.count[e]))
        self.ops[eng].append((waits, None, None, None))

    def replay(self, nc, es):
        sems = {}
        for e in self.ENG:
            sems[e] = es.enter_context(nc.semaphore("s_" + e))
        for d in self.dsems.values():
            sems[d.name] = es.enter_context(nc.semaphore(d.name))
        block = es.enter_context(nc.Block())
        reg = {"pe": block.tensor, "act": block.scalar, "dve": block.vector, "pool": block.gpsimd, "sp": block.sync}
        for e in self.ENG:
            ops = self.ops[e]

            def body(engine, ops=ops, e=e):
                for waits, fn, kind, dsem in ops:
                    for k, v in waits:
                        engine.wait_ge(sems[k], v)
                    if fn is None:
                        continue
                    inst = fn(engine)
                    if kind == "inc":
                        inst.then_inc(sems[e], 1)
                    elif kind == "dma":
                        inst.then_inc(sems[dsem.name], dsem.step)

            reg[e](body)


class WeightStream:
    def __init__(self, prog, name, slots, lookahead):
        self.p = prog
        self.name = name
        self.slots = slots
        self.n = len(slots)
        self.look = lookahead
        self.plan = []
        self.issued = 0
        self.cur = 0
        self.fence = None
        self.limit = None

    def reset_for_real(self):
        self.issued = 0
        self.cur = 0

    def _issue(self, k):
        slot = self.slots[k % self.n]
        key = (self.name, k % self.n)
        ds = self.p.dsem(f"{self.name}{k % self.n}")
        for out_ap, in_ap in self.plan[k](slot):
            self.p.dma("pool", (lambda e, o=out_ap, i=in_ap: e.dma_start(out=o, in_=i)),
                       reads=((self.fence,) if self.fence else ()), writes=(key,), dsem=ds)

    def next(self, loader):
        k = self.cur
        self.cur += 1
        if self.p.dry:
            self.plan.append(loader)
            return self.slots[k % self.n], (self.name, k % self.n)
        lim = self.limit if self.limit is not None else len(self.plan)
        while self.issued < min(len(self.plan), k + self.look + 1, lim):
            self._issue(self.issued)
            self.issued += 1
        return self.slots[k % self.n], (self.name, k % self.n)


def bcast(ap, shape):
    return ap.broadcast_to(list(shape))


def build_program():
    nc = bass.Bass("TRN2", target_bir_lowering=False)
    dt = {}

    def din(name, shape):
        dt[name] = nc.dram_tensor(name, list(shape), F32, kind="ExternalInput").ap()
        return dt[name]

    def dout(name, shape):
        dt[name] = nc.dram_tensor(name, list(shape), F32, kind="ExternalOutput").ap()
        return dt[name]

    def dscr(name, shape, dtype):
        return nc.dram_tensor(name, list(shape), dtype).ap()

    xT_in = din("xT", (D, NCOL))
    gains_in = din("gains", (128, 13 * KC))
    w_gu = [din("ffn1_w_gu", (DEPTH, D, 2 * DFF)), din("ffn2_w_gu", (DEPTH, D, 2 * DFF))]
    w_dn = [din("ffn1_w_down", (DEPTH, DFF, D)), din("ffn2_w_down", (DEPTH, DFF, D))]
    ident_in = din("ident", (128, 128))
    mask3_in = din("mask3", (128, 384))
    flag_in = din("flag", (128, 2))
    ssm_small_in = din("ssm_small", (2, 128, 96))
    ssm_b_in = din("ssm_b", (2, 128, 1024))
    ssm_c_in = din("ssm_c", (2, 128, 1024))
    ssm_dsh_in = din("ssm_dsh", (2, 128, 64))
    ssm_win = din("ssm_w_in", (2, D, D))
    ssm_wglu = din("ssm_w_glu", (2, D, 2 * D))
    sst_in = din("sst_in", (2, 128, 1024))
    ab_w_in = din("ab_w_in", (2, D, 1792))
    ab_w_out = din("ab_w_out", (2, D, D))
    abc_in = din("abc", (2, 128, 140))
    rmat_in = din("rmat", (128, 128))
    b2_in = din("b2", (128, 128))
    amask_in = din("amask", (128, 512))
    rope_in = din("rope", (2, 2, 128, STW))
    ckt_in = din("ckt", (2, 128, 2, 16, 128))
    cv_in = din("cv", (2, 128, 16, 128))
    cknat_in = din("cknat", (2, 16, 128, 128))
    cvnat_in = din("cvnat", (2, 16, 128, 128))
    sconv_in = din("sconv_in", (2, 128, 4, 16, 30))
    yT_out = dout("yT", (D, NCOL))
    pwk_out = dout("pwk", (2, 64, 2, 128))
    pwv_out = dout("pwv", (2, 128, 128))
    pconv_out = dout("pconv", (2, 128, 4, 30))
    swk_out = dout("swk", (2, 16, 128, 128))
    swv_out = dout("swv", (2, 16, 128, 128))
    sconv_out = dout("sconv", (2, 128, 4, 16, 30))
    pssm_out = dout("pssm", (2, 128, 64))
    sssm_out = dout("sssm", (2, 128, 1024))

    T_scr = dscr("T_scr", (2, KC, 128, 8 * 3 * 128), BF16)
    W_scr = dscr("W_scr", (2, KC, 128, 8 * 256), BF16)
    Y_scr = dscr("Y_scr", (2, 2, 128, 32 * 256), BF16)
    scr_u = dscr("scr_u", (KC, 128, 16 * 65), BF16)
    scr_y = dscr("scr_y", (KC, 128, 16 * 65), BF16)
    scr_su = dscr("scr_su", (128, KC * 16), BF16)
    scr_sy = dscr("scr_sy", (16, 64 * 16), BF16)
    ccab_in = nc.dram_tensor("ccab_in", [128, 256], F32)
    ccab_out = nc.dram_tensor("ccab_out", [256, 256], F32)
    cc_in = nc.dram_tensor("cc_in", [128, 64], F32)
    cc_out = nc.dram_tensor("cc_out", [256, 64], F32)
    RG = [[0, 1], [2, 3], [4, 5], [6, 7]]

    es = ExitStack()
    with es:
        def sbp(stack, name, shape, dtype):
            return stack.enter_context(nc.sbuf_tensor(name, list(shape), dtype))

        xT = sbp(es, "xT_sb", (128, KC, NCOL), F32)
        gains = sbp(es, "gains_sb", (128, 13 * KC), F32)
        ones_bf = sbp(es, "ones_bf", (128, 128), BF16)
        ident_f = sbp(es, "ident_f", (128, 128), F32)
        ident_bf = sbp(es, "ident_bf", (128, 128), BF16)
        flag = sbp(es, "flag_sb", (128, 2), F32)
        lamt = sbp(es, "lamt", (128, 2, 6, 32), F32)
        dsh = sbp(es, "dsh", (128, 2, 64), F32)
        wA = [sbp(es, f"wA{i}", (128, 2048), BF16) for i in range(4)]
        fence_t = sbp(es, "fence_t", (128, 8), F32)
        pall = es.enter_context(nc.psum_tensor("pall", [128, 4096], F32))
        ps = [pall[:, b * 512:(b + 1) * 512] for b in range(8)]

        uid = [0]

        def emit(p, wsA, wsB):
            def nm(s):
                uid[0] += 1
                return f"{s}_{uid[0]}"

            p.dma("sp", (lambda e: e.dma_start(out=xT[:, :, :], in_=xT_in.rearrange("(kc p) n -> p kc n", p=128))),
                  reads=(), writes=[("xT", kc, st) for kc in range(KC) for st in (0, 1)], dsem=p.dsem("xload"))
            p.dma("sp", (lambda e: e.dma_start(out=gains[:, :], in_=gains_in[:, :])), reads=(), writes=["gains"], dsem=p.dsem("gload"))
            p.dma("sp", (lambda e: e.dma_start(out=ident_f[:, :], in_=ident_in[:, :])), reads=(), writes=["ident_f"], dsem=p.dsem("iload"))
            p.dma("sp", (lambda e: e.dma_start(out=flag[:, :], in_=flag_in[:, :])), reads=(), writes=["flag"], dsem=p.dsem("fload"))
            p.dma("sp", (lambda e: e.dma_start(out=dsh[:, :, :], in_=ssm_dsh_in.rearrange("l p g -> p l g"))), reads=(), writes=["dsh"], dsem=p.dsem("dload"))
            p.op("dve", lambda e: e.memset(ones_bf[:, :], 1.0), writes=["ones"])
            p.op("dve", lambda e: e.tensor_copy(out=ident_bf[:, :], in_=ident_f[:, :]), reads=["ident_f"], writes=["ident"])

            def rmsnorm(st, nidx, B, out_h=True):
                c0 = st * STW
                for kc in range(KC):
                    b = kc % 2
                    p.op("act", (lambda e, kc=kc, b=b: e.activation(out=B["sq"][b][:, :], in_=xT[:, kc, c0:c0 + STW], func=AF.Square)),
                         reads=[("xT", kc, st)], writes=[(B["id"], "sq", b)])
                    for ti, (t0, t1) in enumerate(TILES):
                        p.op("pe", (lambda e, kc=kc, b=b, ti=ti, t0=t0, t1=t1: e.matmul(
                            ps[5 + ti][:, 0:t1 - t0], ones_bf[:, :], B["sq"][b][:, t0:t1], start=(kc == 0), stop=(kc == KC - 1))),
                            reads=[(B["id"], "sq", b), "ones"], writes=[("ps", 5 + ti)], inc=True)
                for ti, (t0, t1) in enumerate(TILES):
                    p.op("act", (lambda e, ti=ti, t0=t0, t1=t1: e.activation(
                        out=B["rtmp"][:, t0:t1], in_=ps[5 + ti][:, 0:t1 - t0], func=AF.Sqrt, scale=1.0 / D, bias=EPS)),
                        reads=[("ps", 5 + ti)], writes=[(B["id"], "rtmp", ti)])
                    p.op("dve", (lambda e, t0=t0, t1=t1: e.reciprocal(out=B["rstd"][:, t0:t1], in_=B["rtmp"][:, t0:t1])),
                         reads=[(B["id"], "rtmp", ti)], writes=[(B["id"], "rstd", ti)])
                if out_h:
                    for kc in range(KC):
                        p.op("dve", (lambda e, kc=kc: e.scalar_tensor_tensor(
                            out=B["hT"][:, kc, :], in0=xT[:, kc, c0:c0 + STW], scalar=gains[:, nidx * KC + kc:nidx * KC + kc + 1],
                            in1=B["rstd"][:, :], op0=ALU.mult, op1=ALU.mult)),
                            reads=[("xT", kc, st), "gains"] + [(B["id"], "rstd", ti) for ti in range(3)], writes=[(B["id"], "hT", kc)])

            def norm_bufs(stack, tag):
                B = {"id": tag}
                B["hT"] = sbp(stack, nm("hT"), (128, KC, STW), BF16)
                B["sq"] = [sbp(stack, nm("sq"), (128, STW), BF16) for _ in range(2)]
                B["rstd"] = sbp(stack, nm("rstd"), (128, STW), F32)
                B["rtmp"] = sbp(stack, nm("rtmp"), (128, STW), F32)
                return B

            def ffn_phase(l, which, nidx):
                p.barrier()
                with ExitStack() as ph:
                    B = norm_bufs(ph, nm("ffn"))
                    wsB.slots = [sbp(ph, nm("wB"), (128, NJ * 128), BF16) for _ in range(2)]
                    p.op("dve", lambda e: e.memset(fence_t[:, 0:4], 0.0), writes=["fenceB"])
                    wsB.fence = "fenceB"
                    wsB.limit = wsB.cur + 2 * KC
                    act = sbp(ph, nm("act"), (128, NJ, STW), BF16)
                    sg = [sbp(ph, nm("sg"), (128, 352), BF16) for _ in range(2)]
                    hT = B["hT"]
                    Wgu = w_gu[which]
                    Wdn = w_dn[which]
                    for st in (0, 1):
                        c0 = st * STW
                        rmsnorm(st, nidx, B)
                        step = 0
                        for j in range(NJ):
                            def loader(slot, j=j):
                                v = slot[:, :].rearrange("p (kc two f) -> p kc two f", kc=KC, two=2)
                                return [
                                    (v[:, :, 0, :], Wgu[l, :, j * 128:(j + 1) * 128].rearrange("(kc p) f -> p kc f", p=128)),
                                    (v[:, :, 1, :], Wgu[l, :, DFF + j * 128:DFF + (j + 1) * 128].rearrange("(kc p) f -> p kc f", p=128)),
                                ]
                            slot, wkey = wsA.next(loader)
                            wv = slot[:, :].rearrange("p (kc two f) -> p kc two f", kc=KC, two=2)
                            for ti, (t0, t1) in enumerate(TILES):
                                n = t1 - t0
                                pb = step % 2
                                step += 1
                                gps, ups = ps[pb], ps[2 + pb]
                                for kc in range(KC):
                                    p.op("pe", (lambda e, kc=kc, gps=gps, t0=t0, t1=t1, n=n, wv=wv: e.matmul(
                                        gps[:, 0:n], wv[:, kc, 0, :], hT[:, kc, t0:t1], start=(kc == 0), stop=(kc == KC - 1))),
                                        reads=[wkey, (B["id"], "hT", kc)], writes=[("ps", pb)], inc=(kc == KC - 1))
                                for kc in range(KC):
                                    p.op("pe", (lambda e, kc=kc, ups=ups, t0=t0, t1=t1, n=n, wv=wv: e.matmul(
                                        ups[:, 0:n], wv[:, kc, 1, :], hT[:, kc, t0:t1], start=(kc == 0), stop=(kc == KC - 1))),
                                        reads=[wkey, (B["id"], "hT", kc)], writes=[("ps", 2 + pb)], inc=(kc == KC - 1))
                                p.op("act", (lambda e, gps=gps, pb=pb, n=n: e.activation(out=sg[pb][:, 0:n], in_=gps[:, 0:n], func=AF.Silu)),
                                     reads=[("ps", pb)], writes=[(B["id"], "sg", pb)])
                                p.op("dve", (lambda e, ups=ups, pb=pb, n=n, j=j, t0=t0, t1=t1: e.tensor_tensor(
                                    out=act[:, j, t0:t1], in0=sg[pb][:, 0:n], in1=ups[:, 0:n], op=ALU.mult)),
                                    reads=[(B["id"], "sg", pb), ("ps", 2 + pb)], writes=[(B["id"], "act", j, ti)])
                        step = 0
                        for dc in range(KC):
                            def loader2(slot, dc=dc):
                                v = slot[:, :].rearrange("p (j d) -> p j d", j=NJ)
                                return [(v, Wdn[l, :, dc * 128:(dc + 1) * 128].rearrange("(j p) d -> p j d", p=128))]
                            slot, wkey = wsB.next(loader2)
                            wv = slot[:, :].rearrange("p (j d) -> p j d", j=NJ)
                            for ti, (t0, t1) in enumerate(TILES):
                                n = t1 - t0
                                pb = 4 + (step % 2)
                                step += 1
                                ops_ = ps[pb]
                                for j in range(NJ):
                                    p.op("pe", (lambda e, j=j, ops_=ops_, n=n, t0=t0, t1=t1, wv=wv: e.matmul(
                                        ops_[:, 0:n], wv[:, j, :], act[:, j, t0:t1], start=(j == 0), stop=(j == NJ - 1))),
                                        reads=[wkey, (B["id"], "act", j, ti)], writes=[("ps", pb)], inc=(j == NJ - 1))
                                p.op("dve", (lambda e, ops_=ops_, n=n, dc=dc, t0=t0, t1=t1, c0=c0: e.scalar_tensor_tensor(
                                    out=xT[:, dc, c0 + t0:c0 + t1], in0=ops_[:, 0:n], scalar=0.5, in1=xT[:, dc, c0 + t0:c0 + t1],
                                    op0=ALU.mult, op1=ALU.add)),
                                    reads=[("ps", pb), ("xT", dc, st)], writes=[("xT", dc, st)])

            def final_phase():
                p.barrier()
                with ExitStack() as ph:
                    B = norm_bufs(ph, nm("fin"))
                    yo = [sbp(ph, nm("yo"), (128, STW), F32) for _ in range(2)]
                    for st in (0, 1):
                        c0 = st * STW
                        rmsnorm(st, 12, B, out_h=False)
                        for kc in range(KC):
                            b = kc % 2
                            p.op("dve", (lambda e, kc=kc, b=b, c0=c0: e.scalar_tensor_tensor(
                                out=yo[b][:, :], in0=xT[:, kc, c0:c0 + STW], scalar=gains[:, 12 * KC + kc:12 * KC + kc + 1],
                                in1=B["rstd"][:, :], op0=ALU.mult, op1=ALU.mult)),
                                reads=[("xT", kc, st), "gains"] + [(B["id"], "rstd", ti) for ti in range(3)], writes=[(B["id"], "yo", b)])
                            p.dma("sp", (lambda e, kc=kc, b=b, c0=c0: e.dma_start(out=yT_out[kc * 128:(kc + 1) * 128, c0:c0 + STW], in_=yo[b][:, :])),
                                  reads=[(B["id"], "yo", b)], writes=[("yout", kc, st)], dsem=p.dsem(f"ystore{b}"))

            def ssm_precompute(i):
                p.barrier()
                with ExitStack() as ph:
                    pid = nm("pc")
                    sc = sbp(ph, nm("sc"), (128, 40, 32), F32)
                    small = sbp(ph, nm("small"), (128, 3, 32), F32)
                    Bin = sbp(ph, nm("Bin"), (128, 2, 32, 16), F32)
                    Cin = sbp(ph, nm("Cin"), (128, 2, 32, 16), F32)
                    LPr = sbp(ph, nm("LPr"), (128, LCH + 1, 32), F32)
                    LPi = sbp(ph, nm("LPi"), (128, LCH + 1, 32), F32)
                    ILr = sbp(ph, nm("ILr"), (128, LCH + 1, 32), F32)
                    ILi = sbp(ph, nm("ILi"), (128, LCH + 1, 32), F32)
                    BBr = sbp(ph, nm("BBr"), (128, 32, 16), F32)
                    BBi = sbp(ph, nm("BBi"), (128, 32, 16), F32)
                    t1 = sbp(ph, nm("t1"), (128, 8, 256), F32)
                    t2 = sbp(ph, nm("t2"), (128, 8, 256), F32)
                    Xr = sbp(ph, nm("Xr"), (128, 8, 256), BF16)
                    Xi = sbp(ph, nm("Xi"), (128, 8, 256), BF16)
                    Yr = sbp(ph, nm("Yr"), (128, 8, 256), BF16)
                    Yn = sbp(ph, nm("Yn"), (128, 8, 256), BF16)
                    tsb = [sbp(ph, nm("tsb"), (128, 8, 384), BF16) for _ in range(2)]
                    wsb = [sbp(ph, nm("wsb"), (128, 8, 256), BF16) for _ in range(2)]
                    mask3 = sbp(ph, nm("mask3"), (128, 384), F32)
                    K_ = pid

                    def S(k):
                        return sc[:, k, :]

                    seq = {"n": 0}

                    def v(fn, reads, writes, eng="dve"):
                        p.op(eng, fn, reads=[(K_, r) for r in reads], writes=[(K_, w) for w in writes])

                    def tt(o, a, b, op, reads, writes, eng="dve"):
                        v((lambda e: e.tensor_tensor(out=o, in0=a, in1=b, op=op)), reads, writes, eng)

                    def ts(o, a, s1, s2, op0, op1, reads, writes):
                        v((lambda e: e.tensor_scalar(out=o, in0=a, scalar1=s1, scalar2=s2, op0=op0, op1=op1)), reads, writes)

                    def actf(o, a, func, scale, reads, writes):
                        v((lambda e: e.activation(out=o, in_=a, func=func, scale=scale)), reads, writes, "act")

                    p.dma("sp", (lambda e: e.dma_start(out=small[:, :, :], in_=ssm_small_in[i].rearrange("p (a g) -> p a g", a=3))),
                          reads=(), writes=[(K_, "small")], dsem=p.dsem("pc_small"))
                    p.dma("sp", (lambda e: e.dma_start(out=Bin[:, :, :, :], in_=ssm_b_in[i].rearrange("p (a g c) -> p a g c", a=2, g=32))),
                          reads=(), writes=[(K_, "Bin")], dsem=p.dsem("pc_b"))
                    p.dma("sp", (lambda e: e.dma_start(out=Cin[:, :, :, :], in_=ssm_c_in[i].rearrange("p (a g c) -> p a g c", a=2, g=32))),
                          reads=(), writes=[(K_, "Cin")], dsem=p.dsem("pc_c"))
                    p.dma("sp", (lambda e: e.dma_start(out=mask3[:, :], in_=mask3_in[:, :])),
                          reads=(), writes=[(K_, "mask3")], dsem=p.dsem("pc_m"))
                    a_re, a_im, ldt = small[:, 0, :], small[:, 1, :], small[:, 2, :]
                    DT, ARE, ANG, MAG, SN, SH_, CS, TA, TB, RINV, DEN, RDEN, NR, FRE, FIM, TC = range(16)
                    import math as _m
                    YS, UP, X2, TH = 16, 17, 18, 19

                    def stt(o, a, sc_, b, op0, op1, reads, writes):
                        v((lambda e: e.scalar_tensor_tensor(out=o, in0=a, scalar=sc_, in1=b, op0=op0, op1=op1)), reads, writes)

                    def horner(out_slot, x_slot, coefs, xkey, okey):
                        n_ = len(coefs) - 1
                        ts(S(UP), S(x_slot), float(coefs[n_]), None, ALU.mult, ALU.bypass, [xkey], ["up"])
                        for k_ in range(n_ - 1, 0, -1):
                            stt(S(UP), S(UP), float(coefs[k_]), S(x_slot), ALU.add, ALU.mult, ["up", xkey], ["up"])
                        ts(S(out_slot), S(UP), float(coefs[0]), None, ALU.add, ALU.bypass, ["up"], [okey])

                    ts(S(YS), ldt, 0.125, None, ALU.mult, ALU.bypass, ["small"], ["ys"])
                    horner(DT, YS, [1.0 / _m.factorial(k_) for k_ in range(13)], "ys", "dt")
                    for _ in range(3):
                        tt(S(DT), S(DT), S(DT), ALU.mult, ["dt"], ["dt"])
                    tt(S(ARE), a_re, S(DT), ALU.mult, ["small", "dt"], ["are"])
                    tt(S(ANG), a_im, S(DT), ALU.mult, ["small", "dt"], ["ang"])
                    horner(MAG, ARE, [1.0 / _m.factorial(k_) for k_ in range(9)], "are", "mag")
                    ts(S(TH), S(ANG), 1.0 / 16, None, ALU.mult, ALU.bypass, ["ang"], ["th"])
                    tt(S(X2), S(TH), S(TH), ALU.mult, ["th"], ["x2"])
                    horner(SH_, X2, [(-1.0) ** k_ / _m.factorial(2 * k_ + 1) for k_ in range(8)], "x2", "sh")
                    tt(S(SN), S(SH_), S(TH), ALU.mult, ["sh", "th"], ["sn"])
                    horner(CS, X2, [(-1.0) ** k_ / _m.factorial(2 * k_) for k_ in range(9)], "x2", "cs")
                    for _ in range(4):
                        tt(S(TA), S(CS), S(CS), ALU.mult, ["cs"], ["ta"])
                        tt(S(TB), S(SN), S(SN), ALU.mult, ["sn"], ["tb"])
                        v((lambda e: e.scalar_tensor_tensor(out=S(TC), in0=S(SN), scalar=2.0, in1=S(CS), op0=ALU.mult, op1=ALU.mult)),
                          ["sn", "cs"], ["tc"])
                        tt(S(CS), S(TA), S(TB), ALU.subtract, ["ta", "tb"], ["cs"])
                        v((lambda e: e.tensor_copy(out=S(SN), in_=S(TC))), ["tc"], ["sn"])
                    lam_re, lam_im = lamt[:, i, 0, :], lamt[:, i, 1, :]
                    tt(lam_re, S(MAG), S(CS), ALU.mult, ["mag", "cs"], ["lam"])
                    tt(lam_im, S(MAG), S(SN), ALU.mult, ["mag", "sn", "lam"], ["lam"])
                    tt(S(TA), S(MAG), S(MAG), ALU.mult, ["mag"], ["ta"])
                    v((lambda e: e.reciprocal(out=S(RINV), in_=S(TA))), ["ta"], ["rinv"])
                    v((lambda e: e.memset(LPr[:, 0, :], 1.0)), [], ["LP0"])
                    v((lambda e: e.memset(LPi[:, 0, :], 0.0)), ["LP0"], ["LP0"])
                    v((lambda e: e.tensor_copy(out=LPr[:, 1, :], in_=lam_re)), ["lam"], ["LP"])
                    v((lambda e: e.tensor_copy(out=LPi[:, 1, :], in_=lam_im)), ["lam", "LP"], ["LP"])
                    tt(ILr[:, 1, :], lam_re, S(RINV), ALU.mult, ["lam", "rinv"], ["IL"])
                    v((lambda e: e.scalar_tensor_tensor(out=ILi[:, 1, :], in0=lam_im, scalar=-1.0, in1=S(RINV), op0=ALU.mult, op1=ALU.mult)),
                      ["lam", "rinv", "IL"], ["IL"])
                    PT1 = sbp(ph, nm("pt1"), (128, 8, 32), F32)
                    PT2 = sbp(ph, nm("pt2"), (128, 8, 32), F32)
                    for (Pr, Pi, key) in ((LPr, LPi, "LP"), (ILr, ILi, "IL")):
                        n = 1
                        while n < LCH:
                            br = bcast(Pr[:, n:n + 1, :], (128, n, 32))
                            bi = bcast(Pi[:, n:n + 1, :], (128, n, 32))
                            tt(PT1[:, 0:n, :], Pr[:, 1:n + 1, :], br, ALU.mult, [key], ["pt1"])
                            tt(PT2[:, 0:n, :], Pi[:, 1:n + 1, :], bi, ALU.mult, [key], ["pt2"])
                            tt(Pr[:, n + 1:2 * n + 1, :], PT1[:, 0:n, :], PT2[:, 0:n, :], ALU.subtract, ["pt1", "pt2", key], [key + "w"])
                            tt(PT1[:, 0:n, :], Pr[:, 1:n + 1, :], bi, ALU.mult, [key, key + "w"], ["pt1"])
                            tt(PT2[:, 0:n, :], Pi[:, 1:n + 1, :], br, ALU.mult, [key, key + "w"], ["pt2"])
                            tt(Pi[:, n + 1:2 * n + 1, :], PT1[:, 0:n, :], PT2[:, 0:n, :], ALU.add, ["pt1", "pt2", key, key + "w"], [key])
                            n *= 2
                    v((lambda e: e.tensor_copy(out=lamt[:, i, 2, :], in_=LPr[:, LCH, :])), ["LP"], ["lamL"])
                    v((lambda e: e.tensor_copy(out=lamt[:, i, 3, :], in_=LPi[:, LCH, :])), ["LP", "lamL"], ["lamL"])
                    ts(lamt[:, i, 4, :], LPi[:, LCH, :], -1.0, None, ALU.mult, ALU.bypass, ["LP", "lamL"], ["lamL"])
                    tt(S(TA), a_re, a_re, ALU.mult, ["small"], ["ta"])
                    tt(S(TB), a_im, a_im, ALU.mult, ["small"], ["tb"])
                    tt(S(DEN), S(TA), S(TB), ALU.add, ["ta", "tb"], ["den"])
                    v((lambda e: e.reciprocal(out=S(RDEN), in_=S(DEN))), ["den"], ["rden"])
                    ts(S(NR), lam_re, -1.0, None, ALU.add, ALU.bypass, ["lam"], ["nr"])
                    tt(S(TA), S(NR), a_re, ALU.mult, ["nr", "small"], ["ta"])
                    tt(S(TB), lam_im, a_im, ALU.mult, ["lam", "small"], ["tb"])
                    tt(S(TC), S(TA), S(TB), ALU.add, ["ta", "tb"], ["tc"])
                    tt(S(FRE), S(TC), S(RDEN), ALU.mult, ["tc", "rden"], ["fre"])
                    tt(S(TA), lam_im, a_re, ALU.mult, ["lam", "small"], ["ta"])
                    tt(S(TB), S(NR), a_im, ALU.mult, ["nr", "small"], ["tb"])
                    tt(S(TC), S(TA), S(TB), ALU.subtract, ["ta", "tb"], ["tc"])
                    tt(S(FIM), S(TC), S(RDEN), ALU.mult, ["tc", "rden"], ["fim"])
                    fre_b = bcast(S(FRE).unsqueeze(2), (128, 32, 16))
                    fim_b = bcast(S(FIM).unsqueeze(2), (128, 32, 16))
                    tt(t1[:, 0:2, :].rearrange("p a (g c) -> p (a g) c", c=16), fre_b, Bin[:, 0, :, :], ALU.mult, ["fre", "Bin"], ["t1"])
                    tt(t2[:, 0:2, :].rearrange("p a (g c) -> p (a g) c", c=16), fim_b, Bin[:, 1, :, :], ALU.mult, ["fim", "Bin"], ["t2"])
                    tt(BBr[:, :, :], t1[:, 0:2, :].rearrange("p a (g c) -> p (a g) c", c=16), t2[:, 0:2, :].rearrange("p a (g c) -> p (a g) c", c=16),
                       ALU.subtract, ["t1", "t2"], ["BB"])
                    tt(t1[:, 0:2, :].rearrange("p a (g c) -> p (a g) c", c=16), fre_b, Bin[:, 1, :, :], ALU.mult, ["fre", "Bin", "BB"], ["t1"])
                    tt(t2[:, 0:2, :].rearrange("p a (g c) -> p (a g) c", c=16), fim_b, Bin[:, 0, :, :], ALU.mult, ["fim", "Bin", "BB"], ["t2"])
                    tt(BBi[:, :, :], t1[:, 0:2, :].rearrange("p a (g c) -> p (a g) c", c=16), t2[:, 0:2, :].rearrange("p a (g c) -> p (a g) c", c=16),
                       ALU.add, ["t1", "t2", "BB"], ["BB"])
                    for gb in range(4):
                        g0 = gb * 8

                        def pw(P):
                            return bcast(P[:, 1:LCH + 1, g0:g0 + 8].rearrange("p j g -> p g j").unsqueeze(3), (128, 8, LCH, 16))

                        def gc(X):
                            return bcast(X.unsqueeze(2), (128, 8, LCH, 16))

                        t1v = t1[:, :, :].rearrange("p g (s c) -> p g s c", c=16)
                        t2v = t2[:, :, :].rearrange("p g (s c) -> p g s c", c=16)

                        def cmul(outr, outi_neg, Pr, Pi, Ar, Ai, key, negate_im):
                            tt(t1v, pw(Pr), gc(Ar), ALU.mult, ["LP", "IL", "BB", "Cin", key], ["t1"])
                            tt(t2v, pw(Pi), gc(Ai), ALU.mult, ["LP", "IL", "BB", "Cin", key], ["t2"])
                            for sb_ in range(NBK):
                                ov = outr[:, :, sb_ * 128:(sb_ + 1) * 128].rearrange("p g (c s) -> p g s c", s=8)
                                tt(ov, t1v[:, :, sb_ * 8:(sb_ + 1) * 8, :], t2v[:, :, sb_ * 8:(sb_ + 1) * 8, :], ALU.subtract, ["t1", "t2"], [key + "r"])
                            tt(t1v, pw(Pr), gc(Ai), ALU.mult, ["LP", "IL", "BB", "Cin", key + "r"], ["t1"])
                            tt(t2v, pw(Pi), gc(Ar), ALU.mult, ["LP", "IL", "BB", "Cin", key + "r"], ["t2"])
                            for sb_ in range(NBK):
                                ov = outi_neg[:, :, sb_ * 128:(sb_ + 1) * 128].rearrange("p g (c s) -> p g s c", s=8)
                                a_, b_ = t1v[:, :, sb_ * 8:(sb_ + 1) * 8, :], t2v[:, :, sb_ * 8:(sb_ + 1) * 8, :]
                                tt(ov, a_, b_, ALU.add, ["t1", "t2"], [key + "i"])
                            if negate_im:
                                flat = outi_neg[:, :, :].rearrange("p g f -> p (g f)")
                                ts(flat, flat, -1.0, None, ALU.mult, ALU.bypass, [key + "i"], [key + "i"])

                        cmul(Xr, Xi, ILr, ILi, BBr[:, g0:g0 + 8, :], BBi[:, g0:g0 + 8, :], f"X", False)
                        cmul(Yr, Yn, LPr, LPi, Cin[:, 0, g0:g0 + 8, :], Cin[:, 1, g0:g0 + 8, :], f"Y", True)
                        p.dma("sp", (lambda e, g0=g0: e.dma_start(out=Y_scr[i, 0, :, g0 * 256:(g0 + 8) * 256], in_=Yr[:, :, :].rearrange("p g f -> p (g f)"))),
                              reads=[(K_, "Yr")], writes=[("Y_scr", i, 0, gb)], dsem=p.dsem("pc_y0"))
                        p.dma("sp", (lambda e, g0=g0: e.dma_start(out=Y_scr[i, 1, :, g0 * 256:(g0 + 8) * 256], in_=Yn[:, :, :].rearrange("p g f -> p (g f)"))),
                              reads=[(K_, "Yi")], writes=[("Y_scr", i, 1, gb)], dsem=p.dsem("pc_y1"))
                        for gh in range(2):
                            kc = gh * 4 + gb
                            hb = gh * 64
                            tb_ = tsb[gh]
                            wb_ = wsb[gh]
                            for g8 in range(8):
                                pb = g8 % 2
                                tps = ps[pb][:, 0:384].rearrange("p (b f) -> p b f", b=3)
                                wps = ps[2 + pb][:, 0:256].rearrange("p (b f) -> p b f", b=2)
                                for bi_, (sb_, ib_) in enumerate(((0, 0), (0, 1), (1, 1))):
                                    p.op("pe", (lambda e, tps=tps, bi_=bi_, sb_=sb_, ib_=ib_, g8=g8, hb=hb: e.matmul(
                                        tps[:, bi_, :], Xr[hb:hb + 64, g8, sb_ * 128:(sb_ + 1) * 128], Yr[hb:hb + 64, g8, ib_ * 128:(ib_ + 1) * 128],
                                        start=True, stop=False)), reads=[(K_, "Xr"), (K_, "Yr")], writes=[("ps", pb)], inc=False)
                                    p.op("pe", (lambda e, tps=tps, bi_=bi_, sb_=sb_, ib_=ib_, g8=g8, hb=hb: e.matmul(
                                        tps[:, bi_, :], Xi[hb:hb + 64, g8, sb_ * 128:(sb_ + 1) * 128], Yn[hb:hb + 64, g8, ib_ * 128:(ib_ + 1) * 128],
                                        start=False, stop=True)), reads=[(K_, "Xi"), (K_, "Yi")], writes=[("ps", pb)], inc=(bi_ == 2))
                                for sb_ in range(2):
                                    for ri, X in enumerate((Xr, Xi)):
                                        p.op("pe", (lambda e, wps=wps, sb_=sb_, ri=ri, X=X, g8=g8, hb=hb: e.matmul(
                                            wps[:, sb_, ri * 64:(ri + 1) * 64], X[hb:hb + 64, g8, sb_ * 128:(sb_ + 1) * 128], ident_bf[hb:hb + 64, hb:hb + 64],
                                            start=True, stop=True)), reads=[(K_, "Xr"), (K_, "Xi"), "ident"], writes=[("ps", 2 + pb)],
                                            inc=(sb_ == 1 and ri == 1))
                                p.op("dve", (lambda e, tps=tps, tb_=tb_, g8=g8: e.tensor_tensor(
                                    out=tb_[:, g8, :], in0=ps[g8 % 2][:, 0:384], in1=mask3[:, :], op=ALU.mult)),
                                    reads=[("ps", pb), (K_, "mask3")], writes=[(K_, "tsb", gh)])
                                p.op("act", (lambda e, wb_=wb_, g8=g8, pb=pb: e.activation(out=wb_[:, g8, :], in_=ps[2 + pb][:, 0:256], func=AF.Copy)),
                                     reads=[("ps", 2 + pb)], writes=[(K_, "wsb", gh)])
                            p.dma("sp", (lambda e, kc=kc, tb_=tb_: e.dma_start(out=T_scr[i, kc, :, :], in_=tb_[:, :, :].rearrange("p g f -> p (g f)"))),
                                  reads=[(K_, "tsb", gh)], writes=[("T_scr", i, kc)], dsem=p.dsem(f"pc_t{gh}"))
                            p.dma("sp", (lambda e, kc=kc, wb_=wb_: e.dma_start(out=W_scr[i, kc, :, :], in_=wb_[:, :, :].rearrange("p g f -> p (g f)"))),
                                  reads=[(K_, "wsb", gh)], writes=[("W_scr", i, kc)], dsem=p.dsem(f"pc_w{gh}"))

            def ssm_phase(l):
                i = l // 2
                nidx = 3 * l + 1
                p.barrier()
                with ExitStack() as ph:
                    B = {"id": nm("ssm")}
                    B["hT"] = sbp(ph, nm("hT"), (128, KC, STW), BF16)
                    K_ = B["id"]
                    hT = B["hT"]
                    U2 = sbp(ph, nm("U2"), (128, 64, NBK, NCH), BF16)
                    SH = sbp(ph, nm("SH"), (128, 2, 32, NCH + 1), BF16)
                    st_f = [sbp(ph, nm("stf"), (128, 2, 32), F32) for _ in range(2)]
                    wt = sbp(ph, nm("wt"), (128, 2, 32), F32)
                    sct1 = sbp(ph, nm("sct1"), (128, 2, 32), F32)
                    sct2 = sbp(ph, nm("sct2"), (128, 2, 32), F32)
                    hinit = sbp(ph, nm("hinit"), (128, 2, 32), F32)
                    usmp = sbp(ph, nm("usmp"), (128, KC, 16), BF16)
                    U2a = sbp(ph, nm("U2a"), (128, 64, 16), BF16)
                    U2b = sbp(ph, nm("U2b"), (128, 64, 16), BF16)
                    sin_b = sbp(ph, nm("sin_b"), (128, 2, 32, 16), BF16)
                    lr0 = sbp(ph, nm("lr0"), (128, 3, 32), F32)
                    p1 = ExitStack()
                    ph.enter_context(p1)
                    B["sq"] = [sbp(p1, nm("sq"), (128, STW), BF16) for _ in range(2)]
                    B["rstd"] = sbp(p1, nm("rstd"), (128, STW), F32)
                    B["rtmp"] = sbp(p1, nm("rtmp"), (128, STW), F32)
                    sin_f = sbp(p1, nm("sin_f"), (128, 2, 32, 16), F32)
                    nst = sbp(p1, nm("nst"), (128, 2, 32, 16), F32)
                    a1 = sbp(p1, nm("a1"), (128, 32, 16), F32)
                    a2 = sbp(p1, nm("a2"), (128, 32, 16), F32)
                    a3 = sbp(p1, nm("a3"), (128, 2, 32, 16), F32)
                    LR, LI, NLI = lamt[:, i, 2, :], lamt[:, i, 3, :], lamt[:, i, 4, :]
                    lam_re, lam_im = lamt[:, i, 0, :], lamt[:, i, 1, :]

                    p.op("dve", lambda e: e.tensor_scalar(out=lr0[:, 0, :], in0=LR, scalar1=flag[:, 1:2], scalar2=flag[:, 0:1], op0=ALU.mult, op1=ALU.add),
                         reads=["flag"], writes=[(K_, "lr0")])
                    p.op("dve", lambda e: e.tensor_scalar(out=lr0[:, 1, :], in0=LI, scalar1=flag[:, 1:2], scalar2=None, op0=ALU.mult),
                         reads=["flag", (K_, "lr0")], writes=[(K_, "lr0")])
                    p.op("dve", lambda e: e.tensor_scalar(out=lr0[:, 2, :], in0=NLI, scalar1=flag[:, 1:2], scalar2=None, op0=ALU.mult),
                         reads=["flag", (K_, "lr0")], writes=[(K_, "lr0")])
                    p.op("dve", lambda e: e.memset(U2a[:, :, :], 0.0), writes=[(K_, "U2a")])
                    p.op("dve", lambda e: e.memset(U2b[:, :, :], 0.0), writes=[(K_, "U2b")])
                    p.op("dve", lambda e: e.memset(SH[:, :, :, 0:1], 0.0), writes=[(K_, "SH", 0)])
                    p.dma("sp", (lambda e: e.dma_start(out=sin_f[:, :, :, :], in_=sst_in[i].rearrange("p (a g b) -> p a g b", a=2, g=32))),
                          reads=(), writes=[(K_, "sin_f")], dsem=p.dsem("sinload"))
                    p.op("act", lambda e: e.activation(out=sin_b[:, :, :, :], in_=sin_f[:, :, :, :], func=AF.Copy),
                         reads=[(K_, "sin_f")], writes=[(K_, "sin_b")])

                    if True:
                        Wm = [sbp(p1, nm("Wm"), (128, 8, 256), BF16) for _ in range(2)]
                        u_sm = [sbp(p1, nm("u_sm"), (128, 8, NBK, 65), BF16) for _ in range(2)]

                        def bsum(st, kc, Wmk, ncl, cg0, U2src, dst_fn, N, skip_sb0=False):
                            gh = kc // 4
                            hb = gh * 64
                            sps = pall[:, 2048:4096].rearrange("p (a b c) -> p a b c", a=2, b=8)
                            for g8 in range(8):
                                g = kc * 8 + g8
                                for ri in range(2):
                                    sbs = [1] if skip_sb0 else [0, 1]
                                    for sb_ in sbs:
                                        p.op("pe", (lambda e, g8=g8, ri=ri, sb_=sb_, g=g, sbs=sbs: e.matmul(
                                            sps[hb:hb + 64, ri, g8, 0:N], Wmk[:, g8, sb_ * 128 + ri * 64:sb_ * 128 + ri * 64 + 64],
                                            U2src(sb_, g), start=(sb_ == sbs[0]), stop=(sb_ == 1))),
                                            reads=[(K_, "Wm", kc % 2), (K_, "U2", kc, st), (K_, "U2b")],
                                            writes=[("ps", 4), ("ps", 5), ("ps", 6), ("ps", 7)], inc=(g8 == 7 and ri == 1 and sb_ == 1))
                            dst_fn(sps[hb:hb + 64, :, :, 0:N], hb)

                        for st in (0, 1):
                            c0 = st * STW
                            ncl, cg0 = NCL[st], CG0[st]
                            clo = 1 if st == 0 else 0
                            rmsnorm(st, nidx, B)
                            pend = None
                            for kc in range(KC):
                                def loader(slot, kc=kc):
                                    v_ = slot[:, 0:1024].rearrange("p (kc f) -> p kc f", kc=KC)
                                    return [(v_, ssm_win[i, :, kc * 128:(kc + 1) * 128].rearrange("(kc p) f -> p kc f", p=128))]
                                slot, wkey = wsA.next(loader)
                                wv = slot[:, 0:1024].rearrange("p (kc f) -> p kc f", kc=KC)
                                ub = u_sm[kc % 2]
                                for ti, (t0, t1) in enumerate(TILES):
                                    n = t1 - t0
                                    pb = (kc * 3 + ti) % 2
                                    for k2 in range(KC):
                                        p.op("pe", (lambda e, k2=k2, pb=pb, n=n, t0=t0, t1=t1, wv=wv: e.matmul(
                                            ps[pb][:, 0:n], wv[:, k2, :], hT[:, k2, t0:t1], start=(k2 == 0), stop=(k2 == KC - 1))),
                                            reads=[wkey, (K_, "hT", k2)], writes=[("ps", pb)], inc=(k2 == KC - 1))
                                    nmain = min(t1, NMAIN) - t0
                                    ca, cb = t0 // LCH, (t0 + nmain) // LCH
                                    p.op("act", (lambda e, pb=pb, nmain=nmain, ca=ca, cb=cb, ub=ub, clo=clo: e.activation(
                                        out=ub[:, :, :, clo + ca:clo + cb].rearrange("p s b c -> p c b s"),
                                        in_=ps[pb][:, 0:nmain].rearrange("p (c b s) -> p c b s", b=NBK, s=8), func=AF.Copy)),
                                        reads=[("ps", pb)], writes=[(K_, "u_sm", kc % 2)])
                                    if ti == 2:
                                        if st == 0:
                                            p.op("act", (lambda e, pb=pb, nmain=nmain, ub=ub: e.activation(
                                                out=ub[:, :, :, 0:1].rearrange("p s b c -> p c b s"),
                                                in_=ps[pb][:, nmain:nmain + 16].rearrange("p (c b s) -> p c b s", b=NBK, s=8), func=AF.Copy)),
                                                reads=[("ps", pb)], writes=[(K_, "u_sm", kc % 2)])
                                        else:
                                            p.op("act", (lambda e, pb=pb, nmain=nmain, kc=kc: e.activation(
                                                out=usmp[:, kc, :], in_=ps[pb][:, nmain:nmain + 16], func=AF.Copy)),
                                                reads=[("ps", pb)], writes=[(K_, "usmp")])
                                p.dma("sp", (lambda e, kc=kc, ub=ub: e.dma_start(
                                    out=scr_u[kc, :, :], in_=ub[:, :, :, :].rearrange("p s b c -> p (s b c)"))),
                                    reads=[(K_, "u_sm", kc % 2)], writes=[("scr_u", kc)], dsem=p.dsem(f"su_w{kc % 2}"))
                                for g8 in range(8):
                                    src = scr_u[kc, :, :].rearrange("(g c) (s x) -> g (c s) x", c=16, s=8)[g8].rearrange("q (b x) -> q b x", b=NBK)[:, :, 0:ncl]
                                    p.dma("sp", (lambda e, kc=kc, g8=g8, src=src, cg0=cg0, ncl=ncl: e.dma_start(
                                        out=U2[:, kc * 8 + g8, :, cg0:cg0 + ncl], in_=src)),
                                        reads=[("scr_u", kc)], writes=[(K_, "U2", kc, st)], dsem=p.dsem(f"su_r{kc}"))
                                p.dma("sp", (lambda e, kc=kc: e.dma_start(out=Wm[kc % 2][:, :, :], in_=W_scr[i, kc, :, :].rearrange("p (g f) -> p g f", g=8))),
                                      reads=[("W_scr", i, kc)], writes=[(K_, "Wm", kc % 2)], dsem=p.dsem(f"wm{kc % 2}"))

                                def do_b(kc=kc, st=st, ncl=ncl, cg0=cg0):
                                    def dst_fn(src_ps, hb, kc=kc):
                                        g32 = (kc % 4) * 8
                                        p.op("act", (lambda e: e.activation(
                                            out=SH[hb:hb + 64, :, g32:g32 + 8, 1 + cg0:1 + cg0 + ncl], in_=src_ps, func=AF.Copy)),
                                            reads=[("ps", 4), ("ps", 5), ("ps", 6), ("ps", 7)], writes=[(K_, "SH", 1 + st)])
                                    bsum(st, kc, Wm[kc % 2], ncl, cg0, (lambda sb_, g: U2[:, g, sb_, cg0:cg0 + ncl]), dst_fn, ncl)
                                if pend is not None:
                                    pend()
                                pend = do_b
                            pend()
                        p.dma("sp", (lambda e: e.dma_start(out=scr_su[:, :], in_=usmp[:, :, :].rearrange("p k b -> p (k b)"))),
                              reads=[(K_, "usmp")], writes=["scr_su"], dsem=p.dsem("ssu_w"))
                        srcs = scr_su[:, :].rearrange("(g c) (k b) -> c k g b", c=16, b=16)
                        p.dma("sp", (lambda e: e.dma_start(out=U2a[0:128:8, :, :].rearrange("c (k g) b -> c k g b", g=8), in_=srcs)),
                              reads=["scr_su", (K_, "U2a")], writes=[(K_, "U2a")], dsem=p.dsem("ssu_a"))
                        p.dma("sp", (lambda e: e.dma_start(out=U2b[7:128:8, :, :].rearrange("c (k g) b -> c k g b", g=8), in_=srcs)),
                              reads=["scr_su", (K_, "U2b")], writes=[(K_, "U2b")], dsem=p.dsem("ssu_b"))
                        for kc in range(KC):
                            p.dma("sp", (lambda e, kc=kc: e.dma_start(out=Wm[kc % 2][:, :, :], in_=W_scr[i, kc, :, :].rearrange("p (g f) -> p g f", g=8))),
                                  reads=[("W_scr", i, kc)], writes=[(K_, "Wm", kc % 2)], dsem=p.dsem(f"wm{kc % 2}"))

                            def dst_fn(src_ps, hb, kc=kc):
                                g32 = (kc % 4) * 8
                                p.op("act", (lambda e: e.activation(out=nst[hb:hb + 64, :, g32:g32 + 8, :], in_=src_ps, func=AF.Copy)),
                                     reads=[("ps", 4), ("ps", 5), ("ps", 6), ("ps", 7)], writes=[(K_, "nst")])
                            bsum(1, kc, Wm[kc % 2], 16, 0, (lambda sb_, g: U2b[:, g, :]), dst_fn, 16, skip_sb0=True)

                    if True:

                        def b16(t):
                            return bcast(t.unsqueeze(2), (128, 32, 16))

                        def cm(out, xr, xi, cr, ci, key_r):
                            p.op("dve", lambda e: e.tensor_tensor(out=a1[:, :, :], in0=xr, in1=b16(cr), op=ALU.mult), reads=key_r, writes=[(K_, "a1")])
                            p.op("dve", lambda e: e.tensor_tensor(out=a2[:, :, :], in0=xi, in1=b16(ci), op=ALU.mult), reads=key_r, writes=[(K_, "a2")])
                            p.op("dve", lambda e: e.tensor_tensor(out=out[:, 0, :, :], in0=a1[:, :, :], in1=a2[:, :, :], op=ALU.subtract),
                                 reads=[(K_, "a1"), (K_, "a2")], writes=[(K_, "cmo")])
                            p.op("dve", lambda e: e.tensor_tensor(out=a1[:, :, :], in0=xi, in1=b16(cr), op=ALU.mult), reads=key_r + [(K_, "cmo")], writes=[(K_, "a1")])
                            p.op("dve", lambda e: e.tensor_tensor(out=a2[:, :, :], in0=xr, in1=b16(ci), op=ALU.mult), reads=key_r + [(K_, "cmo")], writes=[(K_, "a2")])
                            p.op("dve", lambda e: e.tensor_tensor(out=out[:, 1, :, :], in0=a1[:, :, :], in1=a2[:, :, :], op=ALU.add),
                                 reads=[(K_, "a1"), (K_, "a2")], writes=[(K_, "cmo")])

                        cm(a3, nst[:, 0, :, :], nst[:, 1, :, :], LR, LI, [(K_, "nst")])
                        p.op("dve", lambda e: e.tensor_copy(out=nst[:, :, :, :], in_=a3[:, :, :, :]), reads=[(K_, "cmo")], writes=[(K_, "nst")])
                        cm(a3, sin_f[:, 0, :, :], sin_f[:, 1, :, :], lam_re, lam_im, [(K_, "sin_f"), (K_, "nst")])
                        p.op("dve", lambda e: e.tensor_tensor(out=nst[:, :, :, :], in0=nst[:, :, :, :], in1=a3[:, :, :, :], op=ALU.add),
                             reads=[(K_, "cmo"), (K_, "nst")], writes=[(K_, "nst")])
                        p.dma("sp", (lambda e: e.dma_start(out=sssm_out[i, :, :], in_=nst[:, :, :, :].rearrange("p a g b -> p (a g b)"))),
                              reads=[(K_, "nst")], writes=[("sssm", i)], dsem=p.dsem("sssm_st"))

                    def scan(init_ap, tag):
                        cur = 0
                        if init_ap is None:
                            p.op("dve", lambda e: e.memset(st_f[0][:, :, :], 0.0), reads=[], writes=[(K_, "stf", 0)])
                        else:
                            p.op("dve", lambda e: e.tensor_copy(out=st_f[0][:, :, :], in_=init_ap), reads=[(K_, "hinit")], writes=[(K_, "stf", 0)])
                            p.op("act", lambda e: e.activation(out=SH[:, :, :, 0], in_=init_ap, func=AF.Copy), reads=[(K_, "hinit")], writes=[(K_, "SH", 0)])
                        for cg in range(NCH):
                            s_, d_ = st_f[cur], st_f[1 - cur]
                            last = (cg == NCH - 1)
                            p.op("dve", (lambda e, s_=s_, cg=cg: e.tensor_tensor(out=wt[:, :, :], in0=s_[:, :, :], in1=SH[:, :, :, 1 + cg], op=ALU.add)),
                                 reads=[(K_, "stf", cur), (K_, "SH", 1), (K_, "SH", 2), (K_, "SHs", tag, cg)], writes=[(K_, "wt")])
                            LRc, LIc, NLIc = (lr0[:, 0, :], lr0[:, 1, :], lr0[:, 2, :]) if cg == 0 else (LR, LI, NLI)
                            p.op("dve", (lambda e, LRc=LRc: e.tensor_tensor(out=sct1[:, :, :], in0=wt[:, :, :], in1=bcast(LRc.unsqueeze(1), (128, 2, 32)), op=ALU.mult)),
                                 reads=[(K_, "wt"), (K_, "lr0")], writes=[(K_, "sct1")])
                            p.op("dve", (lambda e, NLIc=NLIc: e.tensor_tensor(out=sct2[:, 0, :], in0=wt[:, 1, :], in1=NLIc, op=ALU.mult)),
                                 reads=[(K_, "wt"), (K_, "lr0")], writes=[(K_, "sct2a")])
                            p.op("dve", (lambda e, LIc=LIc: e.tensor_tensor(out=sct2[:, 1, :], in0=wt[:, 0, :], in1=LIc, op=ALU.mult)),
                                 reads=[(K_, "wt"), (K_, "lr0")], writes=[(K_, "sct2b")])
                            p.op("dve", (lambda e, d_=d_: e.tensor_tensor(out=d_[:, :, :], in0=sct1[:, :, :], in1=sct2[:, :, :], op=ALU.add)),
                                 reads=[(K_, "sct1"), (K_, "sct2a"), (K_, "sct2b")], writes=[(K_, "stf", 1 - cur)])
                            if not last:
                                p.op("act", (lambda e, d_=d_, cg=cg: e.activation(out=SH[:, :, :, 1 + cg], in_=d_[:, :, :], func=AF.Copy)),
                                     reads=[(K_, "stf", 1 - cur)], writes=[(K_, "SHs", tag, cg)])
                            cur = 1 - cur
                        return st_f[cur], cur

                    def scan_final_only():
                        cur = 0
                        p.op("dve", lambda e: e.memset(st_f[0][:, :, :], 0.0), reads=[], writes=[(K_, "stf", 0)])
                        for cg in range(NCH):
                            s_, d_ = st_f[cur], st_f[1 - cur]
                            p.op("dve", (lambda e, s_=s_, cg=cg: e.tensor_tensor(out=wt[:, :, :], in0=s_[:, :, :], in1=SH[:, :, :, 1 + cg], op=ALU.add)),
                                 reads=[(K_, "stf", cur), (K_, "SH", 1), (K_, "SH", 2)], writes=[(K_, "wt")])
                            LRc, LIc, NLIc = (lr0[:, 0, :], lr0[:, 1, :], lr0[:, 2, :]) if cg == 0 else (LR, LI, NLI)
                            p.op("dve", (lambda e, LRc=LRc: e.tensor_tensor(out=sct1[:, :, :], in0=wt[:, :, :], in1=bcast(LRc.unsqueeze(1), (128, 2, 32)), op=ALU.mult)),
                                 reads=[(K_, "wt"), (K_, "lr0")], writes=[(K_, "sct1")])
                            p.op("dve", (lambda e, NLIc=NLIc: e.tensor_tensor(out=sct2[:, 0, :], in0=wt[:, 1, :], in1=NLIc, op=ALU.mult)),
                                 reads=[(K_, "wt"), (K_, "lr0")], writes=[(K_, "sct2a")])
                            p.op("dve", (lambda e, LIc=LIc: e.tensor_tensor(out=sct2[:, 1, :], in0=wt[:, 0, :], in1=LIc, op=ALU.mult)),
                                 reads=[(K_, "wt"), (K_, "lr0")], writes=[(K_, "sct2b")])
                            p.op("dve", (lambda e, d_=d_: e.tensor_tensor(out=d_[:, :, :], in0=sct1[:, :, :], in1=sct2[:, :, :], op=ALU.add)),
                                 reads=[(K_, "sct1"), (K_, "sct2a"), (K_, "sct2b")], writes=[(K_, "stf", 1 - cur)])
                            cur = 1 - cur
                        return st_f[cur], cur

                    p.op("dve", lambda e: e.tensor_scalar(out=SH[:, :, :, 1], in0=SH[:, :, :, 1], scalar1=flag[:, 1:2], scalar2=None, op0=ALU.mult),
                         reads=[(K_, "SH", 1), "flag"], writes=[(K_, "SH", 1)])
                    fin, fcur = scan_final_only()
                    p.dma("sp", (lambda e, fin=fin: e.dma_start(out=cc_in[:, :], in_=fin[:, :, :].rearrange("p a g -> p (a g)"))),
                          reads=[(K_, "stf", fcur)], writes=["cc_in"], dsem=p.dsem(f"ccin"))
                    if DEBUG.get("no_cc"):
                        p.dma("sp", lambda e: e.dma_start(out=cc_out[0:128, :], in_=cc_in[:, :]), reads=["cc_in"], writes=["cc_out"], dsem=p.dsem("ccfake"))
                    else:
                        p.dma("pool", (lambda e: e.collective_compute("AllGather", ALU.bypass, replica_groups=RG,
                                                                      ins=[cc_in.ap().opt()], outs=[cc_out.ap().opt()])),
                              reads=["cc_in"], writes=["cc_out"], dsem=p.dsem("ccsem", 1))
                    p.dma("sp", (lambda e: e.dma_start(out=hinit[:, :, :], in_=cc_out[0:128, :].rearrange("p (a g) -> p a g", a=2))),
                          reads=["cc_out"], writes=[(K_, "hinit")], dsem=p.dsem("ccrd"))
                    p.op("dve", lambda e: e.tensor_scalar(out=hinit[:, :, :], in0=hinit[:, :, :], scalar1=flag[:, 0:1], scalar2=None, op0=ALU.mult),
                         reads=[(K_, "hinit"), "flag"], writes=[(K_, "hinit")])
                    fin2, fcur2 = scan(hinit[:, :, :], "s2")
                    p.dma("sp", (lambda e, fin2=fin2: e.dma_start(out=pssm_out[i, :, :], in_=fin2[:, :, :].rearrange("p a g -> p (a g)"))),
                          reads=[(K_, "stf", fcur2)], writes=[("pssm", i)], dsem=p.dsem("pssm_st"))
                    p.barrier()
                    p1.close()

                    with ExitStack() as p4:
                        Tm = [sbp(p4, nm("Tm"), (128, 8, 384), BF16) for _ in range(1)]
                        Ym = [sbp(p4, nm("Ym"), (128, 2, 8, 256), BF16) for _ in range(1)]
                        G2 = [sbp(p4, nm("G2"), (128, 8, NBK, 65), BF16) for _ in range(2)]
                        e1 = sbp(p4, nm("e1"), (128, LCH * 65), F32)
                        e2 = sbp(p4, nm("e2"), (128, LCH * 65), F32)
                        gsm = sbp(p4, nm("gsm"), (128, 64, 16), BF16)
                        gTs = sbp(p4, nm("gTs"), (128, KC, 16), BF16)
                        sgm = [sbp(p4, nm("sgm"), (128, 400), F32) for _ in range(2)]
                        gT = hT
                        yps = pall[:, 2048:4096].rearrange("p (a b c) -> p a b c", a=8, b=2)

                        def gelu_chain(src_ps_view, u_view, d_view, shape, n_el, out_bf, rkeys, wkey, npart=128):
                            v1 = e1[0:npart, 0:n_el]
                            v2 = e2[0:npart, 0:n_el]

                            def rs(v_):
                                if len(shape) == 3:
                                    return v_.rearrange("p (a b c) -> p a b c", a=shape[0], b=shape[1])
                                return v_.rearrange("p (a b) -> p a b", a=shape[0])
                            p.op("dve", lambda e: e.tensor_tensor(out=rs(v1), in0=u_view, in1=d_view, op=ALU.mult), reads=rkeys, writes=[(K_, "e1")])
                            p.op("dve", lambda e: e.tensor_tensor(out=rs(v1), in0=rs(v1), in1=src_ps_view, op=ALU.add),
                                 reads=[(K_, "e1"), ("ps", 4), ("ps", 5), ("ps", 6), ("ps", 7)], writes=[(K_, "e1")])
                            p.op("dve", lambda e: e.tensor_tensor(out=v2, in0=v1, in1=v1, op=ALU.mult), reads=[(K_, "e1")], writes=[(K_, "e2")])
                            p.op("dve", lambda e: e.tensor_scalar(out=v2, in0=v2, scalar1=0.044715, scalar2=1.0, op0=ALU.mult, op1=ALU.add),
                                 reads=[(K_, "e2")], writes=[(K_, "e2")])
                            p.op("dve", lambda e: e.tensor_tensor(out=v2, in0=v2, in1=v1, op=ALU.mult), reads=[(K_, "e2"), (K_, "e1")], writes=[(K_, "e2")])
                            p.op("act", lambda e: e.activation(out=v2, in_=v2, func=AF.Sigmoid, scale=1.5957691216057308),
                                 reads=[(K_, "e2")], writes=[(K_, "e2")])
                            p.op("dve", lambda e: e.tensor_tensor(out=out_bf, in0=rs(v2), in1=rs(v1), op=ALU.mult),
                                 reads=[(K_, "e2"), (K_, "e1")], writes=[wkey])

                        def ymat(kc, Tmk, Ymk, N, u_fn, h_fn, only_ib0=False):
                            gh = kc // 4
                            hb = gh * 64
                            for g8 in range(8):
                                g = kc * 8 + g8
                                g32 = g % 32
                                for ib_ in ([0] if only_ib0 else [0, 1]):
                                    mms = []
                                    for sb_ in range(ib_ + 1):
                                        blk = {(0, 0): 0, (0, 1): 1, (1, 1): 2}[(sb_, ib_)]
                                        mms.append((Tmk[:, g8, blk * 128:(blk + 1) * 128], u_fn(sb_, g)))
                                    mms.append((Ymk[hb:hb + 64, 0, g8, ib_ * 128:(ib_ + 1) * 128], h_fn(hb, 0, g32)))
                                    mms.append((Ymk[hb:hb + 64, 1, g8, ib_ * 128:(ib_ + 1) * 128], h_fn(hb, 1, g32)))
                                    for mi, (lh, rh) in enumerate(mms):
                                        p.op("pe", (lambda e, lh=lh, rh=rh, mi=mi, nm_=len(mms), ib_=ib_, g8=g8: e.matmul(
                                            yps[:, g8, ib_, 0:N], lh, rh, start=(mi == 0), stop=(mi == nm_ - 1))),
                                            reads=[(K_, "Tm", 0), (K_, "Ym"), (K_, "U2a"), (K_, "sin_b"), (K_, "SH", 0), (K_, "SH", 1), (K_, "SH", 2)]
                                            + [(K_, "SHs", "s2", c_) for c_ in (0, NCH - 2)] + [(K_, "U2", kc, 0), (K_, "U2", kc, 1)],
                                            writes=[("ps", 4), ("ps", 5), ("ps", 6), ("ps", 7)],
                                            inc=(g8 == 7 and mi == len(mms) - 1 and (only_ib0 or ib_ == 1)))

                        def load_mats(kc):
                            gh = kc // 4
                            hb = gh * 64
                            g32 = (kc % 4) * 8
                            p.dma("sp", (lambda e: e.dma_start(out=Tm[0][:, :, :], in_=T_scr[i, kc, :, :].rearrange("p (g f) -> p g f", g=8))),
                                  reads=[("T_scr", i, kc)], writes=[(K_, "Tm", 0)], dsem=p.dsem("tm0"))
                            for ri in range(2):
                                p.dma("sp", (lambda e, ri=ri: e.dma_start(
                                    out=Ym[0][hb:hb + 64, ri, :, :],
                                    in_=Y_scr[i, ri, hb:hb + 64, g32 * 256:(g32 + 8) * 256].rearrange("p (g f) -> p g f", g=8))),
                                    reads=[("Y_scr", i, ri, kc % 4)], writes=[(K_, "Ym")], dsem=p.dsem(f"ym"))

                        for st in (0, 1):
                            c0 = st * STW
                            ncl, cg0 = NCL[st], CG0[st]
                            load_mats(0)
                            for kc in range(KC):
                                ymat(kc, Tm[0], Ym[0], ncl,
                                     (lambda sb_, g: U2[:, g, sb_, cg0:cg0 + ncl]),
                                     (lambda hb, ri, g32: SH[hb:hb + 64, ri, g32, cg0:cg0 + ncl]))
                                if kc + 1 < KC:
                                    load_mats(kc + 1)
                                gb_ = G2[kc % 2]
                                gelu_chain(yps[:, :, :, 0:ncl],
                                           U2[:, kc * 8:(kc + 1) * 8, :, cg0:cg0 + ncl],
                                           bcast(dsh[:, i, kc * 8:(kc + 1) * 8].unsqueeze(2).unsqueeze(3), (128, 8, 2, ncl)),
                                           (8, 2, ncl), 16 * ncl,
                                           gb_[:, :, :, 0:ncl],
                                           [(K_, "U2", kc, st), "dsh"], (K_, "G2", kc % 2))
                                p.dma("sp", (lambda e, kc=kc, gb_=gb_: e.dma_start(
                                    out=scr_y[kc, :, :], in_=gb_[:, :, :, :].rearrange("p g b c -> p (g b c)"))),
                                    reads=[(K_, "G2", kc % 2)], writes=[("scr_y", kc)], dsem=p.dsem(f"sy_w{kc % 2}"))
                                for g8 in range(8):
                                    for ib_ in range(NBK):
                                        src = scr_y[kc, :, :].rearrange("(c i) (g b x) -> g b c i x", i=8, g=8, b=NBK)[g8, ib_][:, :, 0:ncl]
                                        p.dma("sp", (lambda e, kc=kc, g8=g8, ib_=ib_, src=src, ncl=ncl: e.dma_start(
                                            out=gT[g8 * 16:(g8 + 1) * 16, kc, :].rearrange("c (b i x) -> c b i x", b=NBK, i=8)[:, ib_, :, 0:ncl], in_=src)),
                                            reads=[("scr_y", kc)], writes=[(K_, "hT", kc)], dsem=p.dsem(f"sy_r{kc}"))
                            if st == 1:
                                load_mats(0)
                                for kc in range(KC):
                                    ymat(kc, Tm[0], Ym[0], 16,
                                         (lambda sb_, g: U2a[:, g, :]),
                                         (lambda hb, ri, g32: sin_b[hb:hb + 64, ri, g32, :]), only_ib0=True)
                                    if kc + 1 < KC:
                                        load_mats(kc + 1)
                                    gelu_chain(yps[:, :, 0, 0:16],
                                               U2a[:, kc * 8:(kc + 1) * 8, :],
                                               bcast(dsh[:, i, kc * 8:(kc + 1) * 8].unsqueeze(2), (128, 8, 16)),
                                               (8, 16), 128, gsm[:, kc * 8:(kc + 1) * 8, :], [(K_, "U2a"), "dsh"], (K_, "gsm"))
                                p.dma("sp", (lambda e: e.dma_start(out=scr_sy[:, :], in_=gsm[0:128:8, :, :].rearrange("c g b -> c (g b)"))),
                                      reads=[(K_, "gsm")], writes=["scr_sy"], dsem=p.dsem("ssy_w"))
                                for g8 in range(8):
                                    p.dma("sp", (lambda e, g8=g8: e.dma_start(out=gTs[g8 * 16:(g8 + 1) * 16, :, :],
                                                                       in_=scr_sy[:, :].rearrange("c (k g b) -> g c k b", g=8, b=16)[g8])),
                                          reads=["scr_sy"], writes=[(K_, "gTs")], dsem=p.dsem("ssy_r"))
                            tiles = [(s0 * 65, s1 * 65, s0, s1) for (s0, s1) in STILES]
                            if st == 1:
                                tiles.append((None, None, None, None))
                            step = 0
                            for dc in range(KC):
                                def loaderg(slot, dc=dc):
                                    v_ = slot[:, :].rearrange("p (kc two f) -> p kc two f", kc=KC, two=2)
                                    return [
                                        (v_[:, :, 0, :], ssm_wglu[i, :, dc * 128:(dc + 1) * 128].rearrange("(kc p) f -> p kc f", p=128)),
                                        (v_[:, :, 1, :], ssm_wglu[i, :, D + dc * 128:D + (dc + 1) * 128].rearrange("(kc p) f -> p kc f", p=128)),
                                    ]
                                slot, wkey = wsA.next(loaderg)
                                wv = slot[:, :].rearrange("p (kc two f) -> p kc two f", kc=KC, two=2)
                                for (q0, q1, s0, s1) in tiles:
                                    smp = q0 is None
                                    n = 16 if smp else q1 - q0
                                    pb = step % 2
                                    step += 1
                                    for half in range(2):
                                        for k2 in range(KC):
                                            rhs = gTs[:, k2, :] if smp else gT[:, k2, q0:q1]
                                            p.op("pe", (lambda e, k2=k2, half=half, pb=pb, n=n, rhs=rhs, wv=wv: e.matmul(
                                                ps[2 * half + pb][:, 0:n], wv[:, k2, half, :], rhs, start=(k2 == 0), stop=(k2 == KC - 1))),
                                                reads=[wkey, (K_, "hT", k2), (K_, "gTs")], writes=[("ps", 2 * half + pb)], inc=(k2 == KC - 1))
                                    sg_ = sgm[pb]
                                    p.op("act", (lambda e, pb=pb, n=n, sg_=sg_: e.activation(out=sg_[:, 0:n], in_=ps[2 + pb][:, 0:n], func=AF.Sigmoid)),
                                         reads=[("ps", 2 + pb)], writes=[(K_, "sgm", pb)])
                                    p.op("dve", (lambda e, pb=pb, n=n, sg_=sg_: e.tensor_tensor(out=sg_[:, 0:n], in0=sg_[:, 0:n], in1=ps[pb][:, 0:n], op=ALU.mult)),
                                         reads=[(K_, "sgm", pb), ("ps", pb)], writes=[(K_, "sgm", pb)])
                                    if smp:
                                        p.op("dve", (lambda e, dc=dc, sg_=sg_: e.tensor_tensor(
                                            out=xT[:, dc, 2064:2080], in0=xT[:, dc, 2064:2080], in1=sg_[:, 0:16], op=ALU.add)),
                                            reads=[(K_, "sgm", pb), ("xT", dc, 1)], writes=[("xT", dc, 1)])
                                    else:
                                        ns = s1 - s0
                                        clo = 1 if st == 0 else 0
                                        nmc = ncl - clo
                                        xv = xT[:, dc, c0 + s0:c0 + s0 + 1024].rearrange("p (c s) -> p s c", s=LCH)[:, 0:ns, :]
                                        tv = sg_[:, 0:n].rearrange("p (s c) -> p s c", c=65)[:, :, clo:clo + 64]
                                        p.op("dve", (lambda e, xv=xv, tv=tv: e.tensor_tensor(out=xv, in0=xv, in1=tv, op=ALU.add)),
                                             reads=[(K_, "sgm", pb), ("xT", dc, st)], writes=[("xT", dc, st)])
                                        if st == 0:
                                            xp_ = xT[:, dc, 1024 + s0:1024 + s1]
                                            tp_ = sg_[:, 0:n].rearrange("p (s c) -> p s c", c=65)[:, :, 0]
                                            p.op("dve", (lambda e, xp_=xp_, tp_=tp_: e.tensor_tensor(out=xp_, in0=xp_, in1=tp_, op=ALU.add)),
                                                 reads=[(K_, "sgm", pb), ("xT", dc, st)], writes=[("xT", dc, st)])

            def ab_phase(l):
                i = l // 2
                nidx = 3 * l + 1
                p.barrier()
                NEGSC = 0.125
                with ExitStack() as ph:
                    K_ = nm("ab")
                    B = {"id": K_}
                    B["hT"] = sbp(ph, nm("hT"), (128, KC, STW), BF16)
                    hT = B["hT"]
                    qT = sbp(ph, nm("qT"), (128, 4, STW), BF16)
                    kT = sbp(ph, nm("kT"), (128, 2, 128 + STW), BF16)
                    vtok = sbp(ph, nm("vtok"), (128, 10, 128), BF16)
                    zT = sbp(ph, nm("zT"), (128, 4, 30 + STW), BF16)
                    zpre = sbp(ph, nm("zpre"), (128, 4, 46), BF16)
                    aoT = sbp(ph, nm("aoT"), (128, 4, STW), BF16)
                    coT = sbp(ph, nm("coT"), (128, 4, STW), BF16)
                    ropeT = sbp(ph, nm("ropeT"), (128, 2, STW), F32)
                    abc = sbp(ph, nm("abc"), (128, 140), F32)
                    esink = sbp(ph, nm("esink"), (128, 4), F32)
                    rmat = sbp(ph, nm("rmat"), (128, 128), BF16)
                    b2m = sbp(ph, nm("b2m"), (128, 128), BF16)
                    onesf = sbp(ph, nm("onesf"), (128, 128), F32)
                    amask = sbp(ph, nm("amask"), (128, 4, 128), BF16)
                    rpk = sbp(ph, nm("rpk"), (128, 512), BF16)
                    ksf = sbp(ph, nm("ksf"), (128, 2, 16), F32)
                    vsf = sbp(ph, nm("vsf"), (16, 128), F32)
                    vsb = sbp(ph, nm("vsb"), (16, 128), BF16)
                    cst = sbp(ph, nm("cst"), (128, 768), F32)
                    convw = abc[:, 0:124].rearrange("p (c j) -> p c j", c=4)
                    convb, lng, lnb = abc[:, 124:128], abc[:, 128:132], abc[:, 132:136]

                    p.dma("sp", lambda e: e.dma_start(out=abc[:, :], in_=abc_in[i, :, :]), reads=(), writes=[(K_, "abc")], dsem=p.dsem("abc_l"))
                    p.dma("sp", lambda e: e.dma_start(out=cst[:, 0:128], in_=rmat_in[:, :]), reads=(), writes=[(K_, "cst0")], dsem=p.dsem("cst0"))
                    p.dma("sp", lambda e: e.dma_start(out=cst[:, 128:256], in_=b2_in[:, :]), reads=(), writes=[(K_, "cst1")], dsem=p.dsem("cst1"))
                    p.op("dve", lambda e: e.tensor_copy(out=rmat[:, :], in_=cst[:, 0:128]), reads=[(K_, "cst0")], writes=[(K_, "rmat")])
                    p.op("dve", lambda e: e.tensor_copy(out=b2m[:, :], in_=cst[:, 128:256]), reads=[(K_, "cst1")], writes=[(K_, "b2m")])
                    p.op("dve", lambda e: e.memset(onesf[:, :], 1.0 / 512), writes=[(K_, "onesf")])
                    p.op("act", lambda e: e.activation(out=esink[:, :], in_=abc[:, 136:140], func=AF.Exp), reads=[(K_, "abc")], writes=[(K_, "esink")])
                    p.dma("sp", lambda e: e.dma_start(out=cst[:, 256:768], in_=amask_in[:, :]), reads=(), writes=[(K_, "cst2")], dsem=p.dsem("cst2"))
                    p.op("dve", lambda e: e.tensor_copy(out=amask[:, :, :].rearrange("p a b -> p (a b)"), in_=cst[:, 256:768]), reads=[(K_, "cst2")], writes=[(K_, "amask")])
                    p.op("dve", lambda e: e.memset(zpre[:, :, 0:30], 0.0), writes=[(K_, "zpre")])
                    p.op("dve", lambda e: e.memset(vtok[:, 9, :], 0.0), writes=[(K_, "vtok", 9)])

                    if DEBUG.get("ab_stop") == 101:
                        p.barrier()
                        return

                    def load_rope(st):
                        p.dma("sp", (lambda e: e.dma_start(out=ropeT[:, :, :], in_=rope_in[st].rearrange("a p n -> p a n"))),
                              reads=(), writes=[(K_, "rope")], dsem=p.dsem("rope_l"))

                    def project(cols, hsrc, hkeys, rope_cols, dst, tag, with_q=True, sample_cols=None):
                        T = dst["tmp"]
                        fills = []
                        if with_q:
                            fills += [("q", 0), ("q", 2)]
                        fills += [("k", 0)] + [("z", c) for c in range(4)] + [("v", 0)]
                        if DEBUG.get("ab_fills"):
                            fills = [f_ for f_ in fills if f_[0] in DEBUG["ab_fills"]]
                        step = [0]
                        for kind, c in fills:
                            if kind == "q":
                                def loader(slot, c=c):
                                    v_ = slot[:, :].rearrange("p (kc f) -> p kc f", kc=KC)
                                    return [(v_, ab_w_in[i, :, c * 128:(c + 2) * 128].rearrange("(kc p) f -> p kc f", p=128))]
                            elif kind == "k":
                                def loader(slot):
                                    if DEBUG.get("kload") == 1:
                                        v2_ = slot[:, :].rearrange("p (kc f) -> p kc f", kc=KC)
                                        return [(v2_, ab_w_in[i, :, 512:768].rearrange("(kc p) f -> p kc f", p=128))]
                                    v_ = slot[:, :].rearrange("p (kc a f) -> p kc a f", kc=KC, a=4)
                                    return [(v_[:, :, a, :], ab_w_in[i, :, 512 + 64 * (a // 2):512 + 64 * (a // 2) + 64].rearrange("(kc p) f -> p kc f", p=128))
                                            for a in range(4)]
                            elif kind == "z":
                                def loader(slot, c=c):
                                    v_ = slot[:, :].rearrange("p (kc a f) -> p kc a f", kc=KC, a=2)
                                    return [(v_[:, :, 0, :], ab_w_in[i, :, 768 + c * 128:768 + (c + 1) * 128].rearrange("(kc p) f -> p kc f", p=128)),
                                            (v_[:, :, 1, :], ab_w_in[i, :, 1280 + c * 128:1280 + (c + 1) * 128].rearrange("(kc p) f -> p kc f", p=128))]
                            else:
                                def loader(slot):
                                    v_ = slot[:, 0:1024].rearrange("p (kc f) -> p kc f", kc=KC)
                                    return [(v_, ab_w_in[i, :, 640:768].rearrange("(kc p) f -> p kc f", p=128))]
                            slot, wkey = wsA.next(loader)
                            wv = slot[:, :].rearrange("p (kc a f) -> p kc a f", kc=KC, a=2)
                            if kind == "v":
                                wvv = slot[:, 0:1024].rearrange("p (kc f) -> p kc f", kc=KC)
                                for (b0, nb, dkey, dfn) in dst["vblocks"]:
                                    pb = 4 + (step[0] % 2)
                                    step[0] += 1
                                    for k2 in range(KC):
                                        p.op("pe", (lambda e, k2=k2, pb=pb, b0=b0, nb=nb, wvv=wvv: e.matmul(
                                            ps[pb][0:nb, 0:128], hsrc[:, k2, b0:b0 + nb], wvv[:, k2, :], start=(k2 == 0), stop=(k2 == KC - 1))),
                                            reads=[wkey] + hkeys, writes=[("ps", pb)], inc=(k2 == KC - 1))
                                    dfn(ps[pb][0:nb, 0:128], pb)
                                continue
                            for (t0, t1) in cols:
                                n = t1 - t0
                                pb = step[0] % 2
                                step[0] += 1
                                for half in range(2):
                                    for k2 in range(KC):
                                        p.op("pe", (lambda e, k2=k2, half=half, pb=pb, n=n, t0=t0, t1=t1, wv=wv: e.matmul(
                                            ps[2 * half + pb][:, 0:n], wv[:, k2, half, :], hsrc[:, k2, t0:t1], start=(k2 == 0), stop=(k2 == KC - 1))),
                                            reads=[wkey] + hkeys, writes=[("ps", 2 * half + pb)], inc=(k2 == KC - 1))
                                if kind == "z":
                                    sg_ = T["sgz"][pb]
                                    p.op("act", (lambda e, pb=pb, n=n, sg_=sg_: e.activation(out=sg_[:, 0:n], in_=ps[2 + pb][:, 0:n], func=AF.Sigmoid)),
                                         reads=[("ps", 2 + pb)], writes=[(K_, tag, "sgz", pb)])
                                    dst["z"](c, t0, t1, ps[pb][:, 0:n], sg_[:, 0:n], [("ps", pb), (K_, tag, "sgz", pb)])
                                else:
                                    for half in range(2):
                                        src = ps[2 * half + pb]
                                        qr = T["qraw"][half]
                                        cosv, sinv = rope_cols(t0, t1)
                                        if DEBUG.get("rope_skip") != 1:
                                            p.op("dve", (lambda e, src=src, qr=qr, n=n: e.tensor_copy(out=qr[:, 0:n], in_=src[:, 0:n])),
                                                 reads=[("ps", 2 * half + pb)], writes=[(K_, tag, "qraw", half)])
                                        p.op("dve", (lambda e, src=src, n=n, half=half, cosv=cosv: e.tensor_tensor(
                                            out=T["rt1"][half][:, 0:n], in0=src[:, 0:n], in1=cosv, op=ALU.mult)),
                                            reads=[("ps", 2 * half + pb), (K_, "rope")], writes=[(K_, tag, "rt1", half)])
                                        rb = 6 + half
                                        if DEBUG.get("rope_skip") == 1:
                                            rb = 2 * half + pb
                                        else:
                                            p.op("pe", (lambda e, qr=qr, n=n, rb=rb: e.matmul(ps[rb][:, 0:n], rmat[:, :], qr[:, 0:n], start=True, stop=True)),
                                                 reads=[(K_, tag, "qraw", half), (K_, "rmat")], writes=[("ps", rb)])
                                        p.op("dve", (lambda e, n=n, half=half, rb=rb, sinv=sinv: e.tensor_tensor(
                                            out=T["rt2"][half][:, 0:n], in0=ps[rb][:, 0:n], in1=sinv, op=ALU.mult)),
                                            reads=[("ps", rb), (K_, "rope")], writes=[(K_, tag, "rt2", half)])
                                        dst[kind](c + half, t0, t1, T["rt1"][half][:, 0:n], T["rt2"][half][:, 0:n],
                                                  [(K_, tag, "rt1", half), (K_, tag, "rt2", half)])

                    def tmp_bufs(stack):
                        T = {}
                        T["qraw"] = [sbp(stack, nm("qraw"), (128, 352), BF16) for _ in range(2)]
                        T["rt1"] = [sbp(stack, nm("rt1"), (128, 352), F32) for _ in range(2)]
                        T["rt2"] = [sbp(stack, nm("rt2"), (128, 352), F32) for _ in range(2)]
                        T["sgz"] = [sbp(stack, nm("sgz"), (128, 352), F32) for _ in range(2)]
                        return T

                    with ExitStack() as sa:
                        Bm = {"id": K_ + "m"}
                        Bm["sq"] = [sbp(sa, nm("sq"), (128, STW), BF16) for _ in range(2)]
                        Bm["rstd"] = sbp(sa, nm("rstd"), (128, STW), F32)
                        Bm["rtmp"] = sbp(sa, nm("rtmp"), (128, STW), F32)
                        Bm["hT"] = hT
                        T = tmp_bufs(sa)
                        xpk = sbp(sa, nm("xpk"), (128, 512), BF16)
                        p.op("dve", lambda e: e.memset(xpk[:, 504:512], 0.0), writes=[(K_, "xpk")])
                        kbf = sbp(sa, nm("kbf"), (128, 2, 128), F32)
                        vbf = sbp(sa, nm("vbf"), (128, 128), F32)
                        zbf = sbp(sa, nm("zbf"), (128, 4, 128), F32)
                        load_rope(1)
                        rmsnorm(1, nidx, Bm)
                        if DEBUG.get("ab_stop") == 102:
                            p.barrier()
                            return
                        bc0 = 896
                        hk = [(Bm["id"], "hT", k2) for k2 in range(KC)]

                        def d_k(c, t0, t1, a, b, rk):
                            p.op("dve", lambda e: e.tensor_tensor(out=kbf[:, c, :], in0=a, in1=b, op=ALU.add), reads=rk, writes=[(K_, "kbf", c)])
                            if DEBUG.get("dk_skip") != 1:
                                p.op("dve", lambda e: e.tensor_copy(out=xpk[:, c * 128:(c + 1) * 128], in_=kbf[:, c, :]),
                                     reads=[(K_, "kbf", c)], writes=[(K_, "xpk")])

                        def d_z(c, t0, t1, zv_ps, sg_, rk):
                            p.op("dve", lambda e: e.tensor_tensor(out=zbf[:, c, :], in0=zv_ps, in1=sg_, op=ALU.mult), reads=rk, writes=[(K_, "zbf", c)])
                            p.op("act", lambda e: e.activation(out=xpk[:, 384 + 30 * c:384 + 30 * (c + 1)], in_=zbf[:, c, 98:128], func=AF.Copy),
                                 reads=[(K_, "zbf", c)], writes=[(K_, "xpk")])

                        def d_v(src, pb):
                            p.op("act", lambda e: e.activation(out=vbf[:, :], in_=src, func=AF.Copy), reads=[("ps", pb)], writes=[(K_, "vbf")])
                            p.op("dve", lambda e: e.tensor_copy(out=xpk[:, 256:384], in_=vbf[:, :]), reads=[(K_, "vbf")], writes=[(K_, "xpk")])

                        project([(bc0, bc0 + 128)], hT, hk,
                                (lambda t0, t1: (ropeT[:, 0, t0:t1], ropeT[:, 1, t0:t1])),
                                {"tmp": T, "k": d_k, "z": d_z, "vblocks": [(bc0, 128, None, d_v)]}, "mini", with_q=False)
                        if DEBUG.get("ab_stop") == 103:
                            p.barrier()
                            return
                        p.dma("sp", lambda e: e.dma_start(out=pwk_out[i, :, :, :], in_=kbf[0:64, :, :]), reads=[(K_, "kbf", 0), (K_, "kbf", 1)],
                              writes=[("pwk", i)], dsem=p.dsem("pwk_s"))
                        p.dma("sp", lambda e: e.dma_start(out=pwv_out[i, :, :], in_=vbf[:, :]), reads=[(K_, "vbf")], writes=[("pwv", i)], dsem=p.dsem("pwv_s"))
                        p.dma("sp", lambda e: e.dma_start(out=pconv_out[i, :, :, :], in_=zbf[:, :, 98:128]), reads=[(K_, "zbf", c) for c in range(4)],
                              writes=[("pconv", i)], dsem=p.dsem("pconv_s"))
                        if DEBUG.get("ab_stop") == 104:
                            p.barrier()
                            return
                        p.dma("sp", lambda e: e.dma_start(out=ccab_in[:, :], in_=xpk[:, :].bitcast(F32)), reads=[(K_, "xpk")], writes=["ccab_in"], dsem=p.dsem("ccab_w"))
                        if DEBUG.get("no_cc"):
                            p.dma("sp", lambda e: e.dma_start(out=ccab_out[0:128, :], in_=ccab_in[:, :]), reads=["ccab_in"], writes=["ccab_out"], dsem=p.dsem("ccfake2"))
                        else:
                            p.dma("pool", (lambda e: e.collective_compute("AllGather", ALU.bypass, replica_groups=RG,
                                                                          ins=[ccab_in.ap().opt()], outs=[ccab_out.ap().opt()])),
                                  reads=["ccab_in"], writes=["ccab_out"], dsem=p.dsem("ccsem2", 1))
                        p.dma("sp", lambda e: e.dma_start(out=rpk[:, :].bitcast(F32), in_=ccab_out[0:128, :]), reads=["ccab_out"], writes=[(K_, "rpk")], dsem=p.dsem("ccab_r"))
                        p.barrier()
                        if DEBUG.get("ab_stop") == 1:
                            return

                    def do_st(st):
                        c0 = st * STW
                        with ExitStack() as sa:
                            Bn = {"id": K_ + f"n{st}"}
                            Bn["sq"] = [sbp(sa, nm("sq"), (128, STW), BF16) for _ in range(2)]
                            Bn["rstd"] = sbp(sa, nm("rstd"), (128, STW), F32)
                            Bn["rtmp"] = sbp(sa, nm("rtmp"), (128, STW), F32)
                            Bn["hT"] = hT
                            T = tmp_bufs(sa)
                            load_rope(st)
                            rmsnorm(st, nidx, Bn)
                            hk = [(Bn["id"], "hT", k2) for k2 in range(KC)]
                            if st == 0:
                                p.op("dve", lambda e: e.tensor_copy(out=kT[:, :, 0:128], in_=rpk[:, 0:256].rearrange("p (a b) -> p a b", a=2)),
                                     reads=[(K_, "rpk")], writes=[(K_, "kT", "ctx")])
                                p.op("dve", lambda e: e.tensor_copy(out=vtok[:, 0, :], in_=rpk[:, 256:384]), reads=[(K_, "rpk")], writes=[(K_, "vtok", 0)])
                            else:
                                p.op("dve", lambda e: e.tensor_copy(out=kT[:, :, 0:128], in_=kT[:, :, 128 + 896:128 + 1024]),
                                     reads=[(K_, "kT", 7)], writes=[(K_, "kT", "ctx")])
                                p.op("dve", lambda e: e.tensor_copy(out=vtok[:, 0, :], in_=vtok[:, 8, :]), reads=[(K_, "vtok", 8)], writes=[(K_, "vtok", 0)])
                                p.op("dve", lambda e: e.tensor_copy(out=zT[:, :, 0:30], in_=zT[:, :, 1024:1054]), reads=[(K_, "zT", c, 2) for c in range(4)],
                                     writes=[(K_, "zT", "ctx")])

                            def d_q(c, t0, t1, a, b, rk):
                                p.op("dve", lambda e: e.tensor_tensor(out=qT[:, c, t0:t1], in0=a, in1=b, op=ALU.add), reads=rk, writes=[(K_, "qT", c, t0)])

                            def d_k(c, t0, t1, a, b, rk):
                                p.op("dve", lambda e: e.tensor_tensor(out=kT[:, c, 128 + t0:128 + t1], in0=a, in1=b, op=ALU.add), reads=rk,
                                     writes=[(K_, "kT", t0)] + [(K_, "kT", bb) for bb in range(t0 // 128, (t1 + 127) // 128)])
                                if st == 1 and t1 == STW:
                                    nn = t1 - t0
                                    p.op("dve", lambda e: e.tensor_tensor(out=ksf[:, c, :], in0=a[:, nn - 16:nn], in1=b[:, nn - 16:nn], op=ALU.add), reads=rk,
                                         writes=[(K_, "ksf", c)])

                            def d_z(c, t0, t1, zv_ps, sg_, rk):
                                ti = [t[0] for t in TILES].index(t0)
                                p.op("dve", lambda e: e.tensor_tensor(out=zT[:, c, 30 + t0:30 + t1], in0=zv_ps, in1=sg_, op=ALU.mult), reads=rk,
                                     writes=[(K_, "zT", c, ti)])
                                if st == 0 and t1 == STW:
                                    p.op("act", lambda e: e.activation(out=zpre[:, c, 30:46], in_=zT[:, c, 30 + 1024:30 + 1040], func=AF.Copy),
                                         reads=[(K_, "zT", c, ti)], writes=[(K_, "zpre")])

                            vblocks = []
                            for bi_ in range(8):
                                def dfn(src, pb, bi_=bi_):
                                    p.op("act", lambda e: e.activation(out=vtok[:, 1 + bi_, :], in_=src, func=AF.Copy), reads=[("ps", pb)], writes=[(K_, "vtok", 1 + bi_)])
                                vblocks.append((bi_ * 128, 128, None, dfn))

                            def dfn_x(src, pb):
                                if st == 0:
                                    p.op("act", lambda e: e.activation(out=vtok[0:16, 9, :], in_=src, func=AF.Copy), reads=[("ps", pb)], writes=[(K_, "vtok", 9)])
                                else:
                                    p.op("act", lambda e: e.activation(out=vsf[:, :], in_=src, func=AF.Copy), reads=[("ps", pb)], writes=[(K_, "vsf")])
                                    p.op("dve", lambda e: e.tensor_copy(out=vsb[:, :], in_=vsf[:, :]), reads=[(K_, "vsf")], writes=[(K_, "vsb")])
                            vblocks.append((1024, 16, None, dfn_x))
                            project(TILES, hT, hk, (lambda t0, t1: (ropeT[:, 0, t0:t1], ropeT[:, 1, t0:t1])),
                                    {"tmp": T, "q": d_q, "k": d_k, "z": d_z, "vblocks": vblocks}, f"st{st}")
                            if st == 0:
                                p.op("dve", lambda e: e.tensor_scalar(out=zT[:, :, 0:30], in0=rpk[:, 384:504].rearrange("p (c j) -> p c j", c=4),
                                                                      scalar1=flag[:, 0:1], scalar2=None, op0=ALU.mult),
                                     reads=[(K_, "rpk"), "flag"], writes=[(K_, "zT", "ctx")])
                                p.op("dve", lambda e: e.scalar_tensor_tensor(out=zT[:, :, 14:30], in0=zpre[:, :, 30:46], scalar=flag[:, 1:2], in1=zT[:, :, 14:30],
                                                                             op0=ALU.mult, op1=ALU.add),
                                     reads=[(K_, "zpre"), (K_, "zT", "ctx"), "flag"], writes=[(K_, "zT", "ctx")])
                            p.barrier()
                            if DEBUG.get("ab_stop") == 2 + 10 * st:
                                return True

                        with ExitStack() as sb_:
                            ysb = sbp(sb_, nm("ysb"), (128, 4, 352), F32)
                            ysq = sbp(sb_, nm("ysq"), (128, 352), F32)
                            musb = sbp(sb_, nm("musb"), (128, 352), F32)
                            varb = sbp(sb_, nm("varb"), (128, 352), F32)
                            rsb = sbp(sb_, nm("rsb"), (128, 352), F32)
                            lt = [sbp(sb_, nm("lt"), (128, 352), F32) for _ in range(2)]
                            dg = [sbp(sb_, nm("dg"), (128, 31, 128), BF16) for _ in range(2)]
                            pT = [sbp(sb_, nm("pT"), (128, 6, 128), BF16) for _ in range(2)]
                            dn = [sbp(sb_, nm("dn"), (128, 128), F32) for _ in range(2)]
                            if st == 1:
                                zs = sbp(sb_, nm("zs"), (128, 4, 16, 31), F32)
                                zsm = sbp(sb_, nm("zsm"), (128, 16, 31), F32)
                                p.dma("sp", lambda e: e.dma_start(out=zs[:, :, :, 0:30], in_=sconv_in[i, :, :, :, :]), reads=(), writes=[(K_, "zs")], dsem=p.dsem("zs_l"))
                                for c in range(4):
                                    p.op("act", (lambda e, c=c: e.activation(out=zs[:, c, :, 30], in_=zT[:, c, 30 + 1024:30 + 1040], func=AF.Copy)),
                                         reads=[(K_, "zs"), (K_, "zT", c, 2)], writes=[(K_, "zs")])
                                p.dma("sp", lambda e: e.dma_start(out=sconv_out[i, :, :, :, :], in_=zs[:, :, :, 1:31]), reads=[(K_, "zs")], writes=[("sconv", i)], dsem=p.dsem("zs_s"))
                            for ti, (t0, t1) in enumerate(TILES):
                                nmain = min(t1, NMAIN) - t0
                                n = t1 - t0
                                for c in range(4):
                                    if ti == 0 or True:
                                        d_ = dg[c % 2]
                                        for j in range(31):
                                            eng = "act" if j % 2 else "dve"
                                            if eng == "dve":
                                                p.op("dve", (lambda e, d_=d_, c=c, j=j: e.tensor_scalar(out=d_[:, j, :], in0=ident_bf[:, :], scalar1=convw[:, c, j:j + 1],
                                                                                                   scalar2=None, op0=ALU.mult)),
                                                     reads=["ident", (K_, "abc")], writes=[(K_, "dg", c % 2, j)])
                                            else:
                                                p.op("act", (lambda e, d_=d_, c=c, j=j: e.activation(out=d_[:, j, :], in_=ident_bf[:, :], func=AF.Copy, scale=convw[:, c, j:j + 1])),
                                                     reads=["ident", (K_, "abc")], writes=[(K_, "dg", c % 2, j)])
                                    for j in range(31):
                                        p.op("pe", (lambda e, c=c, j=j, t0=t0, nmain=nmain: e.matmul(
                                            ps[c][:, 0:nmain], dg[c % 2][:, j, :], zT[:, c, t0 + j:t0 + j + nmain], start=(j == 0), stop=(j == 30))),
                                            reads=[(K_, "dg", c % 2, j), (K_, "zT", "ctx")] + [(K_, "zT", c, tt_) for tt_ in range(3)],
                                            writes=[("ps", c)], inc=(j == 30))
                                    if ti == 2 and st == 0:
                                        for j in range(31):
                                            p.op("pe", (lambda e, c=c, j=j, nmain=nmain: e.matmul(
                                                ps[c][:, nmain:nmain + 16], dg[c % 2][:, j, :], zpre[:, c, j:j + 16], start=(j == 0), stop=(j == 30))),
                                                reads=[(K_, "dg", c % 2, j), (K_, "zpre")], writes=[("ps", c)], inc=(j == 30))
                                    ncv = n if (ti < 2 or st == 0) else nmain
                                    p.op("act", (lambda e, c=c, ncv=ncv: e.activation(out=ysb[:, c, 0:ncv], in_=ps[c][:, 0:ncv], func=AF.Identity, bias=convb[:, c:c + 1])),
                                         reads=[("ps", c), (K_, "abc")], writes=[(K_, "ysb", c)])
                                    if ti == 2 and st == 1:
                                        p.op("dve", (lambda e, c=c: e.tensor_tensor(out=zsm[:, :, :], in0=zs[:, c, :, :],
                                                                                   in1=bcast(convw[:, c, :].unsqueeze(1), (128, 16, 31)), op=ALU.mult)),
                                             reads=[(K_, "zs"), (K_, "abc")], writes=[(K_, "zsm")])
                                        p.op("dve", (lambda e, c=c, nmain=nmain: e.tensor_reduce(out=ysb[:, c, nmain:nmain + 16], in_=zsm[:, :, :],
                                                                                               axis=mybir.AxisListType.X, op=ALU.add)),
                                             reads=[(K_, "zsm"), (K_, "ysb", c)], writes=[(K_, "ysb", c)])
                                        p.op("dve", (lambda e, c=c, nmain=nmain: e.tensor_scalar(out=ysb[:, c, nmain:nmain + 16], in0=ysb[:, c, nmain:nmain + 16],
                                                                                               scalar1=convb[:, c:c + 1], scalar2=None, op0=ALU.add)),
                                             reads=[(K_, "ysb", c), (K_, "abc")], writes=[(K_, "ysb", c)])
                                for c in range(4):
                                    p.op("pe", (lambda e, c=c, n=n: e.matmul(ps[4][:, 0:n], onesf[:, :], ysb[:, c, 0:n], start=(c == 0), stop=(c == 3))),
                                         reads=[(K_, "ysb", c), (K_, "onesf")], writes=[("ps", 4)], inc=(c == 3))
                                for c in range(4):
                                    p.op("act", (lambda e, c=c, n=n: e.activation(out=ysq[:, 0:n], in_=ysb[:, c, 0:n], func=AF.Square)),
                                         reads=[(K_, "ysb", c)], writes=[(K_, "ysq")])
                                    p.op("pe", (lambda e, c=c, n=n: e.matmul(ps[5][:, 0:n], onesf[:, :], ysq[:, 0:n], start=(c == 0), stop=(c == 3))),
                                         reads=[(K_, "ysq"), (K_, "onesf")], writes=[("ps", 5)], inc=True)
                                p.op("act", (lambda e, n=n: e.activation(out=musb[:, 0:n], in_=ps[4][:, 0:n], func=AF.Copy)), reads=[("ps", 4)], writes=[(K_, "musb")])
                                p.op("dve", (lambda e, n=n: e.tensor_tensor(out=varb[:, 0:n], in0=musb[:, 0:n], in1=musb[:, 0:n], op=ALU.mult)),
                                     reads=[(K_, "musb")], writes=[(K_, "varb")])
                                p.op("dve", (lambda e, n=n: e.tensor_tensor(out=varb[:, 0:n], in0=ps[5][:, 0:n], in1=varb[:, 0:n], op=ALU.subtract)),
                                     reads=[("ps", 5), (K_, "varb")], writes=[(K_, "varb")])
                                p.op("act", (lambda e, n=n: e.activation(out=varb[:, 0:n], in_=varb[:, 0:n], func=AF.Sqrt, bias=EPS)),
                                     reads=[(K_, "varb")], writes=[(K_, "varb")])
                                p.op("dve", (lambda e, n=n: e.reciprocal(out=rsb[:, 0:n], in_=varb[:, 0:n])), reads=[(K_, "varb")], writes=[(K_, "rsb")])
                                for c in range(4):
                                    l_ = lt[c % 2]
                                    p.op("dve", (lambda e, c=c, n=n, l_=l_: e.tensor_tensor(out=l_[:, 0:n], in0=ysb[:, c, 0:n], in1=musb[:, 0:n], op=ALU.subtract)),
                                         reads=[(K_, "ysb", c), (K_, "musb")], writes=[(K_, "lt", c % 2)])
                                    p.op("dve", (lambda e, c=c, n=n, l_=l_: e.tensor_tensor(out=l_[:, 0:n], in0=l_[:, 0:n], in1=rsb[:, 0:n], op=ALU.mult)),
                                         reads=[(K_, "lt", c % 2), (K_, "rsb")], writes=[(K_, "lt", c % 2)])
                                    p.op("dve", (lambda e, c=c, n=n, l_=l_: e.tensor_scalar(out=l_[:, 0:n], in0=l_[:, 0:n], scalar1=lng[:, c:c + 1], scalar2=lnb[:, c:c + 1],
                                                                                         op0=ALU.mult, op1=ALU.add)),
                                         reads=[(K_, "lt", c % 2), (K_, "abc")], writes=[(K_, "lt", c % 2)])
                                    p.op("act", (lambda e, c=c, n=n, l_=l_, t0=t0, t1=t1: e.activation(out=coT[:, c, t0:t1], in_=l_[:, 0:n], func=AF.Silu)),
                                         reads=[(K_, "lt", c % 2)], writes=[(K_, "coT", c, ti)])

                            if DEBUG.get("ab_stop") == 3 + 10 * st:
                                p.barrier()
                                return True
                            def attend(qc0, nq, kblocks, step):
                                for c in range(4):
                                    par = (step * 4 + c) % 2
                                    sbank = ps[2 * par][:, :]
                                    kvh = c // 2
                                    slots = []
                                    sl = 0
                                    for hh in range(2):
                                        hb = hh * 64
                                        for (ko, nk, vi, mi) in kblocks:
                                            bank = 2 * par + (sl // 4)
                                            so = (sl % 4) * 128
                                            p.op("pe", (lambda e, bank=bank, so=so, nk=nk, hb=hb, ko=ko, c=c, kvh=kvh: e.matmul(
                                                ps[bank][0:nk, so:so + nq], kT[hb:hb + 64, kvh, ko:ko + nk], qT[hb:hb + 64, c, qc0:qc0 + nq], start=True, stop=False)),
                                                reads=[(K_, "kT", "ctx")] + [(K_, "kT", bb) for bb in range(9)] + [(K_, "qT", c, tt_[0]) for tt_ in TILES],
                                                writes=[("ps", bank)], inc=False)
                                            p.op("pe", (lambda e, bank=bank, so=so, nk=nk, mi=mi: e.matmul(
                                                ps[bank][0:nk, so:so + nq], ident_bf[0:nk, 0:nk], amask[0:nk, mi, 0:nq], start=False, stop=True)),
                                                reads=["ident", (K_, "amask")], writes=[("ps", bank)], inc=True)
                                            p.op("act", (lambda e, bank=bank, so=so, nk=nk, sl=sl, par=par: e.activation(
                                                out=pT[par][0:nk, sl, 0:nq], in_=ps[bank][0:nk, so:so + nq], func=AF.Exp, scale=NEGSC)),
                                                reads=[("ps", bank)], writes=[(K_, "pT", par, sl)])
                                            slots.append((sl, hb, nk, vi))
                                            sl += 1
                                    ob = 4 + par
                                    for hh in range(2):
                                        hb = hh * 64
                                        mine = [s_ for s_ in slots if s_[1] == hb]
                                        for idx, (sl_, _, nk, vi) in enumerate(mine):
                                            p.op("pe", (lambda e, ob=ob, hb=hb, nk=nk, vi=vi, sl_=sl_, kvh=kvh, idx=idx, nm_=len(mine), par=par: e.matmul(
                                                ps[ob][hb:hb + 64, 0:nq], vtok[0:nk, vi, kvh * 64:kvh * 64 + 64], pT[par][0:nk, sl_, 0:nq],
                                                start=(idx == 0), stop=(idx == nm_ - 1))),
                                                reads=[(K_, "pT", par, sl_), (K_, "vtok", vi)], writes=[("ps", ob)], inc=False)
                                        for idx, (sl_, _, nk, vi) in enumerate(mine):
                                            p.op("pe", (lambda e, ob=ob, hb=hb, nk=nk, sl_=sl_, idx=idx, nm_=len(mine), par=par: e.matmul(
                                                ps[ob][hb:hb + 64, 128:128 + nq], ones_bf[0:nk, 0:64], pT[par][0:nk, sl_, 0:nq],
                                                start=(idx == 0), stop=(idx == nm_ - 1))),
                                                reads=[(K_, "pT", par, sl_), "ones"], writes=[("ps", ob)], inc=(hh == 1 and idx == len(mine) - 1))
                                    d_ = dn[par]
                                    p.op("dve", (lambda e, ob=ob, d_=d_, c=c: e.tensor_scalar(out=d_[:, 0:nq], in0=ps[ob][:, 128:128 + nq], scalar1=esink[:, c:c + 1],
                                                                                         scalar2=None, op0=ALU.add)),
                                         reads=[("ps", ob), (K_, "esink")], writes=[(K_, "dn", par)])
                                    p.op("dve", (lambda e, d_=d_: e.reciprocal(out=d_[:, 0:nq], in_=d_[:, 0:nq])), reads=[(K_, "dn", par)], writes=[(K_, "dn", par)])
                                    p.op("dve", (lambda e, ob=ob, d_=d_, c=c: e.tensor_tensor(out=aoT[:, c, qc0:qc0 + nq], in0=ps[ob][:, 0:nq], in1=d_[:, 0:nq], op=ALU.mult)),
                                         reads=[("ps", ob), (K_, "dn", par)], writes=[(K_, "aoT", c, qc0)])

                            stp = 0
                            for bi_ in range(8):
                                kbl = []
                                if bi_ == 0 and st == 0:
                                    kbl.append((0, 128, 0, 2))
                                    kbl.append((128 + 1024, 16, 9, 3))
                                else:
                                    kbl.append((128 + (bi_ - 1) * 128, 128, bi_, 1))
                                kbl.append((128 + bi_ * 128, 128, 1 + bi_, 0))
                                attend(bi_ * 128, 128, kbl, stp)
                                stp += 1
                            if st == 0:
                                attend(1024, 16, [(128 + 1024, 16, 9, 0)], stp)
                                stp += 1
                            p.barrier()
                            if DEBUG.get("ab_stop") == 4 + 10 * st:
                                return True

                        if st == 1:
                            with ExitStack() as sc_:
                                KTs = sbp(sc_, nm("KTs"), (128, 2, 16, 128), BF16)
                                Vs = sbp(sc_, nm("Vs"), (128, 16, 128), BF16)
                                PTs = sbp(sc_, nm("PTs"), (128, 128), BF16)
                                prod = sbp(sc_, nm("prod"), (128, 4, 16), BF16)
                                pnew = sbp(sc_, nm("pnew"), (128, 4, 16), F32)
                                vdT = sbp(sc_, nm("vdT"), (128, 2, 16), F32)
                                o1 = sbp(sc_, nm("o1"), (128, 4, 16), F32)
                                d1 = sbp(sc_, nm("d1"), (128, 4, 16), F32)
                                kd = sbp(sc_, nm("kd"), (128, 2, 16), BF16)
                                p.op("dve", lambda e: e.memset(fence_t[:, 4:8], 0.0), writes=["fenceC"])
                                for kv_ in range(2):
                                    p.dma("pool", (lambda e, kv_=kv_: e.dma_start(out=KTs[:, kv_, :, :], in_=ckt_in[i, :, kv_, :, :])),
                                          reads=["fenceC"], writes=[(K_, "KTs")], dsem=p.dsem("kts_l"))
                                p.dma("pool", lambda e: e.dma_start(out=Vs[:, :, :], in_=cv_in[i, :, :, :]), reads=["fenceC"], writes=[(K_, "Vs")], dsem=p.dsem("vs_l"))
                                p.dma("sp", lambda e: e.dma_start(out=swk_out[i, :, 0:127, :], in_=cknat_in[i, :, 1:128, :]), reads=(), writes=[("swk", i)], dsem=p.dsem("swk_c"))
                                p.dma("sp", lambda e: e.dma_start(out=swv_out[i, :, 0:127, :], in_=cvnat_in[i, :, 1:128, :]), reads=(), writes=[("swv", i)], dsem=p.dsem("swv_c"))
                                p.dma("sp", lambda e: e.dma_start(out=swv_out[i, :, 127, :], in_=vsf[:, :]), reads=[(K_, "vsf")], writes=[("swv2", i)], dsem=p.dsem("swv_n"))
                                for kv_ in range(2):
                                    p.dma("sp", (lambda e, kv_=kv_: e.dma_start(out=swk_out[i, :, 127, kv_ * 64:(kv_ + 1) * 64].rearrange("b d -> d b"), in_=ksf[0:64, kv_, :],
                                                                              allow_slow_non_contiguous=True)),
                                          reads=[(K_, "ksf", 0), (K_, "ksf", 1)], writes=[("swk2", i, kv_)], dsem=p.dsem("swk_n"))
                                sq0 = 1024
                                for b in range(16):
                                    for h in range(8):
                                        hb = (h % 2) * 64
                                        p.op("pe", (lambda e, b=b, h=h, hb=hb: e.matmul(
                                            ps[0][:, b * 8 + h:b * 8 + h + 1], KTs[hb:hb + 64, h // 4, b, :], qT[hb:hb + 64, h // 2, sq0 + b:sq0 + b + 1],
                                            start=True, stop=True)),
                                            reads=[(K_, "KTs")] + [(K_, "qT", c, TILES[2][0]) for c in range(4)], writes=[("ps", 0)], inc=(b == 15 and h == 7))
                                p.op("act", lambda e: e.activation(out=PTs[:, :], in_=ps[0][:, 0:128], func=AF.Exp, scale=NEGSC), reads=[("ps", 0)], writes=[(K_, "PTs")])
                                osps = ps[4][:, 0:64].rearrange("p (c b) -> p c b", c=4)
                                dsps = ps[4][:, 64:128].rearrange("p (c b) -> p c b", c=4)
                                for b in range(16):
                                    for h in range(8):
                                        hb = (h % 2) * 64
                                        p.op("pe", (lambda e, b=b, h=h, hb=hb: e.matmul(
                                            osps[hb:hb + 64, h // 2, b:b + 1], Vs[:, b, (h // 4) * 64:(h // 4) * 64 + 64], PTs[:, b * 8 + h:b * 8 + h + 1],
                                            start=True, stop=True)), reads=[(K_, "PTs"), (K_, "Vs")], writes=[("ps", 4)], inc=False)
                                        p.op("pe", (lambda e, b=b, h=h, hb=hb: e.matmul(
                                            dsps[hb:hb + 64, h // 2, b:b + 1], ones_bf[:, 0:64], PTs[:, b * 8 + h:b * 8 + h + 1],
                                            start=True, stop=True)), reads=[(K_, "PTs"), "ones"], writes=[("ps", 4)], inc=(b == 15 and h == 7))
                                p.op("act", lambda e: e.activation(out=kd[:, :, :], in_=ksf[:, :, :], func=AF.Copy), reads=[(K_, "ksf", 0), (K_, "ksf", 1)], writes=[(K_, "kd")])
                                for c in range(4):
                                    p.op("dve", (lambda e, c=c: e.tensor_tensor(out=prod[:, c, :], in0=qT[:, c, sq0:sq0 + 16], in1=kd[:, c // 2, :], op=ALU.mult)),
                                         reads=[(K_, "kd"), (K_, "qT", c, TILES[2][0])], writes=[(K_, "prod", c)])
                                    p.op("pe", (lambda e, c=c: e.matmul(ps[5][:, c * 16:(c + 1) * 16], b2m[:, :], prod[:, c, :], start=True, stop=True)),
                                         reads=[(K_, "prod", c), (K_, "b2m")], writes=[("ps", 5)], inc=(c == 3))
                                p.op("act", lambda e: e.activation(out=pnew[:, :, :], in_=ps[5][:, 0:64].rearrange("p (c b) -> p c b", c=4), func=AF.Exp, scale=NEGSC),
                                     reads=[("ps", 5)], writes=[(K_, "pnew")])
                                for kv_ in range(2):
                                    for dup in range(2):
                                        p.op("pe", (lambda e, kv_=kv_, dup=dup: e.matmul(
                                            ps[6][dup * 64:dup * 64 + 64, kv_ * 16:(kv_ + 1) * 16], vsb[0:16, kv_ * 64:kv_ * 64 + 64], ident_bf[0:16, 0:16],
                                            start=True, stop=True)), reads=[(K_, "vsb"), "ident"], writes=[("ps", 6)], inc=(kv_ == 1 and dup == 1))
                                p.op("act", lambda e: e.activation(out=vdT[:, :, :], in_=ps[6][:, 0:32].rearrange("p (k b) -> p k b", k=2), func=AF.Copy),
                                     reads=[("ps", 6)], writes=[(K_, "vdT")])
                                for c in range(4):
                                    p.op("dve", (lambda e, c=c: e.tensor_tensor(out=o1[:, c, :], in0=pnew[:, c, :], in1=vdT[:, c // 2, :], op=ALU.mult)),
                                         reads=[(K_, "pnew"), (K_, "vdT")], writes=[(K_, "o1", c)])
                                    p.op("dve", (lambda e, c=c: e.tensor_tensor(out=o1[:, c, :], in0=o1[:, c, :], in1=osps[:, c, :], op=ALU.add)),
                                         reads=[(K_, "o1", c), ("ps", 4)], writes=[(K_, "o1", c)])
                                    p.op("dve", (lambda e, c=c: e.tensor_tensor(out=d1[:, c, :], in0=pnew[:, c, :], in1=dsps[:, c, :], op=ALU.add)),
                                         reads=[(K_, "pnew"), ("ps", 4)], writes=[(K_, "d1", c)])
                                    p.op("dve", (lambda e, c=c: e.tensor_scalar(out=d1[:, c, :], in0=d1[:, c, :], scalar1=esink[:, c:c + 1], scalar2=None, op0=ALU.add)),
                                         reads=[(K_, "d1", c), (K_, "esink")], writes=[(K_, "d1", c)])
                                    p.op("dve", (lambda e, c=c: e.reciprocal(out=d1[:, c, :], in_=d1[:, c, :])), reads=[(K_, "d1", c)], writes=[(K_, "d1", c)])
                                    p.op("dve", (lambda e, c=c: e.tensor_tensor(out=aoT[:, c, sq0:sq0 + 16], in0=o1[:, c, :], in1=d1[:, c, :], op=ALU.mult)),
                                         reads=[(K_, "o1", c), (K_, "d1", c)], writes=[(K_, "aoT", c, sq0)])
                                p.barrier()

                        step = 0
                        for dc in range(KC):
                            def loadero(slot, dc=dc):
                                v_ = slot[:, 0:1024].rearrange("p (kc f) -> p kc f", kc=KC)
                                return [(v_, ab_w_out[i, :, dc * 128:(dc + 1) * 128].rearrange("(kc p) f -> p kc f", p=128))]
                            slot, wkey = wsA.next(loadero)
                            wv = slot[:, 0:1024].rearrange("p (kc f) -> p kc f", kc=KC)
                            for ti, (t0, t1) in enumerate(TILES):
                                n = t1 - t0
                                pb = 6 + (step % 2)
                                step += 1
                                for k2 in range(KC):
                                    rhs = aoT[:, k2, t0:t1] if k2 < 4 else coT[:, k2 - 4, t0:t1]
                                    p.op("pe", (lambda e, k2=k2, pb=pb, n=n, rhs=rhs, wv=wv: e.matmul(
                                        ps[pb][:, 0:n], wv[:, k2, :], rhs, start=(k2 == 0), stop=(k2 == KC - 1))),
                                        reads=[wkey] + [(K_, "aoT", c, q_) for c in range(4) for q_ in list(range(0, 1024, 128)) + [1024]]
                                        + [(K_, "coT", c, ti) for c in range(4)], writes=[("ps", pb)], inc=(k2 == KC - 1))
                                p.op("dve", (lambda e, pb=pb, n=n, dc=dc, t0=t0, t1=t1, c0=c0: e.tensor_tensor(
                                    out=xT[:, dc, c0 + t0:c0 + t1], in0=xT[:, dc, c0 + t0:c0 + t1], in1=ps[pb][:, 0:n], op=ALU.add)),
                                    reads=[("ps", pb), ("xT", dc, st)], writes=[("xT", dc, st)])
                        p.barrier()
                        return False

                    for st_ in (0, 1):
                        if do_st(st_):
                            return

            mode = DEBUG.get("mode")
            if mode == "ab":
                ab_phase(0)
                final_phase()
                p.final_wait("sp")
                return
            if mode == "ssm":
                ssm_precompute(0)
                ssm_phase(1)
                final_phase()
                p.final_wait("sp")
                return
            for i in range(2):
                ssm_precompute(i)
            sub = 0
            stop = DEBUG["stop_after"]
            done = False
            for l in range(DEPTH):
                for which in (0, 1, 2):
                    if which == 1:
                        if l % 2 == 1:
                            ssm_phase(l)
                        else:
                            ab_phase(l)
                    else:
                        ffn_phase(l, 0 if which == 0 else 1, 3 * l + which)
                    if stop is not None and sub == stop:
                        done = True
                        break
                    sub += 1
                if done:
                    break
            final_phase()
            p.final_wait("sp")

        pd = Prog(dry=True)
        wsA = WeightStream(pd, "wA", wA, 3)
        wsB = WeightStream(pd, "wB", [None, None], 1)
        emit(pd, wsA, wsB)
        p = Prog(dry=False)
        wsA.p = p
        wsB.p = p
        wsA.reset_for_real()
        wsB.reset_for_real()
        emit(p, wsA, wsB)
        p.replay(nc, es)
    return nc


_CACHE = {}


def _host_inputs(inputs):
    f = lambda k: np.asarray(inputs[k], np.float32)
    x_prompt, x_sample, meta = f("x_prompt"), f("x_sample"), f("meta_tokens")
    ng = np.concatenate([f("norm_g").reshape(12, D), f("final_norm_g")[None]], 0)
    gains = np.ascontiguousarray(ng.reshape(13, KC, 128).transpose(2, 0, 1).reshape(128, 13 * KC))
    ident = np.eye(128, dtype=np.float32)
    r = np.arange(128)
    mc = (r[None, :] % 8 >= r[:, None] % 8).astype(np.float32)
    mask3 = np.ascontiguousarray(np.concatenate([mc, np.ones((128, 128), np.float32), mc], 1))

    def gp(a):
        sh = a.shape
        a = a.reshape(2, 32, 64, *sh[2:])
        a = np.moveaxis(a, 2, 1)
        return a.reshape(128, 32, *sh[2:])

    a_re, a_im, ldt = f("ssm_a_re"), f("ssm_a_im"), f("ssm_log_dt")
    small = np.stack([np.concatenate([gp(a_re[i]), gp(a_im[i]), gp(np.broadcast_to(ldt[i][:, None], (64, 64)))], 1) for i in range(2)])
    b_re, b_im, c_re, c_im = f("ssm_b_re"), f("ssm_b_im"), f("ssm_c_re"), f("ssm_c_im")
    ssm_b = np.stack([np.stack([gp(b_re[i]), gp(b_im[i])], 1).reshape(128, 1024) for i in range(2)])
    ssm_c = np.stack([np.stack([gp(c_re[i].transpose(0, 2, 1)), gp(c_im[i].transpose(0, 2, 1))], 1).reshape(128, 1024) for i in range(2)])
    sd = f("ssm_d")
    dsh = np.stack([np.ascontiguousarray(np.repeat(sd[i].reshape(64, 16).T, 8, axis=0)) for i in range(2)])
    common = {"gains": gains, "ident": ident, "mask3": mask3,
              "ssm_small": np.ascontiguousarray(small.reshape(2, 128, 96)), "ssm_b": np.ascontiguousarray(ssm_b),
              "ssm_c": np.ascontiguousarray(ssm_c), "ssm_dsh": np.ascontiguousarray(dsh)}
    for k in ("ffn1_w_gu", "ffn2_w_gu", "ffn1_w_down", "ffn2_w_down", "ssm_w_in", "ssm_w_glu", "ab_w_in", "ab_w_out"):
        common[k] = f(k)
    cw, cb, lg, lb, sk = f("conv_w"), f("conv_b"), f("conv_ln_g"), f("conv_ln_b"), f("attn_sink")
    abc = np.zeros((2, 128, 140), np.float32)
    for i in range(2):
        abc[i, :, 0:124] = cw[i].reshape(31, 4, 128).transpose(2, 1, 0).reshape(128, 124)
        abc[i, :, 124:128] = cb[i].reshape(4, 128).T
        abc[i, :, 128:132] = lg[i].reshape(4, 128).T
        abc[i, :, 132:136] = lb[i].reshape(4, 128).T
        abc[i, :, 136:140] = np.repeat(sk[i].reshape(4, 2), 64, axis=1).T
    common["abc"] = abc
    rmat = np.zeros((128, 128), np.float32)
    for hb in (0, 64):
        for dd in range(8):
            rmat[hb + dd + 8, hb + dd] = -1.0
            rmat[hb + dd, hb + dd + 8] = 1.0
    common["rmat"] = rmat
    b2 = np.zeros((128, 128), np.float32)
    b2[0:64, 0:64] = 1.0
    b2[64:128, 64:128] = 1.0
    common["b2"] = b2
    NEG = -240000.0
    sidx = np.arange(128)[:, None]
    qidx = np.arange(128)[None, :]
    m_own = np.where(sidx <= qidx, 0.0, NEG).astype(np.float32)
    m_prev = np.where(sidx >= qidx, 0.0, NEG).astype(np.float32)
    m_neg = np.full((128, 128), NEG, np.float32)
    m_pre = np.where((sidx >= qidx - 112) & (sidx < 16), 0.0, NEG).astype(np.float32)
    inv = (np.float32(500000.0) ** (-np.arange(8, dtype=np.float32) * np.float32(2.0) / np.float32(16.0))).astype(np.float32)
    ck, cv_, sc_in = f("cache_win_k"), f("cache_win_v"), f("state_conv")
    s_re, s_im = f("state_ssm_re"), f("state_ssm_im")
    in_maps = []
    for rr in range(8):
        seq, par = rr // 2, rr % 2
        cols = np.zeros((NCOL, D), np.float32)
        main = x_prompt[seq, par * 2048:(par + 1) * 2048]
        cols[0:1024] = main[0:1024]
        cols[1040:2064] = main[1024:2048]
        if par == 0:
            cols[1024:1040] = meta
        cols[2064:2080] = x_sample[16 * rr:16 * rr + 16, 0]
        m = dict(common)
        m["xT"] = np.ascontiguousarray(cols.T)
        fl = np.zeros((128, 2), np.float32)
        fl[:, 0] = float(par)
        fl[:, 1] = 1.0 - float(par)
        m["flag"] = fl
        sst = np.zeros((2, 128, 2, 32, 16), np.float32)
        for i in range(2):
            for ri, sarr in enumerate((s_re, s_im)):
                blk = sarr[i, 16 * rr:16 * rr + 16]
                sst[i, :, ri] = gp(blk.transpose(1, 2, 0))
        m["sst_in"] = np.ascontiguousarray(sst.reshape(2, 128, 1024))
        m["amask"] = np.ascontiguousarray(np.stack([m_own, m_prev, m_prev if par == 1 else m_neg, m_pre if par == 0 else m_neg], 1).reshape(128, 512))
        rope = np.zeros((2, 2, 128, STW), np.float32)
        for st in range(2):
            pos = np.zeros(STW, np.float32)
            pos[0:1024] = 16 + 2048 * par + 1024 * st + np.arange(1024)
            pos[1024:1040] = np.arange(16) if st == 0 else 8192
            ang = (pos[:, None] * inv[None, :]).astype(np.float32)
            cs, sn = np.cos(ang).astype(np.float32), np.sin(ang).astype(np.float32)
            rope[st, 0] = 1.0
            for hb in (0, 64):
                for dd in range(16):
                    rope[st, 0, hb + dd] = cs[:, dd % 8]
                    rope[st, 1, hb + dd] = sn[:, dd % 8]
        m["rope"] = rope
        ckb = ck[:, 16 * rr:16 * rr + 16]
        cvb = cv_[:, 16 * rr:16 * rr + 16]
        kt = ckb.transpose(0, 4, 3, 1, 2)
        m["ckt"] = np.ascontiguousarray(np.concatenate([kt, kt], 1))
        m["cv"] = np.ascontiguousarray(cvb.reshape(2, 16, 128, 128).transpose(0, 2, 1, 3))
        m["cknat"] = np.ascontiguousarray(ckb.reshape(2, 16, 128, 128))
        m["cvnat"] = np.ascontiguousarray(cvb.reshape(2, 16, 128, 128))
        scb = sc_in[:, 16 * rr:16 * rr + 16]
        m["sconv_in"] = np.ascontiguousarray(scb.reshape(2, 16, 30, 4, 128).transpose(0, 4, 3, 1, 2))
        in_maps.append(m)
    return in_maps


def _ungp(a):
    sh = a.shape
    a = a.reshape(2, 64, 32, *sh[2:])
    a = np.moveaxis(a, 1, 2)
    return a.reshape(64, 64, *sh[2:])


def kernel(**inputs):
    ncores = 8
    key = str(sorted(DEBUG.items()))
    if key not in _CACHE:
        _CACHE[key] = build_program()
    nc = _CACHE[key]
    in_maps = _host_inputs(inputs)
    res = run_bass_kernel_spmd(nc, in_maps, core_ids=list(range(ncores)))
    kernel.last_res = res
    y_prompt = np.zeros((4, 4096, D), np.float32)
    y_sample = np.zeros((128, 1, D), np.float32)
    p_re = np.zeros((2, 4, 64, 64), np.float32)
    p_im = np.zeros((2, 4, 64, 64), np.float32)
    s_re = np.zeros((2, 128, 64, 64), np.float32)
    s_im = np.zeros((2, 128, 64, 64), np.float32)
    for r in range(ncores):
        seq, par = r // 2, r % 2
        out = res.results[r]
        y = np.asarray(out["yT"]).T
        y_prompt[seq, par * 2048:par * 2048 + 1024] = y[0:1024]
        y_prompt[seq, par * 2048 + 1024:(par + 1) * 2048] = y[1040:2064]
        y_sample[16 * r:16 * r + 16, 0] = y[2064:2080]
        ps_ = np.asarray(out["pssm"]).reshape(2, 128, 2, 32)
        ss_ = np.asarray(out["sssm"]).reshape(2, 128, 2, 32, 16)
        for i in range(2):
            if par == 1:
                p_re[i, seq] = _ungp(ps_[i, :, 0])
                p_im[i, seq] = _ungp(ps_[i, :, 1])
            s_re[i, 16 * r:16 * r + 16] = _ungp(ss_[i, :, 0]).transpose(2, 0, 1)
            s_im[i, 16 * r:16 * r + 16] = _ungp(ss_[i, :, 1]).transpose(2, 0, 1)
    p_conv = np.zeros((2, 4, 30, 512), np.float32)
    p_k = np.zeros((2, 4, 128, 2, 64), np.float32)
    p_v = np.zeros((2, 4, 128, 2, 64), np.float32)
    s_conv = np.zeros((2, 128, 30, 512), np.float32)
    s_k = np.zeros((2, 128, 128, 2, 64), np.float32)
    s_v = np.zeros((2, 128, 128, 2, 64), np.float32)
    for r in range(ncores):
        seq, par = r // 2, r % 2
        out = res.results[r]
        if "pwk" not in out:
            break
        if par == 1:
            p_k[:, seq] = np.asarray(out["pwk"]).transpose(0, 3, 2, 1)
            p_v[:, seq] = np.asarray(out["pwv"]).reshape(2, 128, 2, 64)
            p_conv[:, seq] = np.asarray(out["pconv"]).transpose(0, 3, 2, 1).reshape(2, 30, 512)
        s_k[:, 16 * r:16 * r + 16] = np.asarray(out["swk"]).reshape(2, 16, 128, 2, 64)
        s_v[:, 16 * r:16 * r + 16] = np.asarray(out["swv"]).reshape(2, 16, 128, 2, 64)
        s_conv[:, 16 * r:16 * r + 16] = np.asarray(out["sconv"]).transpose(0, 3, 4, 2, 1).reshape(2, 16, 30, 512)
    kernel.extra = dict(p_re=p_re, p_im=p_im, s_re=s_re, s_im=s_im)
    return (y_prompt, y_sample, p_conv, p_k, p_v, p_re, p_im, s_conv, s_k, s_v, s_re, s_im)
```
